# Optimizing a Trainium2 kernel written in Bass

```python
import math
import jax, jax.numpy as jnp
from jax import lax
import numpy as np

D_MODEL = 2048
BATCH = 4
SEQ = 2048
DEPTH = 4
DEC_BATCH = 128
DEC_SEQ = 4
PAST_LEN = 16384
PAGE_SIZE = 128

S5_WIDTH = D_MODEL // 2
CONV_WIDTH = D_MODEL - S5_WIDTH
S5_GROUP_CH = 16
S5_GROUPS = S5_WIDTH // S5_GROUP_CH
S5_STATE = 64
IN_COLS = S5_WIDTH + 2 * CONV_WIDTH
CONV_K = 31
D_FF = (11 * D_MODEL) // 4
FFN_K = 3
PE_DIM = 256
LN_EPS = 1e-5
DT_MIN = 1e-3
DT_MAX = 1e-1
ALPHA = (2.0 * DEPTH) ** 0.25
BETA = (8.0 * DEPTH) ** -0.25

kernel_name = "hymba_s5_conformer_convffn_deepnorm_step"


def _ln(x, g, b):
    xf = x.astype(jnp.float32)
    mu = xf.mean(-1, keepdims=True)
    var = jnp.square(xf - mu).mean(-1, keepdims=True)
    y = (xf - mu) * lax.rsqrt(var + LN_EPS) * g.astype(jnp.float32) + b.astype(jnp.float32)
    return y.astype(x.dtype)


def _causal_dwconv(x, buf, w, b):
    k = w.shape[0]
    c = x.shape[-1]
    xp = jnp.concatenate([buf.astype(x.dtype), x], axis=1)
    y = lax.conv_general_dilated(xp, w.astype(x.dtype)[:, None, :], window_strides=(1,), padding='VALID',
                                 dimension_numbers=('NWC', 'WIO', 'NWC'), feature_group_count=c)
    return y + b.astype(x.dtype), xp[:, xp.shape[1] - (k - 1):]


def _cplx_combine(e1, e2):
    a1r, a1i, b1r, b1i = e1
    a2r, a2i, b2r, b2i = e2
    ar = a1r * a2r - a1i * a2i
    ai = a1r * a2i + a1i * a2r
    br = a2r * b1r - a2i * b1i + b2r
    bi = a2r * b1i + a2i * b1r + b2i
    return (ar, ai, br, bi)


def _s5(u, h0_re, h0_im, lam_re, lam_im, log_dt, b_re, b_im, c_re, c_im, d):
    n, t, _ = u.shape
    f32 = jnp.float32
    uf = u.astype(f32).reshape(n, t, S5_GROUPS, S5_GROUP_CH)
    lr = lam_re.astype(f32)
    li = lam_im.astype(f32)
    dt = jnp.exp(log_dt.astype(f32))[:, None]
    mag = jnp.exp(lr * dt)
    ab_re = mag * jnp.cos(li * dt)
    ab_im = mag * jnp.sin(li * dt)
    num_re = ab_re - 1.0
    num_im = ab_im
    den = lr * lr + li * li
    q_re = (num_re * lr + num_im * li) / den
    q_im = (num_im * lr - num_re * li) / den
    br = b_re.astype(f32)
    bi = b_im.astype(f32)
    bb_re = q_re[..., None] * br - q_im[..., None] * bi
    bb_im = q_re[..., None] * bi + q_im[..., None] * br
    bu_re = jnp.einsum('ntgh,gph->ntgp', uf, bb_re)
    bu_im = jnp.einsum('ntgh,gph->ntgp', uf, bb_im)
    h0r = h0_re.astype(f32)
    h0i = h0_im.astype(f32)
    bu_re = bu_re.at[:, 0].add(ab_re * h0r - ab_im * h0i)
    bu_im = bu_im.at[:, 0].add(ab_re * h0i + ab_im * h0r)
    a_re = jnp.broadcast_to(ab_re, bu_re.shape)
    a_im = jnp.broadcast_to(ab_im, bu_im.shape)
    _, _, h_re, h_im = lax.associative_scan(_cplx_combine, (a_re, a_im, bu_re, bu_im), axis=1)
    y = (jnp.einsum('ghp,ntgp->ntgh', c_re.astype(f32), h_re)
         - jnp.einsum('ghp,ntgp->ntgh', c_im.astype(f32), h_im)
         + d.astype(f32) * uf)
    return y.reshape(n, t, S5_WIDTH), h_re[:, -1], h_im[:, -1]


def _layer(x, p, h0_re, h0_im, conv_buf, ffn_buf, lw):
    (w_in, s5_lam_re, s5_lam_im, s5_log_dt, s5_b_re, s5_b_im, s5_c_re, s5_c_im, s5_d, s5_w_glu,
     conv_w, conv_b, conv_ln_g, conv_ln_b, w_out, ln1_g, ln1_b,
     ffn_w_up, ffn_conv_w, ffn_conv_b, ffn_w_down, ln2_g, ln2_b,
     pe_w, pe_w_gate, ln3_g, ln3_b) = lw
    z = x @ w_in
    u = z[..., :S5_WIDTH]
    cv = z[..., S5_WIDTH:S5_WIDTH + CONV_WIDTH]
    cg = z[..., S5_WIDTH + CONV_WIDTH:]
    y5, h_re, h_im = _s5(u, h0_re, h0_im, s5_lam_re, s5_lam_im, s5_log_dt,
                         s5_b_re, s5_b_im, s5_c_re, s5_c_im, s5_d)
    g5 = jax.nn.gelu(y5.astype(x.dtype))
    s5_out = g5 * jax.nn.sigmoid(g5 @ s5_w_glu)
    c = cv * jax.nn.sigmoid(cg)
    c, new_conv = _causal_dwconv(c, conv_buf, conv_w, conv_b)
    c = jax.nn.silu(_ln(c, conv_ln_g, conv_ln_b))
    mix = jnp.concatenate([s5_out, c], axis=-1) @ w_out
    x = _ln(ALPHA * x + mix, ln1_g, ln1_b)
    hup = x @ ffn_w_up
    hup, new_ffn = _causal_dwconv(hup, ffn_buf, ffn_conv_w, ffn_conv_b)
    gate, val = jnp.split(hup, 2, axis=-1)
    x = _ln(ALPHA * x + (jax.nn.silu(gate) * val) @ ffn_w_down, ln2_g, ln2_b)
    e = p @ pe_w
    x = _ln(ALPHA * x + jax.nn.sigmoid(x @ pe_w_gate) * e, ln3_g, ln3_b)
    return x, h_re, h_im, new_conv, new_ffn


def setup_inputs(seed: int = 0) -> dict:
    key = jax.random.key(seed)
    ks = jax.random.split(key, 48)
    f32 = jnp.float32
    nrm = lambda k, s: jax.random.normal(k, s, f32)
    L = DEPTH
    n_idx = jnp.arange(S5_STATE, dtype=f32)
    inp = {}
    inp['x_prompt'] = nrm(ks[0], (BATCH, SEQ, D_MODEL))
    inp['x_sample'] = nrm(ks[1], (DEC_BATCH, DEC_SEQ, D_MODEL))
    inp['state_s5_re'] = 0.3 * nrm(ks[2], (L, DEC_BATCH, S5_GROUPS, S5_STATE))
    inp['state_s5_im'] = 0.3 * nrm(ks[3], (L, DEC_BATCH, S5_GROUPS, S5_STATE))
    inp['cache_conv'] = 0.5 * nrm(ks[4], (L, DEC_BATCH, CONV_K - 1, CONV_WIDTH))
    inp['cache_ffn_conv'] = nrm(ks[5], (L, DEC_BATCH, FFN_K - 1, 2 * D_FF))
    inp['p_prompt'] = nrm(ks[6], (L, BATCH, SEQ, PE_DIM))
    inp['p_sample'] = nrm(ks[7], (L, DEC_BATCH, DEC_SEQ, PE_DIM))
    inp['w_in'] = nrm(ks[8], (L, D_MODEL, IN_COLS)) * D_MODEL ** -0.5
    inp['s5_lam_re'] = -0.5 + 0.01 * nrm(ks[9], (L, S5_GROUPS, S5_STATE))
    inp['s5_lam_im'] = math.pi * n_idx + 0.01 * nrm(ks[10], (L, S5_GROUPS, S5_STATE))
    inp['s5_log_dt'] = jax.random.uniform(ks[11], (L, S5_GROUPS), f32, math.log(DT_MIN), math.log(DT_MAX))
    inp['s5_b_re'] = nrm(ks[12], (L, S5_GROUPS, S5_STATE, S5_GROUP_CH)) * (2 * S5_GROUP_CH) ** -0.5
    inp['s5_b_im'] = nrm(ks[13], (L, S5_GROUPS, S5_STATE, S5_GROUP_CH)) * (2 * S5_GROUP_CH) ** -0.5
    inp['s5_c_re'] = nrm(ks[14], (L, S5_GROUPS, S5_GROUP_CH, S5_STATE)) * (2 * S5_STATE) ** -0.5
    inp['s5_c_im'] = nrm(ks[15], (L, S5_GROUPS, S5_GROUP_CH, S5_STATE)) * (2 * S5_STATE) ** -0.5
    inp['s5_d'] = nrm(ks[16], (L, S5_GROUPS, S5_GROUP_CH))
    inp['s5_w_glu'] = nrm(ks[17], (L, S5_WIDTH, S5_WIDTH)) * S5_WIDTH ** -0.5
    inp['conv_w'] = nrm(ks[18], (L, CONV_K, CONV_WIDTH)) * CONV_K ** -0.5
    inp['conv_b'] = 0.01 * nrm(ks[19], (L, CONV_WIDTH))
    inp['conv_ln_g'] = 1.0 + 0.02 * nrm(ks[20], (L, CONV_WIDTH))
    inp['conv_ln_b'] = 0.01 * nrm(ks[21], (L, CONV_WIDTH))
    inp['w_out'] = nrm(ks[22], (L, D_MODEL, D_MODEL)) * D_MODEL ** -0.5 * BETA
    inp['ln1_g'] = 1.0 + 0.02 * nrm(ks[23], (L, D_MODEL))
    inp['ln1_b'] = 0.01 * nrm(ks[24], (L, D_MODEL))
    inp['ffn_w_up'] = nrm(ks[25], (L, D_MODEL, 2 * D_FF)) * D_MODEL ** -0.5
    inp['ffn_conv_w'] = nrm(ks[26], (L, FFN_K, 2 * D_FF)) * FFN_K ** -0.5
    inp['ffn_conv_b'] = 0.01 * nrm(ks[27], (L, 2 * D_FF))
    inp['ffn_w_down'] = nrm(ks[28], (L, D_FF, D_MODEL)) * D_FF ** -0.5 * BETA
    inp['ln2_g'] = 1.0 + 0.02 * nrm(ks[29], (L, D_MODEL))
    inp['ln2_b'] = 0.01 * nrm(ks[30], (L, D_MODEL))
    inp['pe_w'] = nrm(ks[31], (L, PE_DIM, D_MODEL)) * PE_DIM ** -0.5 * BETA
    inp['pe_w_gate'] = nrm(ks[32], (L, D_MODEL, D_MODEL)) * D_MODEL ** -0.5
    inp['ln3_g'] = 1.0 + 0.02 * nrm(ks[33], (L, D_MODEL))
    inp['ln3_b'] = 0.01 * nrm(ks[34], (L, D_MODEL))
    return inp


def reference(x_prompt, x_sample, state_s5_re, state_s5_im, cache_conv, cache_ffn_conv, p_prompt, p_sample,
              w_in, s5_lam_re, s5_lam_im, s5_log_dt, s5_b_re, s5_b_im, s5_c_re, s5_c_im, s5_d, s5_w_glu,
              conv_w, conv_b, conv_ln_g, conv_ln_b, w_out, ln1_g, ln1_b,
              ffn_w_up, ffn_conv_w, ffn_conv_b, ffn_w_down, ln2_g, ln2_b,
              pe_w, pe_w_gate, ln3_g, ln3_b):
    nb = x_prompt.shape[0]
    xp = x_prompt
    xs = x_sample
    zero_h = jnp.zeros((nb, S5_GROUPS, S5_STATE), jnp.float32)
    zero_conv = jnp.zeros((nb, CONV_K - 1, CONV_WIDTH), x_prompt.dtype)
    zero_ffn = jnp.zeros((nb, FFN_K - 1, 2 * D_FF), x_prompt.dtype)
    p_re, p_im, p_cv, p_ff = [], [], [], []
    s_re, s_im, s_cv, s_ff = [], [], [], []
    for i in range(DEPTH):
        lw = (w_in[i], s5_lam_re[i], s5_lam_im[i], s5_log_dt[i], s5_b_re[i], s5_b_im[i], s5_c_re[i], s5_c_im[i],
              s5_d[i], s5_w_glu[i], conv_w[i], conv_b[i], conv_ln_g[i], conv_ln_b[i], w_out[i], ln1_g[i], ln1_b[i],
              ffn_w_up[i], ffn_conv_w[i], ffn_conv_b[i], ffn_w_down[i], ln2_g[i], ln2_b[i],
              pe_w[i], pe_w_gate[i], ln3_g[i], ln3_b[i])
        xp, hr, hi, cvb, ffb = _layer(xp, p_prompt[i], zero_h, zero_h, zero_conv, zero_ffn, lw)
        p_re.append(hr); p_im.append(hi); p_cv.append(cvb); p_ff.append(ffb)
        xs, hr, hi, cvb, ffb = _layer(xs, p_sample[i], state_s5_re[i], state_s5_im[i],
                                      cache_conv[i], cache_ffn_conv[i], lw)
        s_re.append(hr); s_im.append(hi); s_cv.append(cvb); s_ff.append(ffb)
    return (xp, xs,
            jnp.stack(p_re), jnp.stack(p_im), jnp.stack(p_cv), jnp.stack(p_ff),
            jnp.stack(s_re), jnp.stack(s_im), jnp.stack(s_cv), jnp.stack(s_ff))
```

```python
import contextlib
import math
import types
import numpy as np
import concourse.bass as bass
import concourse.mybir as mybir
from concourse.bass_utils import run_bass_kernel_spmd

F32 = mybir.dt.float32
BF16 = mybir.dt.bfloat16
I32 = mybir.dt.int32
AF = mybir.ActivationFunctionType
ALU = mybir.AluOpType

DEPTH = 4
D = 2048
KT = 16
NPASS = 4
NPT = 512
NSEQ = 16
NSMP = 64
NTMAX = NPT + NSMP
NCHP = 128
DFF = 5632
FT = 44
ALPHA = (2.0 * DEPTH) ** 0.25
EPS = 1e-5
O_LN1G, O_LN1B, O_LN2G, O_LN2B, O_LN3G, O_LN3B = 0, 16, 32, 48, 64, 80
O_CB, O_CLG, O_CLB, O_D, O_CW, O_FCW, O_FCB = 96, 104, 112, 120, 128, 376, 640
NPAR = 728
ENGS = ("pe", "act", "dve", "pool", "sp")
PI = math.pi


class Op:
    __slots__ = ("eng", "fn", "deps", "dma", "signal", "ev", "pos", "epoch")

    def __init__(self, eng, fn, deps, dma):
        self.eng, self.fn, self.deps, self.dma = eng, fn, deps, dma
        self.signal, self.ev, self.pos, self.epoch = False, None, 0, 0


def _freeze(fn):
    if fn.__closure__ is None:
        return fn
    cells = []
    for c in fn.__closure__:
        try:
            cells.append(types.CellType(c.cell_contents))
        except ValueError:
            cells.append(c)
    g = types.FunctionType(fn.__code__, fn.__globals__, fn.__name__, fn.__defaults__, tuple(cells))
    g.__kwdefaults__ = fn.__kwdefaults__
    return g


class Sched:
    def __init__(self):
        self.ops = []
        self.last_writer = {}
        self.readers = {}
        self.queues = {e: [] for e in ENGS}
        self.bar = None
        self.dmas_since_bar = []
        self.epoch = 0

    def op(self, eng, fn, reads=(), writes=(), dma=None):
        deps = set()
        if self.bar is not None:
            deps.add(self.bar)
        for k in reads:
            w = self.last_writer.get(k)
            if w is not None:
                deps.add(w)
        for k in writes:
            w = self.last_writer.get(k)
            if w is not None:
                deps.add(w)
            deps.update(self.readers.get(k, ()))
        idx = len(self.ops)
        o = Op(eng, _freeze(fn), deps, dma)
        o.pos = len(self.queues[eng])
        o.epoch = self.epoch
        self.ops.append(o)
        self.queues[eng].append(idx)
        if dma is not None:
            self.dmas_since_bar.append(idx)
        for k in reads:
            self.readers.setdefault(k, []).append(idx)
        for k in writes:
            self.last_writer[k] = idx
            self.readers[k] = []
        return idx

    def barrier(self, fn):
        deps = set(self.dmas_since_bar)
        for e in ENGS:
            if self.queues[e]:
                deps.add(self.queues[e][-1])
        if self.bar is not None:
            deps.add(self.bar)
        idx = len(self.ops)
        o = Op("dve", fn, deps, None)
        o.pos = len(self.queues["dve"])
        o.epoch = self.epoch
        self.ops.append(o)
        self.queues["dve"].append(idx)
        self.bar = idx
        self.dmas_since_bar = []
        self.last_writer = {}
        self.readers = {}
        self.epoch += 1

    def emit(self, nc, stack):
        ops = self.ops
        need = [[] for _ in ops]
        for i, o in enumerate(ops):
            for d in o.deps:
                p = ops[d]
                if p.dma is None and p.eng == o.eng:
                    if o.eng in ("pe", "sp"):
                        continue
                    if o.pos - p.pos > 3:
                        continue
                if p.dma is None:
                    p.signal = True
                need[i].append(d)
        cnt = {}
        dcnt = {}
        for o in ops:
            if o.dma is not None:
                dcnt[o.dma] = dcnt.get(o.dma, 0) + 16
                o.ev = (("dma", o.dma), dcnt[o.dma])
            elif o.signal:
                k = ("eng", o.eng, o.epoch // 12)
                cnt[k] = cnt.get(k, 0) + 1
                o.ev = (k, cnt[k])
        sems = {}
        for k in list(cnt.keys()) + [("dma", c) for c in dcnt]:
            sems[k] = stack.enter_context(nc.semaphore("s%d" % len(sems)))
        self.nsems = len(sems)
        block = stack.enter_context(nc.Block())

        def run(engname):
            def body(eng):
                waited = {}
                for idx in self.queues[engname]:
                    o = ops[idx]
                    w = {}
                    for d in need[idx]:
                        sk, v = ops[d].ev
                        if waited.get(sk, 0) >= v:
                            continue
                        if w.get(sk, 0) < v:
                            w[sk] = v
                    for sk, v in w.items():
                        eng.wait_ge(sems[sk], v)
                        waited[sk] = v
                    ins = o.fn(eng)
                    if ins is None:
                        continue
                    if o.dma is not None:
                        ins.then_inc(sems[o.ev[0]], 16)
                    elif o.signal:
                        ins.then_inc(sems[o.ev[0]], 1)
            return body

        block.tensor(run("pe"))
        block.scalar(run("act"))
        block.vector(run("dve"))
        block.gpsimd(run("pool"))
        block.sync(run("sp"))


class Arena:
    def __init__(self, t, nwords):
        self.t, self.n, self.off, self.peak = t, nwords, 0, 0

    def mark(self):
        return self.off

    def rewind(self, m):
        self.off = m

    def _take(self, nw):
        assert self.off + nw <= self.n, ("arena overflow", self.off, nw, self.n)
        ap = self.t[:, self.off:self.off + nw]
        self.off += nw
        self.peak = max(self.peak, self.off)
        return ap

    def f32(self, shape, dt=None):
        n = int(np.prod(shape))
        ap = self._take(n)
        if dt is not None:
            ap = ap.bitcast(dt)
        return self._shape(ap, shape)

    def bf16(self, shape):
        n = int(np.prod(shape))
        ap = self._take((n + 1) // 2).bitcast(BF16)[:, 0:n]
        return self._shape(ap, shape)

    @staticmethod
    def _shape(ap, shape):
        if len(shape) == 1:
            return ap
        names = " ".join("d%d" % i for i in range(len(shape)))
        kw = {"d%d" % i: s for i, s in enumerate(shape)}
        return ap.rearrange("p (%s) -> p %s" % (names, names), **kw)


def build_program(depth=DEPTH, npass=NPASS, debug=False):
    nc = bass.Bass("TRN2", target_bir_lowering=False)
    S = Sched()

    def din(name, shape):
        return nc.dram_tensor(name, list(shape), F32, kind="ExternalInput").ap()

    def dout(name, shape):
        return nc.dram_tensor(name, list(shape), F32, kind="ExternalOutput").ap()

    xin = din("xin", [npass, 128, KT, NTMAX])
    pin = din("pin", [depth, npass, 128, 2, NTMAX])
    par = din("par", [depth, 128, NPAR])
    lamp = din("lamp", [depth, 128, 3, 32])
    ctp = din("ctp", [depth, 128, 2, 32, 32])
    lamf = din("lamf", [depth, 128, 3, 8, 128])
    btf = din("btf", [depth, 128, 2, 8, 128])
    sin_ = din("sin", [depth, 128, 32, 2, NSEQ])
    ccf = din("ccf", [depth, 128, 8, NSEQ, 30])
    ccr = din("ccr", [depth, NSEQ, 30, 1024])
    cff = din("cff", [depth, 128, 88, NSEQ, 2])
    w_in = din("w_in", [depth, D, 3072])
    w_glu = din("w_glu", [depth, 1024, 1024])
    w_out = din("w_out", [depth, D, D])
    w_up = din("w_up", [depth, D, 2 * DFF])
    w_down = din("w_down", [depth, DFF, D])
    w_pe = din("w_pe", [depth, 256, D])
    w_gate = din("w_gate", [depth, D, D])

    o_y = dout("o_y", [npass, 128, KT, NTMAX])
    o_s5p = dout("o_s5p", [depth, 128, 32, 2])
    o_s5s = dout("o_s5s", [depth, 128, 32, 2, NSEQ])
    o_cvp = dout("o_cvp", [depth, 128, 8, 30])
    o_cvsn = dout("o_cvsn", [depth, 128, 8, NSEQ, 4])
    o_cvso = dout("o_cvso", [depth, NSEQ, 26, 1024])
    o_ffp = dout("o_ffp", [depth, 128, 88, 2])
    o_ffs = dout("o_ffs", [depth, 128, 88, NSEQ, 2])
    if debug:
        dbg_g5 = dout("dbg_g5", [128, 8, NTMAX])
        dbg_u = dout("dbg_u", [128, 8, NTMAX])
        dbg_mix = dout("dbg_mix", [128, KT, NTMAX])
        dbg_x1 = dout("dbg_x1", [128, KT, NTMAX])

    xw_scr = nc.dram_tensor("xw_scr", [depth, 2, 128, 4096], BF16, kind="Internal").ap()
    st = contextlib.ExitStack()
    NW = 52000
    arena_t = st.enter_context(nc.sbuf_tensor("arena", [128, NW], F32))
    A = Arena(arena_t, NW)
    psb = [st.enter_context(nc.psum_tensor("ps%d" % i, [128, 512], F32)) for i in range(8)]
    kps = ["ps%d" % i for i in range(8)]

    uid = [0]

    def key(prefix):
        if prefix == "o":
            uid[0] += 1
            return "oshared%d" % (uid[0] % 4)
        uid[0] += 1
        return "%s#%d" % (prefix, uid[0])

    def dve(fn, r=(), w=()):
        S.op("dve", fn, r, w)

    def act(fn, r=(), w=()):
        S.op("act", fn, r, w)

    def pe(fn, r=(), w=()):
        S.op("pe", fn, r, w)

    X = A.f32([KT, NTMAX]);   kX = "X"
    xb = A.bf16([KT, NTMAX]); kxb = "xb"
    ones = A.f32([128])
    scar = A.f32([depth, 32, 2])
    chist = A.f32([depth, 8, 30])
    fhist = A.f32([depth, 88, 2])
    dummy = A.f32([2])
    consts = A.f32([4])
    PAR = A.f32([NPAR])
    wring = [A.bf16([KT, 512]) for _ in range(2)]
    sig = A.f32([NTMAX])
    lnt = tuple(A.f32([512]) for _ in range(4))
    ident = A.f32([128])
    base0_mark = A.mark()
    mixcat = A.bf16([KT, NTMAX])
    base_mark = A.mark()

    dve(lambda e: e.memset(ones, 1.0), w=["ones"])
    dve(lambda e: e.memset(consts[:, 0:1], EPS), w=["consts"])
    dve(lambda e: e.memset(consts[:, 1:2], PI / 2), w=["consts"])
    dve(lambda e: e.memset(scar, 0.0), w=["scar"])
    dve(lambda e: e.memset(chist, 0.0), w=["chist"])
    dve(lambda e: e.memset(fhist, 0.0), w=["fhist"])
    dve(lambda e: e.memset(dummy, 0.0), w=["dummy"])
    epsb = consts[:, 0:1]
    halfpi = consts[:, 1:2]
    ident_i = lnt[0][:, 0:128].bitcast(I32)
    S.op("pool", lambda e: e.iota(out=ident_i, pattern=[[1, 128]], base=0, channel_multiplier=-1), writes=["ident_i"])
    dve(lambda e: e.tensor_scalar(out=ident, in0=ident_i, scalar1=0.0, scalar2=None, op0=ALU.is_equal), r=["ident_i"], w=["ident"])
    S.barrier(lambda e: e.memset(dummy, 0.0))

    ring_i = [0]

    def load_w(src_ap, nk, ncols):
        s = ring_i[0] % 2
        ring_i[0] += 1
        dst = wring[s].rearrange("p k c -> p (k c)")[:, 0:nk * ncols].rearrange("p (k c) -> p k c", k=nk)
        srcv = src_ap.rearrange("(k p) c -> p k c", p=128)
        S.op("pool", lambda e: e.dma_start(out=dst, in_=srcv), writes=["wr%da" % s, "wr%db" % s], dma="wr%da" % s)
        return dst, ["wr%da" % s, "wr%db" % s]

    def barrier():
        S.barrier(lambda e: e.memset(dummy, 0.0))

    def layernorm(src, ksrc, cols_list, goff, boff, ntile, dst_f32, kdst, dst_bf, kdstb, func=None):
        mean, rstd, sq, t1 = lnt
        inv = 1.0 / (ntile * 128)
        for (c0, n) in cols_list:
            ps_s, ps_q = psb[6], psb[7]
            for m in range(ntile):
                pe(lambda e, m=m: e.matmul(ps_s[:, 0:n], ones, src[:, m, c0:c0 + n], start=(m == 0), stop=(m == ntile - 1)),
                   r=[ksrc, "ones"], w=["ps6"])
            for m in range(ntile):
                act(lambda e, m=m: e.activation(out=sq[:, 0:n], in_=src[:, m, c0:c0 + n], func=AF.Square), r=[ksrc], w=["ln_sq"])
                pe(lambda e, m=m: e.matmul(ps_q[:, 0:n], ones, sq[:, 0:n], start=(m == 0), stop=(m == ntile - 1)),
                   r=["ln_sq", "ones"], w=["ps7"])
            act(lambda e: e.activation(out=mean[:, 0:n], in_=ps_s[:, 0:n], func=AF.Identity, scale=inv), r=["ps6"], w=["ln_mean"])
            dve(lambda e: e.tensor_tensor(out=t1[:, 0:n], in0=mean[:, 0:n], in1=mean[:, 0:n], op=ALU.mult), r=["ln_mean"], w=["ln_t1"])
            dve(lambda e: e.scalar_tensor_tensor(out=rstd[:, 0:n], in0=ps_q[:, 0:n], scalar=inv, in1=t1[:, 0:n], op0=ALU.mult, op1=ALU.subtract),
                r=["ps7", "ln_t1"], w=["ln_rstd"])
            act(lambda e: e.activation(out=rstd[:, 0:n], in_=rstd[:, 0:n], func=AF.Ln, bias=epsb), r=["ln_rstd", "consts"], w=["ln_rstd"])
            act(lambda e: e.activation(out=rstd[:, 0:n], in_=rstd[:, 0:n], func=AF.Exp, scale=-0.5), r=["ln_rstd"], w=["ln_rstd"])
            for m in range(ntile):
                dve(lambda e, m=m: e.tensor_tensor(out=t1[:, 0:n], in0=src[:, m, c0:c0 + n], in1=mean[:, 0:n], op=ALU.subtract),
                    r=[ksrc, "ln_mean"], w=["ln_t1"])
                dve(lambda e, m=m: e.tensor_tensor(out=t1[:, 0:n], in0=t1[:, 0:n], in1=rstd[:, 0:n], op=ALU.mult),
                    r=["ln_t1", "ln_rstd"], w=["ln_t1"])
                if dst_f32 is not None:
                    act(lambda e, m=m: e.activation(out=dst_f32[:, m, c0:c0 + n], in_=t1[:, 0:n], func=AF.Identity,
                                                    scale=PAR[:, goff + m:goff + m + 1], bias=PAR[:, boff + m:boff + m + 1]),
                        r=["ln_t1", "PAR"], w=[kdst])
                    act(lambda e, m=m: e.activation(out=dst_bf[:, m, c0:c0 + n], in_=t1[:, 0:n], func=AF.Identity,
                                                    scale=PAR[:, goff + m:goff + m + 1], bias=PAR[:, boff + m:boff + m + 1]),
                        r=["ln_t1", "PAR"], w=[kdstb])
                else:
                    act(lambda e, m=m: e.activation(out=dst_bf[:, m, c0:c0 + n], in_=t1[:, 0:n], func=func,
                                                    scale=PAR[:, goff + m:goff + m + 1], bias=PAR[:, boff + m:boff + m + 1]),
                        r=["ln_t1", "PAR"], w=[kdstb])

    def matmul_tiles(wt, kw, mi, nk, rhs, krhs, cols, m):
        banks = []
        for gi, (c0, n) in enumerate(cols):
            b = (m % 2) * 2 + gi
            banks.append(b)
            for k in range(nk):
                pe(lambda e, b=b, k=k, c0=c0, n=n: e.matmul(psb[b][:, 0:n], wt[:, k, mi * 128:(mi + 1) * 128], rhs[:, k, c0:c0 + n],
                                                            start=(k == 0), stop=(k == nk - 1)),
                   r=kw + [krhs], w=[kps[b]])
        return banks

    def pw_setup(tag, lr, li, ldt, F):
        kk = lambda s_: "%s_%s" % (tag, s_)
        c = dict(tag=tag, lr=lr, li=li, F=F, kk=kk)
        c["dt"] = A.f32([F]); c["th"] = A.f32([F]); c["ld"] = A.f32([F])
        c["t0"] = A.f32([F]); c["t1"] = A.f32([F]); c["t2"] = A.f32([F]); c["ti"] = A.f32([F], dt=I32)
        dt, th, ld = c["dt"], c["th"], c["ld"]
        act(lambda e: e.activation(out=dt, in_=ldt, func=AF.Exp), r=[kk("in")], w=[kk("dt")])
        dve(lambda e: e.tensor_tensor(out=th, in0=li, in1=dt, op=ALU.mult), r=[kk("in"), kk("dt")], w=[kk("th")])
        dve(lambda e: e.tensor_tensor(out=ld, in0=lr, in1=dt, op=ALU.mult), r=[kk("in"), kk("dt")], w=[kk("ld")])
        return c

    def pw_power(c, n, pr, pi_, kp):
        kk = c["kk"]
        th, ld, t0, t1, t2, ti = c["th"], c["ld"], c["t0"], c["t1"], c["t2"], c["ti"]
        dve(lambda e: e.tensor_scalar(out=t0, in0=th, scalar1=float(n), scalar2=None, op0=ALU.mult), r=[kk("th")], w=[kk("t0")])
        dve(lambda e: e.tensor_scalar(out=t1, in0=t0, scalar1=1.0 / (2 * PI), scalar2=None, op0=ALU.mult), r=[kk("t0")], w=[kk("t1")])
        dve(lambda e: e.tensor_copy(out=ti, in_=t1), r=[kk("t1")], w=[kk("ti")])
        dve(lambda e: e.tensor_copy(out=t1, in_=ti), r=[kk("ti")], w=[kk("t1")])
        dve(lambda e: e.scalar_tensor_tensor(out=t0, in0=t1, scalar=-2 * PI, in1=t0, op0=ALU.mult, op1=ALU.add), r=[kk("t1"), kk("t0")], w=[kk("t0")])
        dve(lambda e: e.tensor_scalar(out=t1, in0=t0, scalar1=PI, scalar2=-2 * PI, op0=ALU.is_gt, op1=ALU.mult), r=[kk("t0")], w=[kk("t1")])
        dve(lambda e: e.tensor_tensor(out=t0, in0=t0, in1=t1, op=ALU.add), r=[kk("t0"), kk("t1")], w=[kk("t0")])
        dve(lambda e: e.tensor_scalar(out=t1, in0=t0, scalar1=-PI, scalar2=2 * PI, op0=ALU.is_lt, op1=ALU.mult), r=[kk("t0")], w=[kk("t1")])
        dve(lambda e: e.tensor_tensor(out=t0, in0=t0, in1=t1, op=ALU.add), r=[kk("t0"), kk("t1")], w=[kk("t0")])
        dve(lambda e: e.tensor_scalar(out=t0, in0=t0, scalar1=-3.1415925, scalar2=3.1415925, op0=ALU.max, op1=ALU.min), r=[kk("t0")], w=[kk("t0")])
        act(lambda e: e.activation(out=t1, in_=t0, func=AF.Sin), r=[kk("t0")], w=[kk("t1")])
        dve(lambda e: e.scalar_tensor_tensor(out=t0, in0=t0, scalar=-1.0, in1=t0, op0=ALU.mult, op1=ALU.max), r=[kk("t0")], w=[kk("t0")])
        act(lambda e: e.activation(out=t0, in_=t0, func=AF.Sin, scale=-1.0, bias=halfpi), r=[kk("t0"), "consts"], w=[kk("t0")])
        act(lambda e: e.activation(out=t2, in_=ld, func=AF.Exp, scale=float(n)), r=[kk("ld")], w=[kk("t2")])
        dve(lambda e: e.tensor_tensor(out=pr, in0=t2, in1=t0, op=ALU.mult), r=[kk("t2"), kk("t0")], w=[kp])
        dve(lambda e: e.tensor_tensor(out=pi_, in0=t2, in1=t1, op=ALU.mult), r=[kk("t2"), kk("t1")], w=[kp])

    def pw_q(c, p1r, p1i, kp1, qr, qi, kq):
        kk = c["kk"]
        lr, li, t0, t1, t2 = c["lr"], c["li"], c["t0"], c["t1"], c["t2"]
        dve(lambda e: e.tensor_tensor(out=t0, in0=lr, in1=lr, op=ALU.mult), r=[kk("in")], w=[kk("t0")])
        dve(lambda e: e.tensor_tensor(out=t1, in0=li, in1=li, op=ALU.mult), r=[kk("in")], w=[kk("t1")])
        dve(lambda e: e.tensor_tensor(out=t0, in0=t0, in1=t1, op=ALU.add), r=[kk("t0"), kk("t1")], w=[kk("t0")])
        dve(lambda e: e.reciprocal(out=t0, in_=t0), r=[kk("t0")], w=[kk("t0")])
        dve(lambda e: e.tensor_scalar(out=t1, in0=p1r, scalar1=-1.0, scalar2=None, op0=ALU.add), r=[kp1], w=[kk("t1")])
        dve(lambda e: e.tensor_tensor(out=qr, in0=t1, in1=lr, op=ALU.mult), r=[kk("t1"), kk("in")], w=[kq])
        dve(lambda e: e.tensor_tensor(out=t2, in0=p1i, in1=li, op=ALU.mult), r=[kp1, kk("in")], w=[kk("t2")])
        dve(lambda e: e.tensor_tensor(out=qr, in0=qr, in1=t2, op=ALU.add), r=[kq, kk("t2")], w=[kq])
        dve(lambda e: e.tensor_tensor(out=qr, in0=qr, in1=t0, op=ALU.mult), r=[kq, kk("t0")], w=[kq])
        dve(lambda e: e.tensor_tensor(out=qi, in0=p1i, in1=lr, op=ALU.mult), r=[kp1, kk("in")], w=[kq])
        dve(lambda e: e.tensor_tensor(out=t2, in0=t1, in1=li, op=ALU.mult), r=[kk("t1"), kk("in")], w=[kk("t2")])
        dve(lambda e: e.tensor_tensor(out=qi, in0=qi, in1=t2, op=ALU.subtract), r=[kq, kk("t2")], w=[kq])
        dve(lambda e: e.tensor_tensor(out=qi, in0=qi, in1=t0, op=ALU.mult), r=[kq, kk("t0")], w=[kq])

    def s5_stage(l, p, NS, cols, u, g5, last):
        NSQ = NS // 4
        NC = NCHP + NSQ
        s5_mark = A.mark()
        LP = A.f32([3, 32])
        S.op("sp", lambda e: e.dma_start(out=LP, in_=lamp[l]), writes=["P_in"], dma="P_in")
        cP = pw_setup("P", LP[:, 0], LP[:, 1], LP[:, 2], 32)
        PP = {}
        npi = {}
        Ta = {}
        Tb = {}
        tpr = A.f32([32]); tpi = A.f32([32])
        for n in (1, 2, 3, 4, 8, 12, 16, 20, 24, 28, 32):
            if n <= 4:
                pr = A.f32([32]); pi_ = A.f32([32]); t = A.f32([32])
                kpn = "P_P%d" % n
            else:
                pr, pi_, t = tpr, tpi, None
                kpn = "P_Pt"
            pw_power(cP, n, pr, pi_, kpn)
            if t is not None:
                dve(lambda e, t=t, pi_=pi_: e.tensor_scalar(out=t, in0=pi_, scalar1=-1.0, scalar2=None, op0=ALU.mult), r=[kpn], w=["P_npi%d" % n])
                PP[n] = (pr, pi_, kpn)
                npi[n] = t
            if n >= 4:
                ta = A.f32([32, 2]); tb = A.f32([32, 2])
                dve(lambda e, ta=ta, pr=pr: e.tensor_copy(out=ta[:, :, 0], in_=pr), r=[kpn], w=["P_T%d" % n])
                dve(lambda e, ta=ta, pr=pr: e.tensor_copy(out=ta[:, :, 1], in_=pr), r=[kpn], w=["P_T%d" % n])
                dve(lambda e, tb=tb, pi_=pi_: e.tensor_scalar(out=tb[:, :, 0], in0=pi_, scalar1=-1.0, scalar2=None, op0=ALU.mult), r=[kpn], w=["P_T%d" % n])
                dve(lambda e, tb=tb, pi_=pi_: e.tensor_copy(out=tb[:, :, 1], in_=pi_), r=[kpn], w=["P_T%d" % n])
                Ta[n], Tb[n] = ta, tb
        CTb = A.bf16([2, 32, 32])
        S.op("pool", lambda e: e.dma_start(out=CTb, in_=ctp[l]), writes=["CTb"], dma="CTb")
        dve(lambda e: e.tensor_scalar(out=CTb[:, 1], in0=CTb[:, 1], scalar1=-1.0, scalar2=None, op0=ALU.mult), r=["CTb"], w=["CTb"])
        half_mark = A.mark()
        for hf in range(2):
            A.rewind(half_mark)
            barrier()
            XW = A.bf16([4, 2, 4, 128])
            fm = A.mark()
            if p == 0:
                LF = A.f32([3, 4, 128])
                BT = A.f32([2, 4, 128])
                S.op("sp", lambda e, hf=hf: e.dma_start(out=LF, in_=lamf[l][:, :, 4 * hf:4 * hf + 4, :]), writes=["F_in"], dma="F_in")
                S.op("sp", lambda e, hf=hf: e.dma_start(out=BT, in_=btf[l][:, :, 4 * hf:4 * hf + 4, :]), writes=["BT"], dma="BT")
                fl = lambda ap: ap.rearrange("p a b -> p (a b)")
                cF = pw_setup("F", fl(LF[:, 0]), fl(LF[:, 1]), fl(LF[:, 2]), 512)
                t0, t1, t2 = cF["t0"], cF["t1"], cF["t2"]
                btr, bti = fl(BT[:, 0]), fl(BT[:, 1])
                pr = A.f32([512]); pi_ = A.f32([512]); qr = A.f32([512]); qi = A.f32([512])
                vr = A.f32([512]); vi = A.f32([512])
                kq, kp, kv = "F_q", "F_P", "F_V"
                for n in range(4):
                    if n == 0:
                        pw_power(cF, 1, pr, pi_, kp)
                        pw_q(cF, pr, pi_, kp, qr, qi, kq)
                        ur, ui, ku = qr, qi, kq
                    else:
                        if n > 1:
                            pw_power(cF, n, pr, pi_, kp)
                        dve(lambda e: e.tensor_tensor(out=vr, in0=pr, in1=qr, op=ALU.mult), r=[kp, kq], w=[kv])
                        dve(lambda e: e.tensor_tensor(out=t0, in0=pi_, in1=qi, op=ALU.mult), r=[kp, kq], w=["F_t0"])
                        dve(lambda e: e.tensor_tensor(out=vr, in0=vr, in1=t0, op=ALU.subtract), r=[kv, "F_t0"], w=[kv])
                        dve(lambda e: e.tensor_tensor(out=vi, in0=pr, in1=qi, op=ALU.mult), r=[kp, kq], w=[kv])
                        dve(lambda e: e.tensor_tensor(out=t0, in0=pi_, in1=qr, op=ALU.mult), r=[kp, kq], w=["F_t0"])
                        dve(lambda e: e.tensor_tensor(out=vi, in0=vi, in1=t0, op=ALU.add), r=[kv, "F_t0"], w=[kv])
                        ur, ui, ku = vr, vi, kv
                    xr = fl(XW[:, n, 0]); xi = fl(XW[:, n, 1])
                    dve(lambda e, ur=ur: e.tensor_tensor(out=t1, in0=ur, in1=btr, op=ALU.mult), r=[ku, "BT"], w=["F_t1"])
                    dve(lambda e, ui=ui: e.tensor_tensor(out=t2, in0=ui, in1=bti, op=ALU.mult), r=[ku, "BT"], w=["F_t2"])
                    dve(lambda e, xr=xr: e.tensor_tensor(out=xr, in0=t1, in1=t2, op=ALU.subtract), r=["F_t1", "F_t2"], w=["XW"])
                    dve(lambda e, ur=ur: e.tensor_tensor(out=t1, in0=ur, in1=bti, op=ALU.mult), r=[ku, "BT"], w=["F_t1"])
                    dve(lambda e, ui=ui: e.tensor_tensor(out=t2, in0=ui, in1=btr, op=ALU.mult), r=[ku, "BT"], w=["F_t2"])
                    dve(lambda e, xi=xi: e.tensor_tensor(out=xi, in0=t1, in1=t2, op=ALU.add), r=["F_t1", "F_t2"], w=["XW"])
                S.op("sp", lambda e, hf=hf: e.dma_start(out=xw_scr[l, hf], in_=XW.rearrange("p a b c d -> p (a b c d)")), reads=["XW"], writes=[key("xwscr")], dma=key("o"))
            else:
                S.op("sp", lambda e, hf=hf: e.dma_start(out=XW.rearrange("p a b c d -> p (a b c d)"), in_=xw_scr[l, hf]), writes=["XW"], dma="XWld")
            A.rewind(fm)
            SP = A.f32([16, 2, NCHP + 1])
            XS = A.f32([16, 2, NSEQ])
            H0 = A.f32([16, 2, NSEQ])
            SO = A.f32([16, 2, NSEQ])
            CB = A.f32([16, 2, 17])
            st1 = A.f32([16, 2, 16]); st2 = A.f32([16, 2, 16])
            Hf = [A.bf16([4, 2, NCHP + NSEQ]) for _ in range(2)]
            tmpc = A.f32([NCHP + NSEQ])
            ysb = A.f32([NTMAX])
            dve(lambda e: e.memset(SP[:, :, :, 0:1], 0.0), r=["XW", "F_t1", "F_t2", "F_t0", "F_in", "BT", "F_q", "F_V", "F_P", "F_th", "F_ld", "F_dt", "F_ti"],
                w=["SP", "XS", "H0", "SO", "CB", "st1", "st2", "Hf0", "Hf1", "tmpc", "ysb"])
            dve(lambda e, hf=hf: e.tensor_copy(out=SP[:, :, :, 0], in_=scar[:, l, 16 * hf:16 * hf + 16, :]), r=["scar"], w=["SP"])
            if NS:
                S.op("sp", lambda e, hf=hf: e.dma_start(out=H0, in_=sin_[l][:, 16 * hf:16 * hf + 16]), writes=["H0"], dma="H0")
            p4r, p4i, kp4 = PP[4]
            sl = slice(16 * hf, 16 * hf + 16)
            for ii in range(16):
                i = 16 * hf + ii
                tl, ip = ii // 4, ii % 4
                t = i // 4
                rows = slice(32 * ip, 32 * ip + 32)
                for ri in range(2):
                    b = (ii * 2 + ri) % 2
                    for s in range(4):
                        pe(lambda e, b=b, s=s, ri=ri, tl=tl, t=t, rows=rows, ip=ip: e.matmul(
                            psb[b][:, 0:NC], XW[rows, 3 - s, ri, tl, :], u[rows, t, s:4 * NC:4],
                            start=(s == 0), stop=(s == 3), tile_position=(32 * ip, 0)),
                           r=["XW", "u"], w=[kps[b]])
                    act(lambda e, b=b, ii=ii, ri=ri: e.activation(out=SP[:, ii, ri, 1:NCHP + 1], in_=psb[b][:, 0:NCHP], func=AF.Copy),
                        r=[kps[b]], w=["SP"])
                    if NS:
                        act(lambda e, b=b, ii=ii, ri=ri: e.activation(out=XS[:, ii, ri, 0:NSQ], in_=psb[b][:, NCHP:NC], func=AF.Copy),
                            r=[kps[b]], w=["XS"])
            SPb = SP[:, :, :, 1:NCHP + 1].rearrange("p a r (b k) -> p a r b k", k=8)
            tab = {n_: Ta[n_][:, sl, :].unsqueeze(3).to_broadcast([128, 16, 2, 16]) for n_ in Ta}
            tbb = {n_: Tb[n_][:, sl, :].unsqueeze(3).to_broadcast([128, 16, 2, 16]) for n_ in Tb}
            for k in range(1, 8):
                dve(lambda e, k=k: e.tensor_tensor(out=st1, in0=SPb[:, :, :, :, k - 1], in1=tab[4], op=ALU.mult), r=["SP", "P_T4"], w=["st1"])
                dve(lambda e, k=k: e.tensor_tensor(out=st2, in0=SPb[:, :, ::-1, :, k - 1], in1=tbb[4], op=ALU.mult), r=["SP", "P_T4"], w=["st2"])
                dve(lambda e, k=k: e.tensor_tensor(out=SPb[:, :, :, :, k], in0=SPb[:, :, :, :, k], in1=st1, op=ALU.add), r=["SP", "st1"], w=["SP"])
                dve(lambda e, k=k: e.tensor_tensor(out=SPb[:, :, :, :, k], in0=SPb[:, :, :, :, k], in1=st2, op=ALU.add), r=["SP", "st2"], w=["SP"])
            dve(lambda e: e.tensor_copy(out=CB[:, :, :, 0], in_=SP[:, :, :, 0]), r=["SP"], w=["CB"])
            for b_ in range(16):
                dve(lambda e, b_=b_: e.tensor_tensor(out=st1[:, :, :, 0], in0=CB[:, :, :, b_], in1=Ta[32][:, sl, :], op=ALU.mult), r=["CB", "P_T32"], w=["st1"])
                dve(lambda e, b_=b_: e.tensor_tensor(out=st2[:, :, :, 0], in0=CB[:, :, ::-1, b_], in1=Tb[32][:, sl, :], op=ALU.mult), r=["CB", "P_T32"], w=["st2"])
                dve(lambda e, b_=b_: e.tensor_tensor(out=st1[:, :, :, 0], in0=st1[:, :, :, 0], in1=st2[:, :, :, 0], op=ALU.add), r=["st1", "st2"], w=["st1"])
                dve(lambda e, b_=b_: e.tensor_tensor(out=CB[:, :, :, b_ + 1], in0=SPb[:, :, :, b_, 7], in1=st1[:, :, :, 0], op=ALU.add), r=["SP", "st1", "CB"], w=["CB"])
            for k in range(8):
                n_ = 4 * (k + 1)
                dve(lambda e, k=k, n_=n_: e.tensor_tensor(out=st1, in0=CB[:, :, :, 0:16], in1=tab[n_], op=ALU.mult), r=["CB", "P_T%d" % n_], w=["st1"])
                dve(lambda e, k=k, n_=n_: e.tensor_tensor(out=st2, in0=CB[:, :, ::-1, 0:16], in1=tbb[n_], op=ALU.mult), r=["CB", "P_T%d" % n_], w=["st2"])
                dve(lambda e, k=k: e.tensor_tensor(out=SPb[:, :, :, :, k], in0=SPb[:, :, :, :, k], in1=st1, op=ALU.add), r=["SP", "st1"], w=["SP"])
                dve(lambda e, k=k: e.tensor_tensor(out=SPb[:, :, :, :, k], in0=SPb[:, :, :, :, k], in1=st2, op=ALU.add), r=["SP", "st2"], w=["SP"])
            dve(lambda e, hf=hf: e.tensor_copy(out=scar[:, l, 16 * hf:16 * hf + 16, :], in_=SP[:, :, :, NCHP]), r=["SP"], w=["scar"])
            if NS:
                for ri in range(2):
                    for ii in range(16):
                        i = 16 * hf + ii
                        dve(lambda e, ii=ii, i=i, ri=ri: e.scalar_tensor_tensor(out=SO[:, ii, ri, :], in0=H0[:, ii, ri, :], scalar=p4r[:, i:i + 1],
                                                                               in1=XS[:, ii, ri, :], op0=ALU.mult, op1=ALU.add),
                            r=["H0", "XS", kp4], w=["SO"])
                        sc = npi[4] if ri == 0 else p4i
                        dve(lambda e, ii=ii, i=i, ri=ri, sc=sc: e.scalar_tensor_tensor(out=SO[:, ii, ri, :], in0=H0[:, ii, 1 - ri, :], scalar=sc[:, i:i + 1],
                                                                                      in1=SO[:, ii, ri, :], op0=ALU.mult, op1=ALU.add),
                            r=["H0", kp4, "P_npi4", "SO"], w=["SO"])
                S.op("sp", lambda e, hf=hf: e.dma_start(out=o_s5s[l][:, 16 * hf:16 * hf + 16], in_=SO), reads=["SO"], writes=[key("o_s5s")], dma=key("o"))
            for tl in range(4):
                t = 4 * hf + tl
                ybase = 4 + (tl % 2) * 2
                for ip in range(4):
                    ii = tl * 4 + ip
                    i = 16 * hf + ii
                    rows = slice(32 * ip, 32 * ip + 32)
                    hb = Hf[ii % 2]
                    khf = "Hf%d" % (ii % 2)
                    for j in range(3):
                        pjr, pji, kpj = PP[j + 1]
                        for ri in range(2):
                            b = (j * 2 + ri) % 4
                            for s in range(j + 1):
                                pe(lambda e, b=b, s=s, j=j, ri=ri, tl=tl, t=t, rows=rows, ip=ip: e.matmul(
                                    psb[b][:, 0:NC], XW[rows, j - s, ri, tl, :], u[rows, t, s:4 * NC:4],
                                    start=(s == 0), stop=(s == j), tile_position=(32 * ip, 0)),
                                   r=["XW", "u"], w=[kps[b]])
                            sc1 = pjr
                            sc2 = npi[j + 1] if ri == 0 else pji
                            groups = [(0, NCHP, SP[:, ii, ri, 0:NCHP], SP[:, ii, 1 - ri, 0:NCHP], "SP")]
                            if NS:
                                groups.append((NCHP, NSQ, H0[:, ii, ri, :], H0[:, ii, 1 - ri, :], "H0"))
                            for (c0, n, sa, sb_, ks) in groups:
                                dve(lambda e, b=b, c0=c0, n=n, sa=sa, i=i, sc1=sc1: e.scalar_tensor_tensor(
                                    out=tmpc[:, c0:c0 + n], in0=sa, scalar=sc1[:, i:i + 1], in1=psb[b][:, c0:c0 + n], op0=ALU.mult, op1=ALU.add),
                                    r=[ks, kps[b], kpj], w=["tmpc"])
                                dve(lambda e, c0=c0, n=n, sb_=sb_, i=i, sc2=sc2, hb=hb, j=j, ri=ri: e.scalar_tensor_tensor(
                                    out=hb[:, j, ri, c0:c0 + n], in0=sb_, scalar=sc2[:, i:i + 1], in1=tmpc[:, c0:c0 + n], op0=ALU.mult, op1=ALU.add),
                                    r=[ks, "tmpc", kpj, "P_npi%d" % (j + 1)], w=[khf])
                    for ri in range(2):
                        act(lambda e, ii=ii, ri=ri, hb=hb: e.activation(out=hb[:, 3, ri, 0:NCHP], in_=SP[:, ii, ri, 1:NCHP + 1], func=AF.Copy), r=["SP"], w=[khf])
                        if NS:
                            act(lambda e, ii=ii, ri=ri, hb=hb: e.activation(out=hb[:, 3, ri, NCHP:NC], in_=SO[:, ii, ri, :], func=AF.Copy), r=["SO"], w=[khf])
                    for j in range(4):
                        for ri in range(2):
                            pe(lambda e, ip=ip, j=j, ri=ri, i=i, hb=hb, ybase=ybase: e.matmul(
                                psb[ybase][32 * ip:32 * ip + 32, j:NPT:4], CTb[:, ri, i, :], hb[:, j, ri, 0:NCHP],
                                start=(ri == 0), stop=(ri == 1), tile_position=(0, 32 * ip)),
                               r=["CTb", khf], w=[kps[ybase]])
                            if NS:
                                pe(lambda e, ip=ip, j=j, ri=ri, i=i, hb=hb, ybase=ybase: e.matmul(
                                    psb[ybase + 1][32 * ip:32 * ip + 32, j:NS:4], CTb[:, ri, i, :], hb[:, j, ri, NCHP:NC],
                                    start=(ri == 0), stop=(ri == 1), tile_position=(0, 32 * ip)),
                                   r=["CTb", khf], w=[kps[ybase + 1]])
                for gi, (c0, n) in enumerate(cols):
                    dve(lambda e, t=t, c0=c0, n=n, b=ybase + gi: e.scalar_tensor_tensor(
                        out=ysb[:, c0:c0 + n], in0=u[:, t, c0:c0 + n], scalar=PAR[:, O_D + t:O_D + t + 1], in1=psb[b][:, 0:n], op0=ALU.mult, op1=ALU.add),
                        r=["u", "PAR", kps[ybase + gi]], w=["ysb"])
                    act(lambda e, t=t, c0=c0, n=n: e.activation(out=g5[:, t, c0:c0 + n], in_=ysb[:, c0:c0 + n], func=AF.Gelu_apprx_tanh),
                        r=["ysb"], w=["g5"])
        if last:
            S.op("sp", lambda e: e.dma_start(out=o_s5p[l], in_=scar[:, l]), reads=["scar"], writes=[key("o_s5p")], dma=key("o"))
        A.rewind(s5_mark)

    for p in range(npass):
        last = (p == npass - 1)
        NS = NSMP if last else 0
        NT = NPT + NS
        cols = [(0, NPT)] + ([(NPT, NS)] if NS else [])
        S.op("sp", lambda e, p=p: e.dma_start(out=X, in_=xin[p]), writes=[kX], dma="X")
        dve(lambda e: e.tensor_copy(out=xb, in_=X), r=[kX], w=[kxb])
        for l in range(depth):
            S.op("sp", lambda e, l=l: e.dma_start(out=PAR, in_=par[l]), writes=["PAR"], dma="PAR")
            A.rewind(base_mark)
            u = A.bf16([8, NTMAX])
            g5 = A.bf16([8, NTMAX])
            for ci in range(2):
                wt, kw = load_w(w_in[l][:, ci * 512:(ci + 1) * 512], KT, 512)
                for mi in range(4):
                    m = ci * 4 + mi
                    banks = matmul_tiles(wt, kw, mi, KT, xb, kxb, cols, m)
                    for gi, (c0, n) in enumerate(cols):
                        b = banks[gi]
                        act(lambda e, b=b, m=m, c0=c0, n=n: e.activation(out=u[:, m, c0:c0 + n], in_=psb[b][:, 0:n], func=AF.Copy),
                            r=[kps[b]], w=["u"])
            s5_stage(l, p, NS, cols, u, g5, last)
            if debug and last and l == 0:
                S.op("pool", lambda e: e.dma_start(out=dbg_g5, in_=g5), reads=["g5"], writes=[key("o_dbg")], dma=key("o"))
                S.op("pool", lambda e: e.dma_start(out=dbg_u, in_=u), reads=["u"], writes=[key("o_dbg")], dma=key("o"))
            for ci in range(2):
                wt, kw = load_w(w_glu[l][:, ci * 512:(ci + 1) * 512], 8, 512)
                for mi in range(4):
                    m = ci * 4 + mi
                    banks = matmul_tiles(wt, kw, mi, 8, g5, "g5", cols, m)
                    for gi, (c0, n) in enumerate(cols):
                        b = banks[gi]
                        act(lambda e, b=b, c0=c0, n=n: e.activation(out=sig[:, c0:c0 + n], in_=psb[b][:, 0:n], func=AF.Sigmoid),
                            r=[kps[b]], w=["sig"])
                        dve(lambda e, m=m, c0=c0, n=n: e.tensor_tensor(out=mixcat[:, m, c0:c0 + n], in0=g5[:, m, c0:c0 + n], in1=sig[:, c0:c0 + n], op=ALU.mult),
                            r=["g5", "sig"], w=["mixcat"])
            barrier()

            A.rewind(base_mark)
            cv = A.f32([8, NTMAX])
            convy = A.f32([8, NTMAX])
            cbp = A.bf16([8, 30 + NPT])
            cbs = A.bf16([8, NSEQ, 34])
            csn = A.f32([8, NSEQ, 4])
            dg = [A.bf16([31, 128]) for _ in range(2)]
            dve(lambda e, l=l: e.tensor_copy(out=cbp[:, :, 0:30], in_=chist[:, l]), r=["chist"], w=["cbp"])
            if NS:
                for hh in range(2):
                    S.op("pool", lambda e, l=l, hh=hh: e.dma_start(out=cbs[:, 4 * hh:4 * hh + 4, :, 0:30], in_=ccf[l][:, 4 * hh:4 * hh + 4]),
                         writes=["cbs"], dma="cbs%d" % hh)
            for ci in range(2, 6):
                wt, kw = load_w(w_in[l][:, ci * 512:(ci + 1) * 512], KT, 512)
                for mi in range(4):
                    m = ci * 4 + mi
                    banks = matmul_tiles(wt, kw, mi, KT, xb, kxb, cols, m)
                    for gi, (c0, n) in enumerate(cols):
                        b = banks[gi]
                        if m < 16:
                            act(lambda e, b=b, m=m, c0=c0, n=n: e.activation(out=cv[:, m - 8, c0:c0 + n], in_=psb[b][:, 0:n], func=AF.Copy),
                                r=[kps[b]], w=["cv"])
                        else:
                            mm = m - 16
                            act(lambda e, b=b, c0=c0, n=n: e.activation(out=sig[:, c0:c0 + n], in_=psb[b][:, 0:n], func=AF.Sigmoid),
                                r=[kps[b]], w=["sig"])
                            if gi == 0:
                                dve(lambda e, mm=mm: e.tensor_tensor(out=cbp[:, mm, 30:30 + NPT], in0=cv[:, mm, 0:NPT], in1=sig[:, 0:NPT], op=ALU.mult),
                                    r=["cv", "sig"], w=["cbp"])
                                dve(lambda e, mm=mm, l=l: e.tensor_tensor(out=chist[:, l, mm, :], in0=cv[:, mm, NPT - 30:NPT], in1=sig[:, NPT - 30:NPT], op=ALU.mult),
                                    r=["cv", "sig", "cbp"], w=["chist"])
                            else:
                                dve(lambda e, mm=mm: e.tensor_tensor(out=csn[:, mm],
                                                                     in0=cv[:, mm, NPT:NPT + NS].rearrange("p (s t) -> p s t", t=4),
                                                                     in1=sig[:, NPT:NPT + NS].rearrange("p (s t) -> p s t", t=4), op=ALU.mult),
                                    r=["cv", "sig"], w=["csn"])
                                dve(lambda e, mm=mm: e.tensor_copy(out=cbs[:, mm, :, 30:34], in_=csn[:, mm]), r=["csn"], w=["cbs"])
            if last:
                S.op("sp", lambda e, l=l: e.dma_start(out=o_cvp[l], in_=chist[:, l]), reads=["chist"], writes=[key("o_cvp")], dma=key("o"))
                S.op("sp", lambda e, l=l: e.dma_start(out=o_cvsn[l], in_=csn), reads=["csn"], writes=[key("o_cvsn")], dma=key("o"))
                S.op("sp", lambda e, l=l: e.dma_start(out=o_cvso[l], in_=ccr[l][:, 4:30, :]), writes=[key("o_cvso")], dma=key("o"))
            for m in range(8):
                d_ = dg[m % 2]
                kd = "dg%d" % (m % 2)
                for k in range(31):
                    S.op("pool", lambda e, d_=d_, m=m, k=k: e.tensor_scalar(out=d_[:, k, :], in0=ident, scalar1=PAR[:, O_CW + m * 31 + k:O_CW + m * 31 + k + 1],
                                                                           scalar2=None, op0=ALU.mult),
                         reads=["ident", "PAR"], writes=[kd])
                for gi, (c0, n) in enumerate(cols):
                    b = (m % 2) * 2 + gi
                    for k in range(31):
                        if gi == 0:
                            pe(lambda e, b=b, k=k, m=m, d_=d_: e.matmul(psb[b][:, 0:NPT], d_[:, k, :], cbp[:, m, k:k + NPT], start=(k == 0), stop=(k == 30)),
                               r=[kd, "cbp"], w=[kps[b]])
                        else:
                            pe(lambda e, b=b, k=k, m=m, d_=d_: e.matmul(psb[b][:, 0:NS].rearrange("p (s t) -> p s t", t=4), d_[:, k, :], cbs[:, m, :, k:k + 4],
                                                                        start=(k == 0), stop=(k == 30)),
                               r=[kd, "cbs"], w=[kps[b]])
                    act(lambda e, b=b, m=m, c0=c0, n=n: e.activation(out=convy[:, m, c0:c0 + n], in_=psb[b][:, 0:n], func=AF.Identity,
                                                                    bias=PAR[:, O_CB + m:O_CB + m + 1]),
                        r=[kps[b], "PAR"], w=["convy"])
            layernorm(convy, "convy", cols, O_CLG, O_CLB, 8, None, None, mixcat[:, 8:16, :], "mixcat", func=AF.Silu)

            def proj_residual(wsrc, nk, rhs, krhs):
                for ci in range(4):
                    wt, kw = load_w(wsrc[:, ci * 512:(ci + 1) * 512], nk, 512)
                    for mi in range(4):
                        m = ci * 4 + mi
                        banks = matmul_tiles(wt, kw, mi, nk, rhs, krhs, cols, m)
                        for gi, (c0, n) in enumerate(cols):
                            b = banks[gi]
                            dve(lambda e, b=b, m=m, c0=c0, n=n: e.scalar_tensor_tensor(out=X[:, m, c0:c0 + n], in0=X[:, m, c0:c0 + n], scalar=ALPHA,
                                                                                      in1=psb[b][:, 0:n], op0=ALU.mult, op1=ALU.add),
                                r=[kX, kps[b]], w=[kX])
            if debug and last and l == 0:
                S.op("pool", lambda e: e.dma_start(out=dbg_mix, in_=mixcat), reads=["mixcat"], writes=[key("o_dbg")], dma=key("o"))
            proj_residual(w_out[l], KT, mixcat, "mixcat")
            layernorm(X, kX, cols, O_LN1G, O_LN1B, KT, X, kX, xb, kxb)
            if debug and last and l == 0:
                S.op("sp", lambda e: e.dma_start(out=dbg_x1, in_=X), reads=[kX], writes=[key("o_dbg")], dma=key("o"))
            barrier()

            A.rewind(base0_mark)
            actb = A.bf16([FT, NTMAX])
            extp = [A.f32([2 + NPT]) for _ in range(2)]
            exts = [A.f32([NSEQ, 6]) for _ in range(2)]
            hy = [A.f32([NTMAX]) for _ in range(2)]
            cfs = [A.f32([NSEQ, 2]) for _ in range(2)]
            ffs_out = A.f32([88, NSEQ, 2])
            pb = A.bf16([2, NTMAX])
            wpe = A.bf16([2, D])
            wv = w_up[l].rearrange("(k p) c -> p k c", p=128)
            for j in range(FT):
                s = ring_i[0] % 2
                ring_i[0] += 1
                wt = wring[s][:, :, 0:256]
                kwa, kwb = "wr%da" % s, "wr%db" % s
                S.op("pool", lambda e, wt=wt, j=j: e.dma_start(out=wt[:, :, 0:128], in_=wv[:, :, j * 128:(j + 1) * 128]), writes=[kwa], dma=kwa)
                S.op("pool", lambda e, wt=wt, j=j: e.dma_start(out=wt[:, :, 128:256], in_=wv[:, :, DFF + j * 128:DFF + (j + 1) * 128]), writes=[kwb], dma=kwb)
                for h in range(2):
                    ft = h * FT + j
                    kwh = [kwa, kwb][h]
                    for gi, (c0, n) in enumerate(cols):
                        b = h * 2 + gi
                        for k in range(KT):
                            pe(lambda e, b=b, k=k, h=h, c0=c0, n=n, wt=wt: e.matmul(psb[b][:, 0:n], wt[:, k, h * 128:(h + 1) * 128], xb[:, k, c0:c0 + n],
                                                                                     start=(k == 0), stop=(k == KT - 1)),
                               r=[kwh, kxb], w=[kps[b]])
                    wf = lambda k, ft=ft: PAR[:, O_FCW + ft * 3 + k:O_FCW + ft * 3 + k + 1]
                    bf = PAR[:, O_FCB + ft:O_FCB + ft + 1]
                    ke = "extp%d" % h
                    kh = "hy%d" % h
                    dve(lambda e, h=h, ft=ft, l=l: e.tensor_copy(out=extp[h][:, 0:2], in_=fhist[:, l, ft, :]), r=["fhist"], w=[ke])
                    act(lambda e, h=h: e.activation(out=extp[h][:, 2:2 + NPT], in_=psb[h * 2][:, 0:NPT], func=AF.Copy), r=[kps[h * 2]], w=[ke])
                    dve(lambda e, h=h, ft=ft, l=l: e.tensor_copy(out=fhist[:, l, ft, :], in_=extp[h][:, NPT:NPT + 2]), r=[ke], w=["fhist"])
                    dve(lambda e, h=h, wf=wf, bf=bf: e.tensor_scalar(out=hy[h][:, 0:NPT], in0=extp[h][:, 2:2 + NPT], scalar1=wf(2), scalar2=bf, op0=ALU.mult, op1=ALU.add),
                        r=[ke, "PAR"], w=[kh])
                    for k in range(2):
                        dve(lambda e, h=h, k=k, wf=wf: e.scalar_tensor_tensor(out=hy[h][:, 0:NPT], in0=extp[h][:, k:k + NPT], scalar=wf(k), in1=hy[h][:, 0:NPT],
                                                                             op0=ALU.mult, op1=ALU.add),
                            r=[ke, "PAR", kh], w=[kh])
                    if NS:
                        kes = "exts%d" % h
                        kcf = "cfs%d" % h
                        hys = hy[h][:, NPT:NPT + NS].rearrange("p (s t) -> p s t", t=4)
                        S.op("sp", lambda e, h=h, ft=ft, l=l: e.dma_start(out=cfs[h], in_=cff[l][:, ft]), writes=[kcf], dma=kcf)
                        dve(lambda e, h=h: e.tensor_copy(out=exts[h][:, :, 0:2], in_=cfs[h]), r=[kcf], w=[kes])
                        act(lambda e, h=h: e.activation(out=exts[h][:, :, 2:6], in_=psb[h * 2 + 1][:, 0:NS].rearrange("p (s t) -> p s t", t=4), func=AF.Copy),
                            r=[kps[h * 2 + 1]], w=[kes])
                        dve(lambda e, h=h, ft=ft: e.tensor_copy(out=ffs_out[:, ft], in_=exts[h][:, :, 4:6]), r=[kes], w=["ffs_out"])
                        dve(lambda e, h=h, wf=wf, bf=bf, hys=hys: e.tensor_scalar(out=hys, in0=exts[h][:, :, 2:6], scalar1=wf(2), scalar2=bf, op0=ALU.mult, op1=ALU.add),
                            r=[kes, "PAR"], w=[kh])
                        for k in range(2):
                            dve(lambda e, h=h, k=k, wf=wf, hys=hys: e.scalar_tensor_tensor(out=hys, in0=exts[h][:, :, k:k + 4], scalar=wf(k), in1=hys,
                                                                                          op0=ALU.mult, op1=ALU.add),
                                r=[kes, "PAR", kh], w=[kh])
                act(lambda e: e.activation(out=hy[0][:, 0:NT], in_=hy[0][:, 0:NT], func=AF.Silu), r=["hy0"], w=["hy0"])
                dve(lambda e, j=j: e.tensor_tensor(out=actb[:, j, 0:NT], in0=hy[0][:, 0:NT], in1=hy[1][:, 0:NT], op=ALU.mult), r=["hy0", "hy1"], w=["actb"])
            if last:
                S.op("sp", lambda e, l=l: e.dma_start(out=o_ffp[l], in_=fhist[:, l]), reads=["fhist"], writes=[key("o_ffp")], dma=key("o"))
                S.op("sp", lambda e, l=l: e.dma_start(out=o_ffs[l], in_=ffs_out), reads=["ffs_out"], writes=[key("o_ffs")], dma=key("o"))
            wdv = w_down[l].rearrange("(k p) c -> p k c", p=128)
            for m in range(KT):
                s_ = ring_i[0] % 2
                ring_i[0] += 1
                wd = wring[s_].rearrange("p k c -> p (k c)")[:, 0:FT * 128].rearrange("p (k c) -> p k c", k=FT)
                kw = ["wr%da" % s_, "wr%db" % s_]
                S.op("pool", lambda e, wd=wd, m=m: e.dma_start(out=wd, in_=wdv[:, :, m * 128:(m + 1) * 128]), writes=kw, dma=kw[0])
                for gi, (c0, n) in enumerate(cols):
                    b = (m % 2) * 2 + gi
                    for k in range(FT):
                        pe(lambda e, b=b, k=k, wd=wd, c0=c0, n=n: e.matmul(psb[b][:, 0:n], wd[:, k, :], actb[:, k, c0:c0 + n], start=(k == 0), stop=(k == FT - 1)),
                           r=kw + ["actb"], w=[kps[b]])
                    dve(lambda e, b=b, m=m, c0=c0, n=n: e.scalar_tensor_tensor(out=X[:, m, c0:c0 + n], in0=X[:, m, c0:c0 + n], scalar=ALPHA,
                                                                              in1=psb[b][:, 0:n], op0=ALU.mult, op1=ALU.add),
                        r=[kX, kps[b]], w=[kX])
            layernorm(X, kX, cols, O_LN2G, O_LN2B, KT, X, kX, xb, kxb)

            S.op("pool", lambda e, l=l, p=p: e.dma_start(out=pb, in_=pin[l, p]), writes=["pb"], dma="pb")
            S.op("pool", lambda e, l=l: e.dma_start(out=wpe, in_=w_pe[l].rearrange("(k p) c -> p k c", p=128)), writes=["wpe"], dma="wpe")
            for ci in range(4):
                wt, kw = load_w(w_gate[l][:, ci * 512:(ci + 1) * 512], KT, 512)
                for mi in range(4):
                    m = ci * 4 + mi
                    banks = matmul_tiles(wt, kw, mi, KT, xb, kxb, cols, m)
                    for gi, (c0, n) in enumerate(cols):
                        b = banks[gi]
                        be = 4 + gi
                        for k in range(2):
                            pe(lambda e, be=be, k=k, m=m, c0=c0, n=n: e.matmul(psb[be][:, 0:n], wpe[:, k, m * 128:(m + 1) * 128], pb[:, k, c0:c0 + n],
                                                                                 start=(k == 0), stop=(k == 1)),
                               r=["wpe", "pb"], w=[kps[be]])
                        act(lambda e, b=b, c0=c0, n=n: e.activation(out=sig[:, c0:c0 + n], in_=psb[b][:, 0:n], func=AF.Sigmoid), r=[kps[b]], w=["sig"])
                        dve(lambda e, be=be, c0=c0, n=n: e.tensor_tensor(out=sig[:, c0:c0 + n], in0=sig[:, c0:c0 + n], in1=psb[be][:, 0:n], op=ALU.mult),
                            r=["sig", kps[be]], w=["sig"])
                        dve(lambda e, m=m, c0=c0, n=n: e.scalar_tensor_tensor(out=X[:, m, c0:c0 + n], in0=X[:, m, c0:c0 + n], scalar=ALPHA,
                                                                             in1=sig[:, c0:c0 + n], op0=ALU.mult, op1=ALU.add),
                            r=[kX, "sig"], w=[kX])
            layernorm(X, kX, cols, O_LN3G, O_LN3B, KT, X, kX, xb, kxb)
            barrier()
        S.op("sp", lambda e, p=p: e.dma_start(out=o_y[p], in_=X), reads=[kX], writes=[key("o_y")], dma="o_y")
    S.barrier(lambda e: e.memset(dummy, 0.0))
    S.emit(nc, st)
    st.close()
    return nc, S, A


def make_core_inputs(inp, b, s0, depth=DEPTH, npass=NPASS):
    f = np.float32
    L = depth
    m = {}
    xin = np.zeros((npass, 128, KT, NTMAX), f)
    pin = np.zeros((L, npass, 128, 2, NTMAX), f)
    for p in range(npass):
        xs = inp["x_prompt"][b, p * NPT:(p + 1) * NPT]
        xin[p, :, :, :NPT] = xs.reshape(NPT, KT, 128).transpose(2, 1, 0)
        ps = inp["p_prompt"][:L, b, p * NPT:(p + 1) * NPT]
        pin[:, p, :, :, :NPT] = ps.reshape(L, NPT, 2, 128).transpose(0, 3, 2, 1)
    xs = inp["x_sample"][s0:s0 + NSEQ].reshape(NSMP, D)
    xin[npass - 1, :, :, NPT:] = xs.reshape(NSMP, KT, 128).transpose(2, 1, 0)
    ps = inp["p_sample"][:L, s0:s0 + NSEQ].reshape(L, NSMP, 256)
    pin[:, npass - 1, :, :, NPT:] = ps.reshape(L, NSMP, 2, 128).transpose(0, 3, 2, 1)
    m["xin"], m["pin"] = xin, pin

    par = np.zeros((L, 128, NPAR), f)

    def put(off, v, ntile):
        par[:, :, off:off + ntile] = v.reshape(L, ntile, 128).transpose(0, 2, 1)
    put(O_LN1G, inp["ln1_g"][:L], 16); put(O_LN1B, inp["ln1_b"][:L], 16)
    put(O_LN2G, inp["ln2_g"][:L], 16); put(O_LN2B, inp["ln2_b"][:L], 16)
    put(O_LN3G, inp["ln3_g"][:L], 16); put(O_LN3B, inp["ln3_b"][:L], 16)
    put(O_CB, inp["conv_b"][:L], 8); put(O_CLG, inp["conv_ln_g"][:L], 8); put(O_CLB, inp["conv_ln_b"][:L], 8)
    put(O_D, inp["s5_d"][:L].reshape(L, 1024), 8)
    par[:, :, O_CW:O_CW + 248] = inp["conv_w"][:L].reshape(L, 31, 8, 128).transpose(0, 3, 2, 1).reshape(L, 128, 248)
    par[:, :, O_FCW:O_FCW + 264] = inp["ffn_conv_w"][:L].reshape(L, 3, 88, 128).transpose(0, 3, 2, 1).reshape(L, 128, 264)
    put(O_FCB, inp["ffn_conv_b"][:L], 88)
    m["par"] = par

    def lay_p(v):
        return v.reshape(L, 32, 2, 64).transpose(0, 2, 3, 1).reshape(L, 128, 32)
    ldt = np.broadcast_to(inp["s5_log_dt"][:L, :, None], (L, 64, 64))
    m["lamp"] = np.ascontiguousarray(np.stack([lay_p(inp["s5_lam_re"][:L]), lay_p(inp["s5_lam_im"][:L]), lay_p(ldt)], axis=2))
    ctp = np.zeros((L, 128, 2, 32, 32), f)
    for ri, nm in enumerate(("s5_c_re", "s5_c_im")):
        c = inp[nm][:L].reshape(L, 32, 2, 16, 64)
        for g2 in range(2):
            ctp[:, g2 * 64:(g2 + 1) * 64, ri, :, g2 * 16:(g2 + 1) * 16] = c[:, :, g2].transpose(0, 3, 1, 2)
    m["ctp"] = ctp
    def lay_f(v):
        w = v.reshape(L, 8, 4, 2, 64)
        w = w.transpose(0, 2, 1, 3, 4).reshape(L, 4, 1, 8, 128)
        return np.broadcast_to(w, (L, 4, 32, 8, 128)).reshape(L, 128, 8, 128)
    m["lamf"] = np.ascontiguousarray(np.stack([lay_f(inp["s5_lam_re"][:L]), lay_f(inp["s5_lam_im"][:L]), lay_f(ldt)], axis=2))
    btf = np.zeros((L, 4, 2, 16, 2, 8, 2, 64), f)
    for ri, nm in enumerate(("s5_b_re", "s5_b_im")):
        bb = inp[nm][:L].reshape(L, 8, 4, 2, 64, 16)
        for g2 in range(2):
            btf[:, :, g2, :, ri, :, g2, :] = bb[:, :, :, g2].transpose(0, 2, 4, 1, 3)
    m["btf"] = btf.reshape(L, 128, 2, 8, 128)
    sin_ = np.zeros((L, 128, 32, 2, NSEQ), f)
    for ri, nm in enumerate(("state_s5_re", "state_s5_im")):
        sv = inp[nm][:L, s0:s0 + NSEQ].reshape(L, NSEQ, 32, 2, 64)
        sin_[:, :, :, ri, :] = sv.transpose(0, 3, 4, 2, 1).reshape(L, 128, 32, NSEQ)
    m["sin"] = sin_
    cc = inp["cache_conv"][:L, s0:s0 + NSEQ]
    m["ccr"] = np.ascontiguousarray(cc)
    m["ccf"] = np.ascontiguousarray(cc.reshape(L, NSEQ, 30, 8, 128).transpose(0, 4, 3, 1, 2))
    cf = inp["cache_ffn_conv"][:L, s0:s0 + NSEQ]
    m["cff"] = np.ascontiguousarray(cf.reshape(L, NSEQ, 2, 88, 128).transpose(0, 4, 3, 1, 2))
    m["w_in"] = inp["w_in"][:L]; m["w_glu"] = inp["s5_w_glu"][:L]; m["w_out"] = inp["w_out"][:L]
    m["w_up"] = inp["ffn_w_up"][:L]; m["w_down"] = inp["ffn_w_down"][:L]
    m["w_pe"] = inp["pe_w"][:L]; m["w_gate"] = inp["pe_w_gate"][:L]
    return m


def unpack_core(res, depth=DEPTH, npass=NPASS):
    L = depth
    o = {}
    y = res["o_y"]
    yp = y[:, :, :, :NPT].transpose(0, 3, 2, 1).reshape(npass * NPT, D)
    ys = y[npass - 1, :, :, NPT:].transpose(2, 1, 0).reshape(NSEQ, 4, D)
    o["y_prompt"], o["y_sample"] = yp, ys
    sp = res["o_s5p"].reshape(L, 2, 64, 32, 2)
    sp = sp.transpose(0, 4, 3, 1, 2).reshape(L, 2, 64, 64)
    o["s5_re_prompt"], o["s5_im_prompt"] = sp[:, 0], sp[:, 1]
    ss = res["o_s5s"].reshape(L, 2, 64, 32, 2, NSEQ)
    ss = ss.transpose(0, 4, 5, 3, 1, 2).reshape(L, 2, NSEQ, 64, 64)
    o["s5_re_sample"], o["s5_im_sample"] = ss[:, 0], ss[:, 1]
    o["conv_prompt"] = res["o_cvp"].transpose(0, 3, 2, 1).reshape(L, 30, 1024)
    new = res["o_cvsn"].transpose(0, 3, 4, 2, 1).reshape(L, NSEQ, 4, 1024)
    o["conv_sample"] = np.concatenate([res["o_cvso"], new], axis=2)
    o["ffn_conv_prompt"] = res["o_ffp"].transpose(0, 3, 2, 1).reshape(L, 2, 11264)
    o["ffn_conv_sample"] = res["o_ffs"].transpose(0, 3, 4, 2, 1).reshape(L, NSEQ, 2, 11264)
    return o


_PROG = {}


def kernel(**inputs):
    inp = {k: np.asarray(v) for k, v in inputs.items()}
    if "nc" not in _PROG:
        _PROG["nc"] = build_program()[0]
    nc = _PROG["nc"]
    n = 8
    in_maps = [make_core_inputs(inp, c % 4, 16 * c) for c in range(n)]
    res = run_bass_kernel_spmd(nc, in_maps, core_ids=list(range(n)))
    outs = [unpack_core(r) for r in res.results]
    f = np.float32
    y_prompt = np.stack([outs[b]["y_prompt"] for b in range(4)]).astype(f)
    y_sample = np.concatenate([outs[c]["y_sample"] for c in range(n)]).astype(f)

    def pstack(nm):
        return np.stack([outs[b][nm] for b in range(4)], axis=1).astype(f)

    def sstack(nm):
        return np.concatenate([outs[c][nm] for c in range(n)], axis=1).astype(f)

    return (y_prompt, y_sample,
            pstack("s5_re_prompt"), pstack("s5_im_prompt"), pstack("conv_prompt"), pstack("ffn_conv_prompt"),
            sstack("s5_re_sample"), sstack("s5_im_sample"), sstack("conv_sample"), sstack("ffn_conv_sample"))
```

```python
import contextlib
import math
import types
import numpy as np
import concourse.bass as bass
import concourse.mybir as mybir
from concourse.bass_utils import run_bass_kernel_spmd

F32 = mybir.dt.float32
BF16 = mybir.dt.bfloat16
I32 = mybir.dt.int32
AF = mybir.ActivationFunctionType
ALU = mybir.AluOpType

DEPTH = 4
D = 2048
KT = 16
NPASS = 4
NPT = 512
NSEQ = 16
NSMP = 64
NTMAX = NPT + NSMP
NCHP = 128
DFF = 5632
FT = 44
ALPHA = (2.0 * DEPTH) ** 0.25
EPS = 1e-5
O_LN1G, O_LN1B, O_LN2G, O_LN2B, O_LN3G, O_LN3B = 0, 16, 32, 48, 64, 80
O_CB, O_CLG, O_CLB, O_D, O_CW, O_FCW, O_FCB = 96, 104, 112, 120, 128, 376, 640
NPAR = 728
ENGS = ("pe", "act", "dve", "pool", "sp")
PI = math.pi


class Op:
    __slots__ = ("eng", "fn", "deps", "dma", "signal", "ev", "pos", "epoch")

    def __init__(self, eng, fn, deps, dma):
        self.eng, self.fn, self.deps, self.dma = eng, fn, deps, dma
        self.signal, self.ev, self.pos, self.epoch = False, None, 0, 0


def _freeze(fn):
    if fn.__closure__ is None:
        return fn
    cells = []
    for c in fn.__closure__:
        try:
            cells.append(types.CellType(c.cell_contents))
        except ValueError:
            cells.append(c)
    g = types.FunctionType(fn.__code__, fn.__globals__, fn.__name__, fn.__defaults__, tuple(cells))
    g.__kwdefaults__ = fn.__kwdefaults__
    return g


class Sched:
    def __init__(self):
        self.ops = []
        self.last_writer = {}
        self.readers = {}
        self.queues = {e: [] for e in ENGS}
        self.bar = None
        self.dmas_since_bar = []
        self.epoch = 0

    def op(self, eng, fn, reads=(), writes=(), dma=None):
        deps = set()
        if self.bar is not None:
            deps.add(self.bar)
        for k in reads:
            w = self.last_writer.get(k)
            if w is not None:
                deps.add(w)
        for k in writes:
            w = self.last_writer.get(k)
            if w is not None:
                deps.add(w)
            deps.update(self.readers.get(k, ()))
        idx = len(self.ops)
        o = Op(eng, _freeze(fn), deps, dma)
        o.pos = len(self.queues[eng])
        o.epoch = self.epoch
        self.ops.append(o)
        self.queues[eng].append(idx)
        if dma is not None:
            self.dmas_since_bar.append(idx)
        for k in reads:
            self.readers.setdefault(k, []).append(idx)
        for k in writes:
            self.last_writer[k] = idx
            self.readers[k] = []
        return idx

    def barrier(self, fn):
        deps = set(self.dmas_since_bar)
        for e in ENGS:
            if self.queues[e]:
                deps.add(self.queues[e][-1])
        if self.bar is not None:
            deps.add(self.bar)
        idx = len(self.ops)
        o = Op("dve", fn, deps, None)
        o.pos = len(self.queues["dve"])
        o.epoch = self.epoch
        self.ops.append(o)
        self.queues["dve"].append(idx)
        self.bar = idx
        self.dmas_since_bar = []
        self.last_writer = {}
        self.readers = {}
        self.epoch += 1

    def emit(self, nc, stack):
        ops = self.ops
        need = [[] for _ in ops]
        for i, o in enumerate(ops):
            for d in o.deps:
                p = ops[d]
                if p.dma is None and p.eng == o.eng:
                    if o.eng in ("pe", "sp"):
                        continue
                    if o.pos - p.pos > 3:
                        continue
                if p.dma is None:
                    p.signal = True
                need[i].append(d)
        cnt = {}
        dcnt = {}
        for o in ops:
            if o.dma is not None:
                dcnt[o.dma] = dcnt.get(o.dma, 0) + 16
                o.ev = (("dma", o.dma), dcnt[o.dma])
            elif o.signal:
                k = ("eng", o.eng, o.epoch // 12)
                cnt[k] = cnt.get(k, 0) + 1
                o.ev = (k, cnt[k])
        sems = {}
        for k in list(cnt.keys()) + [("dma", c) for c in dcnt]:
            sems[k] = stack.enter_context(nc.semaphore("s%d" % len(sems)))
        self.nsems = len(sems)
        block = stack.enter_context(nc.Block())

        def run(engname):
            def body(eng):
                waited = {}
                for idx in self.queues[engname]:
                    o = ops[idx]
                    w = {}
                    for d in need[idx]:
                        sk, v = ops[d].ev
                        if waited.get(sk, 0) >= v:
                            continue
                        if w.get(sk, 0) < v:
                            w[sk] = v
                    for sk, v in w.items():
                        eng.wait_ge(sems[sk], v)
                        waited[sk] = v
                    ins = o.fn(eng)
                    if ins is None:
                        continue
                    if o.dma is not None:
                        ins.then_inc(sems[o.ev[0]], 16)
                    elif o.signal:
                        ins.then_inc(sems[o.ev[0]], 1)
            return body

        block.tensor(run("pe"))
        block.scalar(run("act"))
        block.vector(run("dve"))
        block.gpsimd(run("pool"))
        block.sync(run("sp"))


class Arena:
    def __init__(self, t, nwords):
        self.t, self.n, self.off, self.peak = t, nwords, 0, 0

    def mark(self):
        return self.off

    def rewind(self, m):
        self.off = m

    def _take(self, nw):
        assert self.off + nw <= self.n, ("arena overflow", self.off, nw, self.n)
        ap = self.t[:, self.off:self.off + nw]
        self.off += nw
        self.peak = max(self.peak, self.off)
        return ap

    def f32(self, shape, dt=None):
        n = int(np.prod(shape))
        ap = self._take(n)
        if dt is not None:
            ap = ap.bitcast(dt)
        return self._shape(ap, shape)

    def bf16(self, shape):
        n = int(np.prod(shape))
        ap = self._take((n + 1) // 2).bitcast(BF16)[:, 0:n]
        return self._shape(ap, shape)

    @staticmethod
    def _shape(ap, shape):
        if len(shape) == 1:
            return ap
        names = " ".join("d%d" % i for i in range(len(shape)))
        kw = {"d%d" % i: s for i, s in enumerate(shape)}
        return ap.rearrange("p (%s) -> p %s" % (names, names), **kw)


def build_program(depth=DEPTH, npass=NPASS, debug=False):
    nc = bass.Bass("TRN2", target_bir_lowering=False)
    S = Sched()

    def din(name, shape):
        return nc.dram_tensor(name, list(shape), F32, kind="ExternalInput").ap()

    def dout(name, shape):
        return nc.dram_tensor(name, list(shape), F32, kind="ExternalOutput").ap()

    xin = din("xin", [npass, 128, KT, NTMAX])
    pin = din("pin", [depth, npass, 128, 2, NTMAX])
    par = din("par", [depth, 128, NPAR])
    lamp = din("lamp", [depth, 128, 3, 32])
    ctp = din("ctp", [depth, 128, 2, 32, 32])
    lamf = din("lamf", [depth, 128, 3, 8, 128])
    btf = din("btf", [depth, 128, 2, 8, 128])
    sin_ = din("sin", [depth, 128, 32, 2, NSEQ])
    ccf = din("ccf", [depth, 128, 8, NSEQ, 30])
    ccr = din("ccr", [depth, NSEQ, 30, 1024])
    cff = din("cff", [depth, 128, 88, NSEQ, 2])
    w_in = din("w_in", [depth, D, 3072])
    w_glu = din("w_glu", [depth, 1024, 1024])
    w_out = din("w_out", [depth, D, D])
    w_up = din("w_up", [depth, D, 2 * DFF])
    w_down = din("w_down", [depth, DFF, D])
    w_pe = din("w_pe", [depth, 256, D])
    w_gate = din("w_gate", [depth, D, D])

    o_y = dout("o_y", [npass, 128, KT, NTMAX])
    o_s5p = dout("o_s5p", [depth, 128, 32, 2])
    o_s5s = dout("o_s5s", [depth, 128, 32, 2, NSEQ])
    o_cvp = dout("o_cvp", [depth, 128, 8, 30])
    o_cvsn = dout("o_cvsn", [depth, 128, 8, NSEQ, 4])
    o_cvso = dout("o_cvso", [depth, NSEQ, 26, 1024])
    o_ffp = dout("o_ffp", [depth, 128, 88, 2])
    o_ffs = dout("o_ffs", [depth, 128, 88, NSEQ, 2])
    if debug:
        dbg_g5 = dout("dbg_g5", [128, 8, NTMAX])
        dbg_u = dout("dbg_u", [128, 8, NTMAX])
        dbg_mix = dout("dbg_mix", [128, KT, NTMAX])
        dbg_x1 = dout("dbg_x1", [128, KT, NTMAX])

    xw_scr = nc.dram_tensor("xw_scr", [depth, 2, 128, 4096], BF16, kind="Internal").ap()
    st = contextlib.ExitStack()
    NW = 52000
    arena_t = st.enter_context(nc.sbuf_tensor("arena", [128, NW], F32))
    A = Arena(arena_t, NW)
    psb = [st.enter_context(nc.psum_tensor("ps%d" % i, [128, 512], F32)) for i in range(8)]
    kps = ["ps%d" % i for i in range(8)]

    uid = [0]

    def key(prefix):
        if prefix == "o":
            uid[0] += 1
            return "oshared%d" % (uid[0] % 4)
        uid[0] += 1
        return "%s#%d" % (prefix, uid[0])

    def dve(fn, r=(), w=()):
        S.op("dve", fn, r, w)

    def act(fn, r=(), w=()):
        S.op("act", fn, r, w)

    def pe(fn, r=(), w=()):
        S.op("pe", fn, r, w)

    X = A.f32([KT, NTMAX]);   kX = "X"
    xb = A.bf16([KT, NTMAX]); kxb = "xb"
    ones = A.f32([128])
    scar = A.f32([depth, 32, 2])
    chist = A.f32([depth, 8, 30])
    fhist = A.f32([depth, 88, 2])
    dummy = A.f32([2])
    consts = A.f32([4])
    PAR = A.f32([NPAR])
    wring = [A.bf16([KT, 512]) for _ in range(2)]
    sig = A.f32([NTMAX])
    lnt = tuple(A.f32([512]) for _ in range(4))
    ident = A.f32([128])
    base0_mark = A.mark()
    mixcat = A.bf16([KT, NTMAX])
    base_mark = A.mark()

    dve(lambda e: e.memset(ones, 1.0), w=["ones"])
    dve(lambda e: e.memset(consts[:, 0:1], EPS), w=["consts"])
    dve(lambda e: e.memset(consts[:, 1:2], PI / 2), w=["consts"])
    dve(lambda e: e.memset(scar, 0.0), w=["scar"])
    dve(lambda e: e.memset(chist, 0.0), w=["chist"])
    dve(lambda e: e.memset(fhist, 0.0), w=["fhist"])
    dve(lambda e: e.memset(dummy, 0.0), w=["dummy"])
    epsb = consts[:, 0:1]
    halfpi = consts[:, 1:2]
    ident_i = lnt[0][:, 0:128].bitcast(I32)
    S.op("pool", lambda e: e.iota(out=ident_i, pattern=[[1, 128]], base=0, channel_multiplier=-1), writes=["ident_i"])
    dve(lambda e: e.tensor_scalar(out=ident, in0=ident_i, scalar1=0.0, scalar2=None, op0=ALU.is_equal), r=["ident_i"], w=["ident"])
    S.barrier(lambda e: e.memset(dummy, 0.0))

    ring_i = [0]

    def load_w(src_ap, nk, ncols):
        s = ring_i[0] % 2
        ring_i[0] += 1
        dst = wring[s].rearrange("p k c -> p (k c)")[:, 0:nk * ncols].rearrange("p (k c) -> p k c", k=nk)
        srcv = src_ap.rearrange("(k p) c -> p k c", p=128)
        S.op("pool", lambda e: e.dma_start(out=dst, in_=srcv), writes=["wr%da" % s, "wr%db" % s], dma="wr%da" % s)
        return dst, ["wr%da" % s, "wr%db" % s]

    def barrier():
        S.barrier(lambda e: e.memset(dummy, 0.0))

    def layernorm(src, ksrc, cols_list, goff, boff, ntile, dst_f32, kdst, dst_bf, kdstb, func=None):
        mean, rstd, sq, t1 = lnt
        inv = 1.0 / (ntile * 128)
        for (c0, n) in cols_list:
            ps_s, ps_q = psb[6], psb[7]
            for m in range(ntile):
                pe(lambda e, m=m: e.matmul(ps_s[:, 0:n], ones, src[:, m, c0:c0 + n], start=(m == 0), stop=(m == ntile - 1)),
                   r=[ksrc, "ones"], w=["ps6"])
            for m in range(ntile):
                act(lambda e, m=m: e.activation(out=sq[:, 0:n], in_=src[:, m, c0:c0 + n], func=AF.Square), r=[ksrc], w=["ln_sq"])
                pe(lambda e, m=m: e.matmul(ps_q[:, 0:n], ones, sq[:, 0:n], start=(m == 0), stop=(m == ntile - 1)),
                   r=["ln_sq", "ones"], w=["ps7"])
            act(lambda e: e.activation(out=mean[:, 0:n], in_=ps_s[:, 0:n], func=AF.Identity, scale=inv), r=["ps6"], w=["ln_mean"])
            dve(lambda e: e.tensor_tensor(out=t1[:, 0:n], in0=mean[:, 0:n], in1=mean[:, 0:n], op=ALU.mult), r=["ln_mean"], w=["ln_t1"])
            dve(lambda e: e.scalar_tensor_tensor(out=rstd[:, 0:n], in0=ps_q[:, 0:n], scalar=inv, in1=t1[:, 0:n], op0=ALU.mult, op1=ALU.subtract),
                r=["ps7", "ln_t1"], w=["ln_rstd"])
            act(lambda e: e.activation(out=rstd[:, 0:n], in_=rstd[:, 0:n], func=AF.Ln, bias=epsb), r=["ln_rstd", "consts"], w=["ln_rstd"])
            act(lambda e: e.activation(out=rstd[:, 0:n], in_=rstd[:, 0:n], func=AF.Exp, scale=-0.5), r=["ln_rstd"], w=["ln_rstd"])
            for m in range(ntile):
                dve(lambda e, m=m: e.tensor_tensor(out=t1[:, 0:n], in0=src[:, m, c0:c0 + n], in1=mean[:, 0:n], op=ALU.subtract),
                    r=[ksrc, "ln_mean"], w=["ln_t1"])
                dve(lambda e, m=m: e.tensor_tensor(out=t1[:, 0:n], in0=t1[:, 0:n], in1=rstd[:, 0:n], op=ALU.mult),
                    r=["ln_t1", "ln_rstd"], w=["ln_t1"])
                if dst_f32 is not None:
                    act(lambda e, m=m: e.activation(out=dst_f32[:, m, c0:c0 + n], in_=t1[:, 0:n], func=AF.Identity,
                                                    scale=PAR[:, goff + m:goff + m + 1], bias=PAR[:, boff + m:boff + m + 1]),
                        r=["ln_t1", "PAR"], w=[kdst])
                    act(lambda e, m=m: e.activation(out=dst_bf[:, m, c0:c0 + n], in_=t1[:, 0:n], func=AF.Identity,
                                                    scale=PAR[:, goff + m:goff + m + 1], bias=PAR[:, boff + m:boff + m + 1]),
                        r=["ln_t1", "PAR"], w=[kdstb])
                else:
                    act(lambda e, m=m: e.activation(out=dst_bf[:, m, c0:c0 + n], in_=t1[:, 0:n], func=func,
                                                    scale=PAR[:, goff + m:goff + m + 1], bias=PAR[:, boff + m:boff + m + 1]),
                        r=["ln_t1", "PAR"], w=[kdstb])

    def matmul_tiles(wt, kw, mi, nk, rhs, krhs, cols, m):
        banks = []
        for gi, (c0, n) in enumerate(cols):
            b = (m % 2) * 2 + gi
            banks.append(b)
            for k in range(nk):
                pe(lambda e, b=b, k=k, c0=c0, n=n: e.matmul(psb[b][:, 0:n], wt[:, k, mi * 128:(mi + 1) * 128], rhs[:, k, c0:c0 + n],
                                                            start=(k == 0), stop=(k == nk - 1)),
                   r=kw + [krhs], w=[kps[b]])
        return banks

    def pw_setup(tag, lr, li, ldt, F):
        kk = lambda s_: "%s_%s" % (tag, s_)
        c = dict(tag=tag, lr=lr, li=li, F=F, kk=kk)
        c["dt"] = A.f32([F]); c["th"] = A.f32([F]); c["ld"] = A.f32([F])
        c["t0"] = A.f32([F]); c["t1"] = A.f32([F]); c["t2"] = A.f32([F]); c["ti"] = A.f32([F], dt=I32)
        dt, th, ld = c["dt"], c["th"], c["ld"]
        act(lambda e: e.activation(out=dt, in_=ldt, func=AF.Exp), r=[kk("in")], w=[kk("dt")])
        dve(lambda e: e.tensor_tensor(out=th, in0=li, in1=dt, op=ALU.mult), r=[kk("in"), kk("dt")], w=[kk("th")])
        dve(lambda e: e.tensor_tensor(out=ld, in0=lr, in1=dt, op=ALU.mult), r=[kk("in"), kk("dt")], w=[kk("ld")])
        return c

    def pw_power(c, n, pr, pi_, kp):
        kk = c["kk"]
        th, ld, t0, t1, t2, ti = c["th"], c["ld"], c["t0"], c["t1"], c["t2"], c["ti"]
        dve(lambda e: e.tensor_scalar(out=t0, in0=th, scalar1=float(n), scalar2=None, op0=ALU.mult), r=[kk("th")], w=[kk("t0")])
        dve(lambda e: e.tensor_scalar(out=t1, in0=t0, scalar1=1.0 / (2 * PI), scalar2=None, op0=ALU.mult), r=[kk("t0")], w=[kk("t1")])
        dve(lambda e: e.tensor_copy(out=ti, in_=t1), r=[kk("t1")], w=[kk("ti")])
        dve(lambda e: e.tensor_copy(out=t1, in_=ti), r=[kk("ti")], w=[kk("t1")])
        dve(lambda e: e.scalar_tensor_tensor(out=t0, in0=t1, scalar=-2 * PI, in1=t0, op0=ALU.mult, op1=ALU.add), r=[kk("t1"), kk("t0")], w=[kk("t0")])
        dve(lambda e: e.tensor_scalar(out=t1, in0=t0, scalar1=PI, scalar2=-2 * PI, op0=ALU.is_gt, op1=ALU.mult), r=[kk("t0")], w=[kk("t1")])
        dve(lambda e: e.tensor_tensor(out=t0, in0=t0, in1=t1, op=ALU.add), r=[kk("t0"), kk("t1")], w=[kk("t0")])
        dve(lambda e: e.tensor_scalar(out=t1, in0=t0, scalar1=-PI, scalar2=2 * PI, op0=ALU.is_lt, op1=ALU.mult), r=[kk("t0")], w=[kk("t1")])
        dve(lambda e: e.tensor_tensor(out=t0, in0=t0, in1=t1, op=ALU.add), r=[kk("t0"), kk("t1")], w=[kk("t0")])
        dve(lambda e: e.tensor_scalar(out=t0, in0=t0, scalar1=-3.1415925, scalar2=3.1415925, op0=ALU.max, op1=ALU.min), r=[kk("t0")], w=[kk("t0")])
        act(lambda e: e.activation(out=t1, in_=t0, func=AF.Sin), r=[kk("t0")], w=[kk("t1")])
        dve(lambda e: e.scalar_tensor_tensor(out=t0, in0=t0, scalar=-1.0, in1=t0, op0=ALU.mult, op1=ALU.max), r=[kk("t0")], w=[kk("t0")])
        act(lambda e: e.activation(out=t0, in_=t0, func=AF.Sin, scale=-1.0, bias=halfpi), r=[kk("t0"), "consts"], w=[kk("t0")])
        act(lambda e: e.activation(out=t2, in_=ld, func=AF.Exp, scale=float(n)), r=[kk("ld")], w=[kk("t2")])
        dve(lambda e: e.tensor_tensor(out=pr, in0=t2, in1=t0, op=ALU.mult), r=[kk("t2"), kk("t0")], w=[kp])
        dve(lambda e: e.tensor_tensor(out=pi_, in0=t2, in1=t1, op=ALU.mult), r=[kk("t2"), kk("t1")], w=[kp])

    def pw_q(c, p1r, p1i, kp1, qr, qi, kq):
        kk = c["kk"]
        lr, li, t0, t1, t2 = c["lr"], c["li"], c["t0"], c["t1"], c["t2"]
        dve(lambda e: e.tensor_tensor(out=t0, in0=lr, in1=lr, op=ALU.mult), r=[kk("in")], w=[kk("t0")])
        dve(lambda e: e.tensor_tensor(out=t1, in0=li, in1=li, op=ALU.mult), r=[kk("in")], w=[kk("t1")])
        dve(lambda e: e.tensor_tensor(out=t0, in0=t0, in1=t1, op=ALU.add), r=[kk("t0"), kk("t1")], w=[kk("t0")])
        dve(lambda e: e.reciprocal(out=t0, in_=t0), r=[kk("t0")], w=[kk("t0")])
        dve(lambda e: e.tensor_scalar(out=t1, in0=p1r, scalar1=-1.0, scalar2=None, op0=ALU.add), r=[kp1], w=[kk("t1")])
        dve(lambda e: e.tensor_tensor(out=qr, in0=t1, in1=lr, op=ALU.mult), r=[kk("t1"), kk("in")], w=[kq])
        dve(lambda e: e.tensor_tensor(out=t2, in0=p1i, in1=li, op=ALU.mult), r=[kp1, kk("in")], w=[kk("t2")])
        dve(lambda e: e.tensor_tensor(out=qr, in0=qr, in1=t2, op=ALU.add), r=[kq, kk("t2")], w=[kq])
        dve(lambda e: e.tensor_tensor(out=qr, in0=qr, in1=t0, op=ALU.mult), r=[kq, kk("t0")], w=[kq])
        dve(lambda e: e.tensor_tensor(out=qi, in0=p1i, in1=lr, op=ALU.mult), r=[kp1, kk("in")], w=[kq])
        dve(lambda e: e.tensor_tensor(out=t2, in0=t1, in1=li, op=ALU.mult), r=[kk("t1"), kk("in")], w=[kk("t2")])
        dve(lambda e: e.tensor_tensor(out=qi, in0=qi, in1=t2, op=ALU.subtract), r=[kq, kk("t2")], w=[kq])
        dve(lambda e: e.tensor_tensor(out=qi, in0=qi, in1=t0, op=ALU.mult), r=[kq, kk("t0")], w=[kq])

    def s5_stage(l, p, NS, cols, u, g5, last):
        NSQ = NS // 4
        NC = NCHP + NSQ
        s5_mark = A.mark()
        LP = A.f32([3, 32])
        S.op("sp", lambda e: e.dma_start(out=LP, in_=lamp[l]), writes=["P_in"], dma="P_in")
        cP = pw_setup("P", LP[:, 0], LP[:, 1], LP[:, 2], 32)
        PP = {}
        npi = {}
        Ta = {}
        Tb = {}
        tpr = A.f32([32]); tpi = A.f32([32])
        for n in (1, 2, 3, 4, 8, 12, 16, 20, 24, 28, 32):
            if n <= 4:
                pr = A.f32([32]); pi_ = A.f32([32]); t = A.f32([32])
                kpn = "P_P%d" % n
            else:
                pr, pi_, t = tpr, tpi, None
                kpn = "P_Pt"
            pw_power(cP, n, pr, pi_, kpn)
            if t is not None:
                dve(lambda e, t=t, pi_=pi_: e.tensor_scalar(out=t, in0=pi_, scalar1=-1.0, scalar2=None, op0=ALU.mult), r=[kpn], w=["P_npi%d" % n])
                PP[n] = (pr, pi_, kpn)
                npi[n] = t
            if n >= 4:
                ta = A.f32([32, 2]); tb = A.f32([32, 2])
                dve(lambda e, ta=ta, pr=pr: e.tensor_copy(out=ta[:, :, 0], in_=pr), r=[kpn], w=["P_T%d" % n])
                dve(lambda e, ta=ta, pr=pr: e.tensor_copy(out=ta[:, :, 1], in_=pr), r=[kpn], w=["P_T%d" % n])
                dve(lambda e, tb=tb, pi_=pi_: e.tensor_scalar(out=tb[:, :, 0], in0=pi_, scalar1=-1.0, scalar2=None, op0=ALU.mult), r=[kpn], w=["P_T%d" % n])
                dve(lambda e, tb=tb, pi_=pi_: e.tensor_copy(out=tb[:, :, 1], in_=pi_), r=[kpn], w=["P_T%d" % n])
                Ta[n], Tb[n] = ta, tb
        CTb = A.bf16([2, 32, 32])
        S.op("pool", lambda e: e.dma_start(out=CTb, in_=ctp[l]), writes=["CTb"], dma="CTb")
        dve(lambda e: e.tensor_scalar(out=CTb[:, 1], in0=CTb[:, 1], scalar1=-1.0, scalar2=None, op0=ALU.mult), r=["CTb"], w=["CTb"])
        half_mark = A.mark()
        for hf in range(2):
            A.rewind(half_mark)
            barrier()
            XW = A.bf16([4, 2, 4, 128])
            fm = A.mark()
            if p == 0:
                LF = A.f32([3, 4, 128])
                BT = A.f32([2, 4, 128])
                S.op("sp", lambda e, hf=hf: e.dma_start(out=LF, in_=lamf[l][:, :, 4 * hf:4 * hf + 4, :]), writes=["F_in"], dma="F_in")
                S.op("sp", lambda e, hf=hf: e.dma_start(out=BT, in_=btf[l][:, :, 4 * hf:4 * hf + 4, :]), writes=["BT"], dma="BT")
                fl = lambda ap: ap.rearrange("p a b -> p (a b)")
                cF = pw_setup("F", fl(LF[:, 0]), fl(LF[:, 1]), fl(LF[:, 2]), 512)
                t0, t1, t2 = cF["t0"], cF["t1"], cF["t2"]
                btr, bti = fl(BT[:, 0]), fl(BT[:, 1])
                pr = A.f32([512]); pi_ = A.f32([512]); qr = A.f32([512]); qi = A.f32([512])
                vr = A.f32([512]); vi = A.f32([512])
                kq, kp, kv = "F_q", "F_P", "F_V"
                for n in range(4):
                    if n == 0:
                        pw_power(cF, 1, pr, pi_, kp)
                        pw_q(cF, pr, pi_, kp, qr, qi, kq)
                        ur, ui, ku = qr, qi, kq
                    else:
                        if n > 1:
                            pw_power(cF, n, pr, pi_, kp)
                        dve(lambda e: e.tensor_tensor(out=vr, in0=pr, in1=qr, op=ALU.mult), r=[kp, kq], w=[kv])
                        dve(lambda e: e.tensor_tensor(out=t0, in0=pi_, in1=qi, op=ALU.mult), r=[kp, kq], w=["F_t0"])
                        dve(lambda e: e.tensor_tensor(out=vr, in0=vr, in1=t0, op=ALU.subtract), r=[kv, "F_t0"], w=[kv])
                        dve(lambda e: e.tensor_tensor(out=vi, in0=pr, in1=qi, op=ALU.mult), r=[kp, kq], w=[kv])
                        dve(lambda e: e.tensor_tensor(out=t0, in0=pi_, in1=qr, op=ALU.mult), r=[kp, kq], w=["F_t0"])
                        dve(lambda e: e.tensor_tensor(out=vi, in0=vi, in1=t0, op=ALU.add), r=[kv, "F_t0"], w=[kv])
                        ur, ui, ku = vr, vi, kv
                    xr = fl(XW[:, n, 0]); xi = fl(XW[:, n, 1])
                    dve(lambda e, ur=ur: e.tensor_tensor(out=t1, in0=ur, in1=btr, op=ALU.mult), r=[ku, "BT"], w=["F_t1"])
                    dve(lambda e, ui=ui: e.tensor_tensor(out=t2, in0=ui, in1=bti, op=ALU.mult), r=[ku, "BT"], w=["F_t2"])
                    dve(lambda e, xr=xr: e.tensor_tensor(out=xr, in0=t1, in1=t2, op=ALU.subtract), r=["F_t1", "F_t2"], w=["XW"])
                    dve(lambda e, ur=ur: e.tensor_tensor(out=t1, in0=ur, in1=bti, op=ALU.mult), r=[ku, "BT"], w=["F_t1"])
                    dve(lambda e, ui=ui: e.tensor_tensor(out=t2, in0=ui, in1=btr, op=ALU.mult), r=[ku, "BT"], w=["F_t2"])
                    dve(lambda e, xi=xi: e.tensor_tensor(out=xi, in0=t1, in1=t2, op=ALU.add), r=["F_t1", "F_t2"], w=["XW"])
                S.op("sp", lambda e, hf=hf: e.dma_start(out=xw_scr[l, hf], in_=XW.rearrange("p a b c d -> p (a b c d)")), reads=["XW"], writes=[key("xwscr")], dma=key("o"))
            else:
                S.op("sp", lambda e, hf=hf: e.dma_start(out=XW.rearrange("p a b c d -> p (a b c d)"), in_=xw_scr[l, hf]), writes=["XW"], dma="XWld")
            A.rewind(fm)
            SP = A.f32([16, 2, NCHP + 1])
            XS = A.f32([16, 2, NSEQ])
            H0 = A.f32([16, 2, NSEQ])
            SO = A.f32([16, 2, NSEQ])
            CB = A.f32([16, 2, 17])
            st1 = A.f32([16, 2, 16]); st2 = A.f32([16, 2, 16])
            Hf = [A.bf16([4, 2, NCHP + NSEQ]) for _ in range(2)]
            tmpc = A.f32([NCHP + NSEQ])
            ysb = A.f32([NTMAX])
            dve(lambda e: e.memset(SP[:, :, :, 0:1], 0.0), r=["XW", "F_t1", "F_t2", "F_t0", "F_in", "BT", "F_q", "F_V", "F_P", "F_th", "F_ld", "F_dt", "F_ti"],
                w=["SP", "XS", "H0", "SO", "CB", "st1", "st2", "Hf0", "Hf1", "tmpc", "ysb"])
            dve(lambda e, hf=hf: e.tensor_copy(out=SP[:, :, :, 0], in_=scar[:, l, 16 * hf:16 * hf + 16, :]), r=["scar"], w=["SP"])
            if NS:
                S.op("sp", lambda e, hf=hf: e.dma_start(out=H0, in_=sin_[l][:, 16 * hf:16 * hf + 16]), writes=["H0"], dma="H0")
            p4r, p4i, kp4 = PP[4]
            sl = slice(16 * hf, 16 * hf + 16)
            for ii in range(16):
                i = 16 * hf + ii
                tl, ip = ii // 4, ii % 4
                t = i // 4
                rows = slice(32 * ip, 32 * ip + 32)
                for ri in range(2):
                    b = (ii * 2 + ri) % 2
                    for s in range(4):
                        pe(lambda e, b=b, s=s, ri=ri, tl=tl, t=t, rows=rows, ip=ip: e.matmul(
                            psb[b][:, 0:NC], XW[rows, 3 - s, ri, tl, :], u[rows, t, s:4 * NC:4],
                            start=(s == 0), stop=(s == 3), tile_position=(32 * ip, 0)),
                           r=["XW", "u"], w=[kps[b]])
                    act(lambda e, b=b, ii=ii, ri=ri: e.activation(out=SP[:, ii, ri, 1:NCHP + 1], in_=psb[b][:, 0:NCHP], func=AF.Copy),
                        r=[kps[b]], w=["SP"])
                    if NS:
                        act(lambda e, b=b, ii=ii, ri=ri: e.activation(out=XS[:, ii, ri, 0:NSQ], in_=psb[b][:, NCHP:NC], func=AF.Copy),
                            r=[kps[b]], w=["XS"])
            SPb = SP[:, :, :, 1:NCHP + 1].rearrange("p a r (b k) -> p a r b k", k=8)
            tab = {n_: Ta[n_][:, sl, :].unsqueeze(3).to_broadcast([128, 16, 2, 16]) for n_ in Ta}
            tbb = {n_: Tb[n_][:, sl, :].unsqueeze(3).to_broadcast([128, 16, 2, 16]) for n_ in Tb}
            for k in range(1, 8):
                dve(lambda e, k=k: e.tensor_tensor(out=st1, in0=SPb[:, :, :, :, k - 1], in1=tab[4], op=ALU.mult), r=["SP", "P_T4"], w=["st1"])
                dve(lambda e, k=k: e.tensor_tensor(out=st2, in0=SPb[:, :, ::-1, :, k - 1], in1=tbb[4], op=ALU.mult), r=["SP", "P_T4"], w=["st2"])
                dve(lambda e, k=k: e.tensor_tensor(out=SPb[:, :, :, :, k], in0=SPb[:, :, :, :, k], in1=st1, op=ALU.add), r=["SP", "st1"], w=["SP"])
                dve(lambda e, k=k: e.tensor_tensor(out=SPb[:, :, :, :, k], in0=SPb[:, :, :, :, k], in1=st2, op=ALU.add), r=["SP", "st2"], w=["SP"])
            dve(lambda e: e.tensor_copy(out=CB[:, :, :, 0], in_=SP[:, :, :, 0]), r=["SP"], w=["CB"])
            for b_ in range(16):
                dve(lambda e, b_=b_: e.tensor_tensor(out=st1[:, :, :, 0], in0=CB[:, :, :, b_], in1=Ta[32][:, sl, :], op=ALU.mult), r=["CB", "P_T32"], w=["st1"])
                dve(lambda e, b_=b_: e.tensor_tensor(out=st2[:, :, :, 0], in0=CB[:, :, ::-1, b_], in1=Tb[32][:, sl, :], op=ALU.mult), r=["CB", "P_T32"], w=["st2"])
                dve(lambda e, b_=b_: e.tensor_tensor(out=st1[:, :, :, 0], in0=st1[:, :, :, 0], in1=st2[:, :, :, 0], op=ALU.add), r=["st1", "st2"], w=["st1"])
                dve(lambda e, b_=b_: e.tensor_tensor(out=CB[:, :, :, b_ + 1], in0=SPb[:, :, :, b_, 7], in1=st1[:, :, :, 0], op=ALU.add), r=["SP", "st1", "CB"], w=["CB"])
            for k in range(8):
                n_ = 4 * (k + 1)
                dve(lambda e, k=k, n_=n_: e.tensor_tensor(out=st1, in0=CB[:, :, :, 0:16], in1=tab[n_], op=ALU.mult), r=["CB", "P_T%d" % n_], w=["st1"])
                dve(lambda e, k=k, n_=n_: e.tensor_tensor(out=st2, in0=CB[:, :, ::-1, 0:16], in1=tbb[n_], op=ALU.mult), r=["CB", "P_T%d" % n_], w=["st2"])
                dve(lambda e, k=k: e.tensor_tensor(out=SPb[:, :, :, :, k], in0=SPb[:, :, :, :, k], in1=st1, op=ALU.add), r=["SP", "st1"], w=["SP"])
                dve(lambda e, k=k: e.tensor_tensor(out=SPb[:, :, :, :, k], in0=SPb[:, :, :, :, k], in1=st2, op=ALU.add), r=["SP", "st2"], w=["SP"])
            dve(lambda e, hf=hf: e.tensor_copy(out=scar[:, l, 16 * hf:16 * hf + 16, :], in_=SP[:, :, :, NCHP]), r=["SP"], w=["scar"])
            if NS:
                for ri in range(2):
                    for ii in range(16):
                        i = 16 * hf + ii
                        dve(lambda e, ii=ii, i=i, ri=ri: e.scalar_tensor_tensor(out=SO[:, ii, ri, :], in0=H0[:, ii, ri, :], scalar=p4r[:, i:i + 1],
                                                                               in1=XS[:, ii, ri, :], op0=ALU.mult, op1=ALU.add),
                            r=["H0", "XS", kp4], w=["SO"])
                        sc = npi[4] if ri == 0 else p4i
                        dve(lambda e, ii=ii, i=i, ri=ri, sc=sc: e.scalar_tensor_tensor(out=SO[:, ii, ri, :], in0=H0[:, ii, 1 - ri, :], scalar=sc[:, i:i + 1],
                                                                                      in1=SO[:, ii, ri, :], op0=ALU.mult, op1=ALU.add),
                            r=["H0", kp4, "P_npi4", "SO"], w=["SO"])
                S.op("sp", lambda e, hf=hf: e.dma_start(out=o_s5s[l][:, 16 * hf:16 * hf + 16], in_=SO), reads=["SO"], writes=[key("o_s5s")], dma=key("o"))
            for tl in range(4):
                t = 4 * hf + tl
                ybase = 4 + (tl % 2) * 2
                for ip in range(4):
                    ii = tl * 4 + ip
                    i = 16 * hf + ii
                    rows = slice(32 * ip, 32 * ip + 32)
                    hb = Hf[ii % 2]
                    khf = "Hf%d" % (ii % 2)
                    for j in range(3):
                        pjr, pji, kpj = PP[j + 1]
                        for ri in range(2):
                            b = (j * 2 + ri) % 4
                            for s in range(j + 1):
                                pe(lambda e, b=b, s=s, j=j, ri=ri, tl=tl, t=t, rows=rows, ip=ip: e.matmul(
                                    psb[b][:, 0:NC], XW[rows, j - s, ri, tl, :], u[rows, t, s:4 * NC:4],
                                    start=(s == 0), stop=(s == j), tile_position=(32 * ip, 0)),
                                   r=["XW", "u"], w=[kps[b]])
                            sc1 = pjr
                            sc2 = npi[j + 1] if ri == 0 else pji
                            groups = [(0, NCHP, SP[:, ii, ri, 0:NCHP], SP[:, ii, 1 - ri, 0:NCHP], "SP")]
                            if NS:
                                groups.append((NCHP, NSQ, H0[:, ii, ri, :], H0[:, ii, 1 - ri, :], "H0"))
                            for (c0, n, sa, sb_, ks) in groups:
                                dve(lambda e, b=b, c0=c0, n=n, sa=sa, i=i, sc1=sc1: e.scalar_tensor_tensor(
                                    out=tmpc[:, c0:c0 + n], in0=sa, scalar=sc1[:, i:i + 1], in1=psb[b][:, c0:c0 + n], op0=ALU.mult, op1=ALU.add),
                                    r=[ks, kps[b], kpj], w=["tmpc"])
                                dve(lambda e, c0=c0, n=n, sb_=sb_, i=i, sc2=sc2, hb=hb, j=j, ri=ri: e.scalar_tensor_tensor(
                                    out=hb[:, j, ri, c0:c0 + n], in0=sb_, scalar=sc2[:, i:i + 1], in1=tmpc[:, c0:c0 + n], op0=ALU.mult, op1=ALU.add),
                                    r=[ks, "tmpc", kpj, "P_npi%d" % (j + 1)], w=[khf])
                    for ri in range(2):
                        act(lambda e, ii=ii, ri=ri, hb=hb: e.activation(out=hb[:, 3, ri, 0:NCHP], in_=SP[:, ii, ri, 1:NCHP + 1], func=AF.Copy), r=["SP"], w=[khf])
                        if NS:
                            act(lambda e, ii=ii, ri=ri, hb=hb: e.activation(out=hb[:, 3, ri, NCHP:NC], in_=SO[:, ii, ri, :], func=AF.Copy), r=["SO"], w=[khf])
                    for j in range(4):
                        for ri in range(2):
                            pe(lambda e, ip=ip, j=j, ri=ri, i=i, hb=hb, ybase=ybase: e.matmul(
                                psb[ybase][32 * ip:32 * ip + 32, j:NPT:4], CTb[:, ri, i, :], hb[:, j, ri, 0:NCHP],
                                start=(ri == 0), stop=(ri == 1), tile_position=(0, 32 * ip)),
                               r=["CTb", khf], w=[kps[ybase]])
                            if NS:
                                pe(lambda e, ip=ip, j=j, ri=ri, i=i, hb=hb, ybase=ybase: e.matmul(
                                    psb[ybase + 1][32 * ip:32 * ip + 32, j:NS:4], CTb[:, ri, i, :], hb[:, j, ri, NCHP:NC],
                                    start=(ri == 0), stop=(ri == 1), tile_position=(0, 32 * ip)),
                                   r=["CTb", khf], w=[kps[ybase + 1]])
                for gi, (c0, n) in enumerate(cols):
                    dve(lambda e, t=t, c0=c0, n=n, b=ybase + gi: e.scalar_tensor_tensor(
                        out=ysb[:, c0:c0 + n], in0=u[:, t, c0:c0 + n], scalar=PAR[:, O_D + t:O_D + t + 1], in1=psb[b][:, 0:n], op0=ALU.mult, op1=ALU.add),
                        r=["u", "PAR", kps[ybase + gi]], w=["ysb"])
                    act(lambda e, t=t, c0=c0, n=n: e.activation(out=g5[:, t, c0:c0 + n], in_=ysb[:, c0:c0 + n], func=AF.Gelu_apprx_tanh),
                        r=["ysb"], w=["g5"])
        if last:
            S.op("sp", lambda e: e.dma_start(out=o_s5p[l], in_=scar[:, l]), reads=["scar"], writes=[key("o_s5p")], dma=key("o"))
        A.rewind(s5_mark)

    for p in range(npass):
        last = (p == npass - 1)
        NS = NSMP if last else 0
        NT = NPT + NS
        cols = [(0, NPT)] + ([(NPT, NS)] if NS else [])
        S.op("sp", lambda e, p=p: e.dma_start(out=X, in_=xin[p]), writes=[kX], dma="X")
        dve(lambda e: e.tensor_copy(out=xb, in_=X), r=[kX], w=[kxb])
        for l in range(depth):
            S.op("sp", lambda e, l=l: e.dma_start(out=PAR, in_=par[l]), writes=["PAR"], dma="PAR")
            A.rewind(base_mark)
            u = A.bf16([8, NTMAX])
            g5 = A.bf16([8, NTMAX])
            for ci in range(2):
                wt, kw = load_w(w_in[l][:, ci * 512:(ci + 1) * 512], KT, 512)
                for mi in range(4):
                    m = ci * 4 + mi
                    banks = matmul_tiles(wt, kw, mi, KT, xb, kxb, cols, m)
                    for gi, (c0, n) in enumerate(cols):
                        b = banks[gi]
                        act(lambda e, b=b, m=m, c0=c0, n=n: e.activation(out=u[:, m, c0:c0 + n], in_=psb[b][:, 0:n], func=AF.Copy),
                            r=[kps[b]], w=["u"])
            s5_stage(l, p, NS, cols, u, g5, last)
            if debug and last and l == 0:
                S.op("pool", lambda e: e.dma_start(out=dbg_g5, in_=g5), reads=["g5"], writes=[key("o_dbg")], dma=key("o"))
                S.op("pool", lambda e: e.dma_start(out=dbg_u, in_=u), reads=["u"], writes=[key("o_dbg")], dma=key("o"))
            for ci in range(2):
                wt, kw = load_w(w_glu[l][:, ci * 512:(ci + 1) * 512], 8, 512)
                for mi in range(4):
                    m = ci * 4 + mi
                    banks = matmul_tiles(wt, kw, mi, 8, g5, "g5", cols, m)
                    for gi, (c0, n) in enumerate(cols):
                        b = banks[gi]
                        act(lambda e, b=b, c0=c0, n=n: e.activation(out=sig[:, c0:c0 + n], in_=psb[b][:, 0:n], func=AF.Sigmoid),
                            r=[kps[b]], w=["sig"])
                        dve(lambda e, m=m, c0=c0, n=n: e.tensor_tensor(out=mixcat[:, m, c0:c0 + n], in0=g5[:, m, c0:c0 + n], in1=sig[:, c0:c0 + n], op=ALU.mult),
                            r=["g5", "sig"], w=["mixcat"])
            barrier()

            A.rewind(base_mark)
            cv = A.f32([8, NTMAX])
            convy = A.f32([8, NTMAX])
            cbp = A.bf16([8, 30 + NPT])
            cbs = A.bf16([8, NSEQ, 34])
            csn = A.f32([8, NSEQ, 4])
            dg = [A.bf16([31, 128]) for _ in range(2)]
            dve(lambda e, l=l: e.tensor_copy(out=cbp[:, :, 0:30], in_=chist[:, l]), r=["chist"], w=["cbp"])
            if NS:
                for hh in range(2):
                    S.op("pool", lambda e, l=l, hh=hh: e.dma_start(out=cbs[:, 4 * hh:4 * hh + 4, :, 0:30], in_=ccf[l][:, 4 * hh:4 * hh + 4]),
                         writes=["cbs"], dma="cbs%d" % hh)
            for ci in range(2, 6):
                wt, kw = load_w(w_in[l][:, ci * 512:(ci + 1) * 512], KT, 512)
                for mi in range(4):
                    m = ci * 4 + mi
                    banks = matmul_tiles(wt, kw, mi, KT, xb, kxb, cols, m)
                    for gi, (c0, n) in enumerate(cols):
                        b = banks[gi]
                        if m < 16:
                            act(lambda e, b=b, m=m, c0=c0, n=n: e.activation(out=cv[:, m - 8, c0:c0 + n], in_=psb[b][:, 0:n], func=AF.Copy),
                                r=[kps[b]], w=["cv"])
                        else:
                            mm = m - 16
                            act(lambda e, b=b, c0=c0, n=n: e.activation(out=sig[:, c0:c0 + n], in_=psb[b][:, 0:n], func=AF.Sigmoid),
                                r=[kps[b]], w=["sig"])
                            if gi == 0:
                                dve(lambda e, mm=mm: e.tensor_tensor(out=cbp[:, mm, 30:30 + NPT], in0=cv[:, mm, 0:NPT], in1=sig[:, 0:NPT], op=ALU.mult),
                                    r=["cv", "sig"], w=["cbp"])
                                dve(lambda e, mm=mm, l=l: e.tensor_tensor(out=chist[:, l, mm, :], in0=cv[:, mm, NPT - 30:NPT], in1=sig[:, NPT - 30:NPT], op=ALU.mult),
                                    r=["cv", "sig", "cbp"], w=["chist"])
                            else:
                                dve(lambda e, mm=mm: e.tensor_tensor(out=csn[:, mm],
                                                                     in0=cv[:, mm, NPT:NPT + NS].rearrange("p (s t) -> p s t", t=4),
                                                                     in1=sig[:, NPT:NPT + NS].rearrange("p (s t) -> p s t", t=4), op=ALU.mult),
                                    r=["cv", "sig"], w=["csn"])
                                dve(lambda e, mm=mm: e.tensor_copy(out=cbs[:, mm, :, 30:34], in_=csn[:, mm]), r=["csn"], w=["cbs"])
            if last:
                S.op("sp", lambda e, l=l: e.dma_start(out=o_cvp[l], in_=chist[:, l]), reads=["chist"], writes=[key("o_cvp")], dma=key("o"))
                S.op("sp", lambda e, l=l: e.dma_start(out=o_cvsn[l], in_=csn), reads=["csn"], writes=[key("o_cvsn")], dma=key("o"))
                S.op("sp", lambda e, l=l: e.dma_start(out=o_cvso[l], in_=ccr[l][:, 4:30, :]), writes=[key("o_cvso")], dma=key("o"))
            for m in range(8):
                d_ = dg[m % 2]
                kd = "dg%d" % (m % 2)
                for k in range(31):
                    act(lambda e, d_=d_, m=m, k=k: e.activation(out=d_[:, k, :], in_=ident, func=AF.Identity,
                                                                scale=PAR[:, O_CW + m * 31 + k:O_CW + m * 31 + k + 1]),
                        r=["ident", "PAR"], w=[kd])
                for gi, (c0, n) in enumerate(cols):
                    b = (m % 2) * 2 + gi
                    for k in range(31):
                        if gi == 0:
                            pe(lambda e, b=b, k=k, m=m, d_=d_: e.matmul(psb[b][:, 0:NPT], d_[:, k, :], cbp[:, m, k:k + NPT], start=(k == 0), stop=(k == 30)),
                               r=[kd, "cbp"], w=[kps[b]])
                        else:
                            pe(lambda e, b=b, k=k, m=m, d_=d_: e.matmul(psb[b][:, 0:NS].rearrange("p (s t) -> p s t", t=4), d_[:, k, :], cbs[:, m, :, k:k + 4],
                                                                        start=(k == 0), stop=(k == 30)),
                               r=[kd, "cbs"], w=[kps[b]])
                    act(lambda e, b=b, m=m, c0=c0, n=n: e.activation(out=convy[:, m, c0:c0 + n], in_=psb[b][:, 0:n], func=AF.Identity,
                                                                    bias=PAR[:, O_CB + m:O_CB + m + 1]),
                        r=[kps[b], "PAR"], w=["convy"])
            layernorm(convy, "convy", cols, O_CLG, O_CLB, 8, None, None, mixcat[:, 8:16, :], "mixcat", func=AF.Silu)

            def proj_residual(wsrc, nk, rhs, krhs):
                for ci in range(4):
                    wt, kw = load_w(wsrc[:, ci * 512:(ci + 1) * 512], nk, 512)
                    for mi in range(4):
                        m = ci * 4 + mi
                        banks = matmul_tiles(wt, kw, mi, nk, rhs, krhs, cols, m)
                        for gi, (c0, n) in enumerate(cols):
                            b = banks[gi]
                            dve(lambda e, b=b, m=m, c0=c0, n=n: e.scalar_tensor_tensor(out=X[:, m, c0:c0 + n], in0=X[:, m, c0:c0 + n], scalar=ALPHA,
                                                                                      in1=psb[b][:, 0:n], op0=ALU.mult, op1=ALU.add),
                                r=[kX, kps[b]], w=[kX])
            if debug and last and l == 0:
                S.op("pool", lambda e: e.dma_start(out=dbg_mix, in_=mixcat), reads=["mixcat"], writes=[key("o_dbg")], dma=key("o"))
            proj_residual(w_out[l], KT, mixcat, "mixcat")
            layernorm(X, kX, cols, O_LN1G, O_LN1B, KT, X, kX, xb, kxb)
            if debug and last and l == 0:
                S.op("sp", lambda e: e.dma_start(out=dbg_x1, in_=X), reads=[kX], writes=[key("o_dbg")], dma=key("o"))
            barrier()

            A.rewind(base0_mark)
            actb = A.bf16([FT, NTMAX])
            extp = [A.f32([2 + NPT]) for _ in range(2)]
            exts = [A.f32([NSEQ, 6]) for _ in range(2)]
            hy = [A.f32([NTMAX]) for _ in range(2)]
            cfs = [A.f32([NSEQ, 2]) for _ in range(2)]
            ffs_out = A.f32([88, NSEQ, 2])
            pb = A.bf16([2, NTMAX])
            wpe = A.bf16([2, D])
            wv = w_up[l].rearrange("(k p) c -> p k c", p=128)
            for j in range(FT):
                s = ring_i[0] % 2
                ring_i[0] += 1
                wt = wring[s][:, :, 0:256]
                kwa, kwb = "wr%da" % s, "wr%db" % s
                S.op("pool", lambda e, wt=wt, j=j: e.dma_start(out=wt[:, :, 0:128], in_=wv[:, :, j * 128:(j + 1) * 128]), writes=[kwa], dma=kwa)
                S.op("pool", lambda e, wt=wt, j=j: e.dma_start(out=wt[:, :, 128:256], in_=wv[:, :, DFF + j * 128:DFF + (j + 1) * 128]), writes=[kwb], dma=kwb)
                for h in range(2):
                    ft = h * FT + j
                    kwh = [kwa, kwb][h]
                    for gi, (c0, n) in enumerate(cols):
                        b = h * 2 + gi
                        for k in range(KT):
                            pe(lambda e, b=b, k=k, h=h, c0=c0, n=n, wt=wt: e.matmul(psb[b][:, 0:n], wt[:, k, h * 128:(h + 1) * 128], xb[:, k, c0:c0 + n],
                                                                                     start=(k == 0), stop=(k == KT - 1)),
                               r=[kwh, kxb], w=[kps[b]])
                    wf = lambda k, ft=ft: PAR[:, O_FCW + ft * 3 + k:O_FCW + ft * 3 + k + 1]
                    bf = PAR[:, O_FCB + ft:O_FCB + ft + 1]
                    ke = "extp%d" % h
                    kh = "hy%d" % h
                    dve(lambda e, h=h, ft=ft, l=l: e.tensor_copy(out=extp[h][:, 0:2], in_=fhist[:, l, ft, :]), r=["fhist"], w=[ke])
                    act(lambda e, h=h: e.activation(out=extp[h][:, 2:2 + NPT], in_=psb[h * 2][:, 0:NPT], func=AF.Copy), r=[kps[h * 2]], w=[ke])
                    dve(lambda e, h=h, ft=ft, l=l: e.tensor_copy(out=fhist[:, l, ft, :], in_=extp[h][:, NPT:NPT + 2]), r=[ke], w=["fhist"])
                    dve(lambda e, h=h, wf=wf, bf=bf: e.tensor_scalar(out=hy[h][:, 0:NPT], in0=extp[h][:, 2:2 + NPT], scalar1=wf(2), scalar2=bf, op0=ALU.mult, op1=ALU.add),
                        r=[ke, "PAR"], w=[kh])
                    for k in range(2):
                        dve(lambda e, h=h, k=k, wf=wf: e.scalar_tensor_tensor(out=hy[h][:, 0:NPT], in0=extp[h][:, k:k + NPT], scalar=wf(k), in1=hy[h][:, 0:NPT],
                                                                             op0=ALU.mult, op1=ALU.add),
                            r=[ke, "PAR", kh], w=[kh])
                    if NS:
                        kes = "exts%d" % h
                        kcf = "cfs%d" % h
                        hys = hy[h][:, NPT:NPT + NS].rearrange("p (s t) -> p s t", t=4)
                        S.op("sp", lambda e, h=h, ft=ft, l=l: e.dma_start(out=cfs[h], in_=cff[l][:, ft]), writes=[kcf], dma=kcf)
                        dve(lambda e, h=h: e.tensor_copy(out=exts[h][:, :, 0:2], in_=cfs[h]), r=[kcf], w=[kes])
                        act(lambda e, h=h: e.activation(out=exts[h][:, :, 2:6], in_=psb[h * 2 + 1][:, 0:NS].rearrange("p (s t) -> p s t", t=4), func=AF.Copy),
                            r=[kps[h * 2 + 1]], w=[kes])
                        dve(lambda e, h=h, ft=ft: e.tensor_copy(out=ffs_out[:, ft], in_=exts[h][:, :, 4:6]), r=[kes], w=["ffs_out"])
                        dve(lambda e, h=h, wf=wf, bf=bf, hys=hys: e.tensor_scalar(out=hys, in0=exts[h][:, :, 2:6], scalar1=wf(2), scalar2=bf, op0=ALU.mult, op1=ALU.add),
                            r=[kes, "PAR"], w=[kh])
                        for k in range(2):
                            dve(lambda e, h=h, k=k, wf=wf, hys=hys: e.scalar_tensor_tensor(out=hys, in0=exts[h][:, :, k:k + 4], scalar=wf(k), in1=hys,
                                                                                          op0=ALU.mult, op1=ALU.add),
                                r=[kes, "PAR", kh], w=[kh])
                act(lambda e: e.activation(out=hy[0][:, 0:NT], in_=hy[0][:, 0:NT], func=AF.Silu), r=["hy0"], w=["hy0"])
                dve(lambda e, j=j: e.tensor_tensor(out=actb[:, j, 0:NT], in0=hy[0][:, 0:NT], in1=hy[1][:, 0:NT], op=ALU.mult), r=["hy0", "hy1"], w=["actb"])
            if last:
                S.op("sp", lambda e, l=l: e.dma_start(out=o_ffp[l], in_=fhist[:, l]), reads=["fhist"], writes=[key("o_ffp")], dma=key("o"))
                S.op("sp", lambda e, l=l: e.dma_start(out=o_ffs[l], in_=ffs_out), reads=["ffs_out"], writes=[key("o_ffs")], dma=key("o"))
            wdv = w_down[l].rearrange("(k p) c -> p k c", p=128)
            for m in range(KT):
                s_ = ring_i[0] % 2
                ring_i[0] += 1
                wd = wring[s_].rearrange("p k c -> p (k c)")[:, 0:FT * 128].rearrange("p (k c) -> p k c", k=FT)
                kw = ["wr%da" % s_, "wr%db" % s_]
                S.op("pool", lambda e, wd=wd, m=m: e.dma_start(out=wd, in_=wdv[:, :, m * 128:(m + 1) * 128]), writes=kw, dma=kw[0])
                for gi, (c0, n) in enumerate(cols):
                    b = (m % 2) * 2 + gi
                    for k in range(FT):
                        pe(lambda e, b=b, k=k, wd=wd, c0=c0, n=n: e.matmul(psb[b][:, 0:n], wd[:, k, :], actb[:, k, c0:c0 + n], start=(k == 0), stop=(k == FT - 1)),
                           r=kw + ["actb"], w=[kps[b]])
                    dve(lambda e, b=b, m=m, c0=c0, n=n: e.scalar_tensor_tensor(out=X[:, m, c0:c0 + n], in0=X[:, m, c0:c0 + n], scalar=ALPHA,
                                                                              in1=psb[b][:, 0:n], op0=ALU.mult, op1=ALU.add),
                        r=[kX, kps[b]], w=[kX])
            layernorm(X, kX, cols, O_LN2G, O_LN2B, KT, X, kX, xb, kxb)

            S.op("pool", lambda e, l=l, p=p: e.dma_start(out=pb, in_=pin[l, p]), writes=["pb"], dma="pb")
            S.op("pool", lambda e, l=l: e.dma_start(out=wpe, in_=w_pe[l].rearrange("(k p) c -> p k c", p=128)), writes=["wpe"], dma="wpe")
            for ci in range(4):
                wt, kw = load_w(w_gate[l][:, ci * 512:(ci + 1) * 512], KT, 512)
                for mi in range(4):
                    m = ci * 4 + mi
                    banks = matmul_tiles(wt, kw, mi, KT, xb, kxb, cols, m)
                    for gi, (c0, n) in enumerate(cols):
                        b = banks[gi]
                        be = 4 + gi
                        for k in range(2):
                            pe(lambda e, be=be, k=k, m=m, c0=c0, n=n: e.matmul(psb[be][:, 0:n], wpe[:, k, m * 128:(m + 1) * 128], pb[:, k, c0:c0 + n],
                                                                                 start=(k == 0), stop=(k == 1)),
                               r=["wpe", "pb"], w=[kps[be]])
                        act(lambda e, b=b, c0=c0, n=n: e.activation(out=sig[:, c0:c0 + n], in_=psb[b][:, 0:n], func=AF.Sigmoid), r=[kps[b]], w=["sig"])
                        dve(lambda e, be=be, c0=c0, n=n: e.tensor_tensor(out=sig[:, c0:c0 + n], in0=sig[:, c0:c0 + n], in1=psb[be][:, 0:n], op=ALU.mult),
                            r=["sig", kps[be]], w=["sig"])
                        dve(lambda e, m=m, c0=c0, n=n: e.scalar_tensor_tensor(out=X[:, m, c0:c0 + n], in0=X[:, m, c0:c0 + n], scalar=ALPHA,
                                                                             in1=sig[:, c0:c0 + n], op0=ALU.mult, op1=ALU.add),
                            r=[kX, "sig"], w=[kX])
            layernorm(X, kX, cols, O_LN3G, O_LN3B, KT, X, kX, xb, kxb)
            barrier()
        S.op("sp", lambda e, p=p: e.dma_start(out=o_y[p], in_=X), reads=[kX], writes=[key("o_y")], dma="o_y")
    S.barrier(lambda e: e.memset(dummy, 0.0))
    S.emit(nc, st)
    st.close()
    return nc, S, A


def make_core_inputs(inp, b, s0, depth=DEPTH, npass=NPASS):
    f = np.float32
    L = depth
    m = {}
    xin = np.zeros((npass, 128, KT, NTMAX), f)
    pin = np.zeros((L, npass, 128, 2, NTMAX), f)
    for p in range(npass):
        xs = inp["x_prompt"][b, p * NPT:(p + 1) * NPT]
        xin[p, :, :, :NPT] = xs.reshape(NPT, KT, 128).transpose(2, 1, 0)
        ps = inp["p_prompt"][:L, b, p * NPT:(p + 1) * NPT]
        pin[:, p, :, :, :NPT] = ps.reshape(L, NPT, 2, 128).transpose(0, 3, 2, 1)
    xs = inp["x_sample"][s0:s0 + NSEQ].reshape(NSMP, D)
    xin[npass - 1, :, :, NPT:] = xs.reshape(NSMP, KT, 128).transpose(2, 1, 0)
    ps = inp["p_sample"][:L, s0:s0 + NSEQ].reshape(L, NSMP, 256)
    pin[:, npass - 1, :, :, NPT:] = ps.reshape(L, NSMP, 2, 128).transpose(0, 3, 2, 1)
    m["xin"], m["pin"] = xin, pin

    par = np.zeros((L, 128, NPAR), f)

    def put(off, v, ntile):
        par[:, :, off:off + ntile] = v.reshape(L, ntile, 128).transpose(0, 2, 1)
    put(O_LN1G, inp["ln1_g"][:L], 16); put(O_LN1B, inp["ln1_b"][:L], 16)
    put(O_LN2G, inp["ln2_g"][:L], 16); put(O_LN2B, inp["ln2_b"][:L], 16)
    put(O_LN3G, inp["ln3_g"][:L], 16); put(O_LN3B, inp["ln3_b"][:L], 16)
    put(O_CB, inp["conv_b"][:L], 8); put(O_CLG, inp["conv_ln_g"][:L], 8); put(O_CLB, inp["conv_ln_b"][:L], 8)
    put(O_D, inp["s5_d"][:L].reshape(L, 1024), 8)
    par[:, :, O_CW:O_CW + 248] = inp["conv_w"][:L].reshape(L, 31, 8, 128).transpose(0, 3, 2, 1).reshape(L, 128, 248)
    par[:, :, O_FCW:O_FCW + 264] = inp["ffn_conv_w"][:L].reshape(L, 3, 88, 128).transpose(0, 3, 2, 1).reshape(L, 128, 264)
    put(O_FCB, inp["ffn_conv_b"][:L], 88)
    m["par"] = par

    def lay_p(v):
        return v.reshape(L, 32, 2, 64).transpose(0, 2, 3, 1).reshape(L, 128, 32)
    ldt = np.broadcast_to(inp["s5_log_dt"][:L, :, None], (L, 64, 64))
    m["lamp"] = np.ascontiguousarray(np.stack([lay_p(inp["s5_lam_re"][:L]), lay_p(inp["s5_lam_im"][:L]), lay_p(ldt)], axis=2))
    ctp = np.zeros((L, 128, 2, 32, 32), f)
    for ri, nm in enumerate(("s5_c_re", "s5_c_im")):
        c = inp[nm][:L].reshape(L, 32, 2, 16, 64)
        for g2 in range(2):
            ctp[:, g2 * 64:(g2 + 1) * 64, ri, :, g2 * 16:(g2 + 1) * 16] = c[:, :, g2].transpose(0, 3, 1, 2)
    m["ctp"] = ctp
    def lay_f(v):
        w = v.reshape(L, 8, 4, 2, 64)
        w = w.transpose(0, 2, 1, 3, 4).reshape(L, 4, 1, 8, 128)
        return np.broadcast_to(w, (L, 4, 32, 8, 128)).reshape(L, 128, 8, 128)
    m["lamf"] = np.ascontiguousarray(np.stack([lay_f(inp["s5_lam_re"][:L]), lay_f(inp["s5_lam_im"][:L]), lay_f(ldt)], axis=2))
    btf = np.zeros((L, 4, 2, 16, 2, 8, 2, 64), f)
    for ri, nm in enumerate(("s5_b_re", "s5_b_im")):
        bb = inp[nm][:L].reshape(L, 8, 4, 2, 64, 16)
        for g2 in range(2):
            btf[:, :, g2, :, ri, :, g2, :] = bb[:, :, :, g2].transpose(0, 2, 4, 1, 3)
    m["btf"] = btf.reshape(L, 128, 2, 8, 128)
    sin_ = np.zeros((L, 128, 32, 2, NSEQ), f)
    for ri, nm in enumerate(("state_s5_re", "state_s5_im")):
        sv = inp[nm][:L, s0:s0 + NSEQ].reshape(L, NSEQ, 32, 2, 64)
        sin_[:, :, :, ri, :] = sv.transpose(0, 3, 4, 2, 1).reshape(L, 128, 32, NSEQ)
    m["sin"] = sin_
    cc = inp["cache_conv"][:L, s0:s0 + NSEQ]
    m["ccr"] = np.ascontiguousarray(cc)
    m["ccf"] = np.ascontiguousarray(cc.reshape(L, NSEQ, 30, 8, 128).transpose(0, 4, 3, 1, 2))
    cf = inp["cache_ffn_conv"][:L, s0:s0 + NSEQ]
    m["cff"] = np.ascontiguousarray(cf.reshape(L, NSEQ, 2, 88, 128).transpose(0, 4, 3, 1, 2))
    m["w_in"] = inp["w_in"][:L]; m["w_glu"] = inp["s5_w_glu"][:L]; m["w_out"] = inp["w_out"][:L]
    m["w_up"] = inp["ffn_w_up"][:L]; m["w_down"] = inp["ffn_w_down"][:L]
    m["w_pe"] = inp["pe_w"][:L]; m["w_gate"] = inp["pe_w_gate"][:L]
    return m


def unpack_core(res, depth=DEPTH, npass=NPASS):
    L = depth
    o = {}
    y = res["o_y"]
    yp = y[:, :, :, :NPT].transpose(0, 3, 2, 1).reshape(npass * NPT, D)
    ys = y[npass - 1, :, :, NPT:].transpose(2, 1, 0).reshape(NSEQ, 4, D)
    o["y_prompt"], o["y_sample"] = yp, ys
    sp = res["o_s5p"].reshape(L, 2, 64, 32, 2)
    sp = sp.transpose(0, 4, 3, 1, 2).reshape(L, 2, 64, 64)
    o["s5_re_prompt"], o["s5_im_prompt"] = sp[:, 0], sp[:, 1]
    ss = res["o_s5s"].reshape(L, 2, 64, 32, 2, NSEQ)
    ss = ss.transpose(0, 4, 5, 3, 1, 2).reshape(L, 2, NSEQ, 64, 64)
    o["s5_re_sample"], o["s5_im_sample"] = ss[:, 0], ss[:, 1]
    o["conv_prompt"] = res["o_cvp"].transpose(0, 3, 2, 1).reshape(L, 30, 1024)
    new = res["o_cvsn"].transpose(0, 3, 4, 2, 1).reshape(L, NSEQ, 4, 1024)
    o["conv_sample"] = np.concatenate([res["o_cvso"], new], axis=2)
    o["ffn_conv_prompt"] = res["o_ffp"].transpose(0, 3, 2, 1).reshape(L, 2, 11264)
    o["ffn_conv_sample"] = res["o_ffs"].transpose(0, 3, 4, 2, 1).reshape(L, NSEQ, 2, 11264)
    return o


_PROG = {}


def kernel(**inputs):
    inp = {k: np.asarray(v) for k, v in inputs.items()}
    if "nc" not in _PROG:
        _PROG["nc"] = build_program()[0]
    nc = _PROG["nc"]
    n = 8
    in_maps = [make_core_inputs(inp, c % 4, 16 * c) for c in range(n)]
    res = run_bass_kernel_spmd(nc, in_maps, core_ids=list(range(n)))
    outs = [unpack_core(r) for r in res.results]
    f = np.float32
    y_prompt = np.stack([outs[b]["y_prompt"] for b in range(4)]).astype(f)
    y_sample = np.concatenate([outs[c]["y_sample"] for c in range(n)]).astype(f)

    def pstack(nm):
        return np.stack([outs[b][nm] for b in range(4)], axis=1).astype(f)

    def sstack(nm):
        return np.concatenate([outs[c][nm] for c in range(n)], axis=1).astype(f)

    return (y_prompt, y_sample,
            pstack("s5_re_prompt"), pstack("s5_im_prompt"), pstack("conv_prompt"), pstack("ffn_conv_prompt"),
            sstack("s5_re_sample"), sstack("s5_im_sample"), sstack("conv_sample"), sstack("ffn_conv_sample"))
```

```python
import contextlib
import math
import types
import numpy as np
import concourse.bass as bass
import concourse.mybir as mybir
from concourse.bass_utils import run_bass_kernel_spmd

F32 = mybir.dt.float32
BF16 = mybir.dt.bfloat16
I32 = mybir.dt.int32
AF = mybir.ActivationFunctionType
ALU = mybir.AluOpType

DEPTH = 4
D = 2048
KT = 16
NPASS = 4
NPT = 512
NSEQ = 16
NSMP = 64
NTMAX = NPT + NSMP
NCHP = 128
DFF = 5632
FT = 44
ALPHA = (2.0 * DEPTH) ** 0.25
EPS = 1e-5
O_LN1G, O_LN1B, O_LN2G, O_LN2B, O_LN3G, O_LN3B = 0, 16, 32, 48, 64, 80
O_CB, O_CLG, O_CLB, O_D, O_CW, O_FCW, O_FCB = 96, 104, 112, 120, 128, 376, 640
NPAR = 728
ENGS = ("pe", "act", "dve", "pool", "sp")
PI = math.pi


class Op:
    __slots__ = ("eng", "fn", "deps", "dma", "signal", "ev", "pos", "epoch")

    def __init__(self, eng, fn, deps, dma):
        self.eng, self.fn, self.deps, self.dma = eng, fn, deps, dma
        self.signal, self.ev, self.pos, self.epoch = False, None, 0, 0


def _freeze(fn):
    if fn.__closure__ is None:
        return fn
    cells = []
    for c in fn.__closure__:
        try:
            cells.append(types.CellType(c.cell_contents))
        except ValueError:
            cells.append(c)
    g = types.FunctionType(fn.__code__, fn.__globals__, fn.__name__, fn.__defaults__, tuple(cells))
    g.__kwdefaults__ = fn.__kwdefaults__
    return g


class Sched:
    def __init__(self):
        self.ops = []
        self.last_writer = {}
        self.readers = {}
        self.queues = {e: [] for e in ENGS}
        self.bar = None
        self.dmas_since_bar = []
        self.epoch = 0

    def op(self, eng, fn, reads=(), writes=(), dma=None, nobar=False):
        deps = set()
        if self.bar is not None and not nobar:
            deps.add(self.bar)
        for k in reads:
            w = self.last_writer.get(k)
            if w is not None:
                deps.add(w)
        for k in writes:
            w = self.last_writer.get(k)
            if w is not None:
                deps.add(w)
            deps.update(self.readers.get(k, ()))
        idx = len(self.ops)
        o = Op(eng, _freeze(fn), deps, dma)
        o.pos = len(self.queues[eng])
        o.epoch = self.epoch
        self.ops.append(o)
        self.queues[eng].append(idx)
        if dma is not None:
            self.dmas_since_bar.append(idx)
        for k in reads:
            self.readers.setdefault(k, []).append(idx)
        for k in writes:
            self.last_writer[k] = idx
            self.readers[k] = []
        return idx

    def barrier(self, fn):
        deps = set(self.dmas_since_bar)
        for e in ENGS:
            if self.queues[e]:
                deps.add(self.queues[e][-1])
        if self.bar is not None:
            deps.add(self.bar)
        idx = len(self.ops)
        o = Op("dve", fn, deps, None)
        o.pos = len(self.queues["dve"])
        o.epoch = self.epoch
        self.ops.append(o)
        self.queues["dve"].append(idx)
        self.bar = idx
        self.dmas_since_bar = []
        self.last_writer = {k: v for k, v in self.last_writer.items() if k.startswith("wr")}
        self.readers = {k: v for k, v in self.readers.items() if k.startswith("wr")}
        self.epoch += 1

    def emit(self, nc, stack):
        ops = self.ops
        need = [[] for _ in ops]
        for i, o in enumerate(ops):
            for d in o.deps:
                p = ops[d]
                if p.dma is None and p.eng == o.eng:
                    if o.eng in ("pe", "sp"):
                        continue
                    if o.pos - p.pos > 3:
                        continue
                if p.dma is None:
                    p.signal = True
                need[i].append(d)
        cnt = {}
        dcnt = {}
        for o in ops:
            if o.dma is not None:
                dcnt[o.dma] = dcnt.get(o.dma, 0) + 16
                o.ev = (("dma", o.dma), dcnt[o.dma])
            elif o.signal:
                k = ("eng", o.eng, o.epoch // 12)
                cnt[k] = cnt.get(k, 0) + 1
                o.ev = (k, cnt[k])
        sems = {}
        for k in list(cnt.keys()) + [("dma", c) for c in dcnt]:
            sems[k] = stack.enter_context(nc.semaphore("s%d" % len(sems)))
        self.nsems = len(sems)
        block = stack.enter_context(nc.Block())

        def run(engname):
            def body(eng):
                waited = {}
                for idx in self.queues[engname]:
                    o = ops[idx]
                    w = {}
                    for d in need[idx]:
                        sk, v = ops[d].ev
                        if waited.get(sk, 0) >= v:
                            continue
                        if w.get(sk, 0) < v:
                            w[sk] = v
                    for sk, v in w.items():
                        eng.wait_ge(sems[sk], v)
                        waited[sk] = v
                    ins = o.fn(eng)
                    if ins is None:
                        continue
                    if o.dma is not None:
                        ins.then_inc(sems[o.ev[0]], 16)
                    elif o.signal:
                        ins.then_inc(sems[o.ev[0]], 1)
            return body

        block.tensor(run("pe"))
        block.scalar(run("act"))
        block.vector(run("dve"))
        block.gpsimd(run("pool"))
        block.sync(run("sp"))


class Arena:
    def __init__(self, t, nwords):
        self.t, self.n, self.off, self.peak = t, nwords, 0, 0

    def mark(self):
        return self.off

    def rewind(self, m):
        self.off = m

    def _take(self, nw):
        assert self.off + nw <= self.n, ("arena overflow", self.off, nw, self.n)
        ap = self.t[:, self.off:self.off + nw]
        self.off += nw
        self.peak = max(self.peak, self.off)
        return ap

    def f32(self, shape, dt=None):
        n = int(np.prod(shape))
        ap = self._take(n)
        if dt is not None:
            ap = ap.bitcast(dt)
        return self._shape(ap, shape)

    def bf16(self, shape):
        n = int(np.prod(shape))
        ap = self._take((n + 1) // 2).bitcast(BF16)[:, 0:n]
        return self._shape(ap, shape)

    @staticmethod
    def _shape(ap, shape):
        if len(shape) == 1:
            return ap
        names = " ".join("d%d" % i for i in range(len(shape)))
        kw = {"d%d" % i: s for i, s in enumerate(shape)}
        return ap.rearrange("p (%s) -> p %s" % (names, names), **kw)


def build_program(depth=DEPTH, npass=NPASS, debug=False):
    nc = bass.Bass("TRN2", target_bir_lowering=False)
    S = Sched()

    def din(name, shape):
        return nc.dram_tensor(name, list(shape), F32, kind="ExternalInput").ap()

    def dout(name, shape):
        return nc.dram_tensor(name, list(shape), F32, kind="ExternalOutput").ap()

    xin = din("xin", [npass, 128, KT, NTMAX])
    pin = din("pin", [depth, npass, 128, 2, NTMAX])
    par = din("par", [depth, 128, NPAR])
    lamp = din("lamp", [depth, 128, 3, 32])
    ctp = din("ctp", [depth, 128, 2, 32, 32])
    lamf = din("lamf", [depth, 128, 3, 8, 128])
    btf = din("btf", [depth, 128, 2, 8, 128])
    sin_ = din("sin", [depth, 128, 32, 2, NSEQ])
    ccf = din("ccf", [depth, 128, 8, NSEQ, 30])
    ccr = din("ccr", [depth, NSEQ, 30, 1024])
    cff = din("cff", [depth, 128, 88, NSEQ, 2])
    w_in = din("w_in", [depth, D, 3072])
    w_glu = din("w_glu", [depth, 1024, 1024])
    w_out = din("w_out", [depth, D, D])
    w_up = din("w_up", [depth, D, 2 * DFF])
    w_down = din("w_down", [depth, DFF, D])
    w_pe = din("w_pe", [depth, 256, D])
    w_gate = din("w_gate", [depth, D, D])

    o_y = dout("o_y", [npass, 128, KT, NTMAX])
    o_s5p = dout("o_s5p", [depth, 128, 32, 2])
    o_s5s = dout("o_s5s", [depth, 128, 32, 2, NSEQ])
    o_cvp = dout("o_cvp", [depth, 128, 8, 30])
    o_cvsn = dout("o_cvsn", [depth, 128, 8, NSEQ, 4])
    o_cvso = dout("o_cvso", [depth, NSEQ, 26, 1024])
    o_ffp = dout("o_ffp", [depth, 128, 88, 2])
    o_ffs = dout("o_ffs", [depth, 128, 88, NSEQ, 2])
    if debug:
        dbg_g5 = dout("dbg_g5", [128, 8, NTMAX])
        dbg_u = dout("dbg_u", [128, 8, NTMAX])
        dbg_mix = dout("dbg_mix", [128, KT, NTMAX])
        dbg_x1 = dout("dbg_x1", [128, KT, NTMAX])

    xw_scr = nc.dram_tensor("xw_scr", [depth, 2, 128, 4096], BF16, kind="Internal").ap()
    pt_scr = nc.dram_tensor("pt_scr", [depth, 128, 1408], F32, kind="Internal").ap()
    st = contextlib.ExitStack()
    NW = 52000
    arena_t = st.enter_context(nc.sbuf_tensor("arena", [128, NW], F32))
    A = Arena(arena_t, NW)
    ps_all = st.enter_context(nc.psum_tensor("ps_all", [128, 8 * 512], F32))
    psb = [ps_all[:, i * 512:(i + 1) * 512] for i in range(8)]
    kps = ["ps%d" % i for i in range(8)]

    uid = [0]

    def key(prefix):
        if prefix == "o":
            uid[0] += 1
            return "oshared%d" % (uid[0] % 4)
        uid[0] += 1
        return "%s#%d" % (prefix, uid[0])

    def dve(fn, r=(), w=()):
        S.op("dve", fn, r, w)

    def act(fn, r=(), w=()):
        S.op("act", fn, r, w)

    def pe(fn, r=(), w=()):
        S.op("pe", fn, r, w)

    X = A.f32([KT, NTMAX]);   kX = "X"
    xb = A.bf16([KT, NTMAX]); kxb = "xb"
    ones = A.f32([128])
    scar = A.f32([depth, 32, 2])
    chist = A.f32([depth, 8, 30])
    fhist = A.f32([depth, 88, 2])
    dummy = A.f32([2])
    consts = A.f32([4])
    PAR = A.f32([NPAR])
    wring = [A.bf16([KT, 512]) for _ in range(2)]
    sig = A.f32([NTMAX])
    lnt = tuple(A.f32([512]) for _ in range(4))
    ident = A.f32([128])
    base0_mark = A.mark()
    mixcat = A.bf16([KT, NTMAX])
    base_mark = A.mark()

    dve(lambda e: e.memset(ones, 1.0), w=["ones"])
    dve(lambda e: e.memset(consts[:, 0:1], EPS), w=["consts"])
    dve(lambda e: e.memset(consts[:, 1:2], PI / 2), w=["consts"])
    dve(lambda e: e.memset(scar, 0.0), w=["scar"])
    dve(lambda e: e.memset(chist, 0.0), w=["chist"])
    dve(lambda e: e.memset(fhist, 0.0), w=["fhist"])
    dve(lambda e: e.memset(dummy, 0.0), w=["dummy"])
    epsb = consts[:, 0:1]
    halfpi = consts[:, 1:2]
    ident_i = lnt[0][:, 0:128].bitcast(I32)
    S.op("pool", lambda e: e.iota(out=ident_i, pattern=[[1, 128]], base=0, channel_multiplier=-1), writes=["ident_i"])
    dve(lambda e: e.tensor_scalar(out=ident, in0=ident_i, scalar1=0.0, scalar2=None, op0=ALU.is_equal), r=["ident_i"], w=["ident"])
    S.barrier(lambda e: e.memset(dummy, 0.0))

    ring_i = [0]

    def load_w(src_ap, nk, ncols):
        s = ring_i[0] % 2
        ring_i[0] += 1
        dst = wring[s].rearrange("p k c -> p (k c)")[:, 0:nk * ncols].rearrange("p (k c) -> p k c", k=nk)
        srcv = src_ap.rearrange("(k p) c -> p k c", p=128)
        S.op("pool", lambda e: e.dma_start(out=dst, in_=srcv), writes=["wr%da" % s, "wr%db" % s], dma="wr%da" % s, nobar=True)
        return dst, ["wr%da" % s, "wr%db" % s]

    def barrier():
        S.barrier(lambda e: e.memset(dummy, 0.0))

    def layernorm(src, ksrc, cols_list, goff, boff, ntile, dst_f32, kdst, dst_bf, kdstb, func=None):
        mean, rstd, sq, t1 = lnt
        inv = 1.0 / (ntile * 128)
        for (c0, n) in cols_list:
            ps_s, ps_q = psb[6], psb[7]
            for m in range(ntile):
                pe(lambda e, m=m: e.matmul(ps_s[:, 0:n], ones, src[:, m, c0:c0 + n], start=(m == 0), stop=(m == ntile - 1)),
                   r=[ksrc, "ones"], w=["ps6"])
            for m in range(ntile):
                act(lambda e, m=m: e.activation(out=sq[:, 0:n], in_=src[:, m, c0:c0 + n], func=AF.Square), r=[ksrc], w=["ln_sq"])
                pe(lambda e, m=m: e.matmul(ps_q[:, 0:n], ones, sq[:, 0:n], start=(m == 0), stop=(m == ntile - 1)),
                   r=["ln_sq", "ones"], w=["ps7"])
            act(lambda e: e.activation(out=mean[:, 0:n], in_=ps_s[:, 0:n], func=AF.Identity, scale=inv), r=["ps6"], w=["ln_mean"])
            dve(lambda e: e.tensor_tensor(out=t1[:, 0:n], in0=mean[:, 0:n], in1=mean[:, 0:n], op=ALU.mult), r=["ln_mean"], w=["ln_t1"])
            dve(lambda e: e.scalar_tensor_tensor(out=rstd[:, 0:n], in0=ps_q[:, 0:n], scalar=inv, in1=t1[:, 0:n], op0=ALU.mult, op1=ALU.subtract),
                r=["ps7", "ln_t1"], w=["ln_rstd"])
            act(lambda e: e.activation(out=rstd[:, 0:n], in_=rstd[:, 0:n], func=AF.Ln, bias=epsb), r=["ln_rstd", "consts"], w=["ln_rstd"])
            act(lambda e: e.activation(out=rstd[:, 0:n], in_=rstd[:, 0:n], func=AF.Exp, scale=-0.5), r=["ln_rstd"], w=["ln_rstd"])
            for m in range(ntile):
                dve(lambda e, m=m: e.tensor_tensor(out=t1[:, 0:n], in0=src[:, m, c0:c0 + n], in1=mean[:, 0:n], op=ALU.subtract),
                    r=[ksrc, "ln_mean"], w=["ln_t1"])
                dve(lambda e, m=m: e.tensor_tensor(out=t1[:, 0:n], in0=t1[:, 0:n], in1=rstd[:, 0:n], op=ALU.mult),
                    r=["ln_t1", "ln_rstd"], w=["ln_t1"])
                if dst_f32 is not None:
                    act(lambda e, m=m: e.activation(out=dst_f32[:, m, c0:c0 + n], in_=t1[:, 0:n], func=AF.Identity,
                                                    scale=PAR[:, goff + m:goff + m + 1], bias=PAR[:, boff + m:boff + m + 1]),
                        r=["ln_t1", "PAR"], w=[kdst])
                    act(lambda e, m=m: e.activation(out=dst_bf[:, m, c0:c0 + n], in_=t1[:, 0:n], func=AF.Identity,
                                                    scale=PAR[:, goff + m:goff + m + 1], bias=PAR[:, boff + m:boff + m + 1]),
                        r=["ln_t1", "PAR"], w=[kdstb])
                else:
                    act(lambda e, m=m: e.activation(out=dst_bf[:, m, c0:c0 + n], in_=t1[:, 0:n], func=func,
                                                    scale=PAR[:, goff + m:goff + m + 1], bias=PAR[:, boff + m:boff + m + 1]),
                        r=["ln_t1", "PAR"], w=[kdstb])

    def matmul_tiles(wt, kw, mi, nk, rhs, krhs, cols, m):
        banks = []
        for gi, (c0, n) in enumerate(cols):
            b = (m % 2) * 2 + gi
            banks.append(b)
            for k in range(nk):
                pe(lambda e, b=b, k=k, c0=c0, n=n: e.matmul(psb[b][:, 0:n], wt[:, k, mi * 128:(mi + 1) * 128], rhs[:, k, c0:c0 + n],
                                                            start=(k == 0), stop=(k == nk - 1)),
                   r=kw + [krhs], w=[kps[b]])
        return banks

    def pw_setup(tag, lr, li, ldt, F):
        kk = lambda s_: "%s_%s" % (tag, s_)
        c = dict(tag=tag, lr=lr, li=li, F=F, kk=kk)
        c["dt"] = A.f32([F]); c["th"] = A.f32([F]); c["ld"] = A.f32([F])
        c["t0"] = A.f32([F]); c["t1"] = A.f32([F]); c["t2"] = A.f32([F]); c["ti"] = A.f32([F], dt=I32)
        dt, th, ld = c["dt"], c["th"], c["ld"]
        act(lambda e: e.activation(out=dt, in_=ldt, func=AF.Exp), r=[kk("in")], w=[kk("dt")])
        dve(lambda e: e.tensor_tensor(out=th, in0=li, in1=dt, op=ALU.mult), r=[kk("in"), kk("dt")], w=[kk("th")])
        dve(lambda e: e.tensor_tensor(out=ld, in0=lr, in1=dt, op=ALU.mult), r=[kk("in"), kk("dt")], w=[kk("ld")])
        return c

    def pw_power(c, n, pr, pi_, kp):
        kk = c["kk"]
        th, ld, t0, t1, t2, ti = c["th"], c["ld"], c["t0"], c["t1"], c["t2"], c["ti"]
        dve(lambda e: e.tensor_scalar(out=t0, in0=th, scalar1=float(n), scalar2=None, op0=ALU.mult), r=[kk("th")], w=[kk("t0")])
        dve(lambda e: e.tensor_scalar(out=t1, in0=t0, scalar1=1.0 / (2 * PI), scalar2=None, op0=ALU.mult), r=[kk("t0")], w=[kk("t1")])
        dve(lambda e: e.tensor_copy(out=ti, in_=t1), r=[kk("t1")], w=[kk("ti")])
        dve(lambda e: e.tensor_copy(out=t1, in_=ti), r=[kk("ti")], w=[kk("t1")])
        dve(lambda e: e.scalar_tensor_tensor(out=t0, in0=t1, scalar=-2 * PI, in1=t0, op0=ALU.mult, op1=ALU.add), r=[kk("t1"), kk("t0")], w=[kk("t0")])
        dve(lambda e: e.tensor_scalar(out=t1, in0=t0, scalar1=PI, scalar2=-2 * PI, op0=ALU.is_gt, op1=ALU.mult), r=[kk("t0")], w=[kk("t1")])
        dve(lambda e: e.tensor_tensor(out=t0, in0=t0, in1=t1, op=ALU.add), r=[kk("t0"), kk("t1")], w=[kk("t0")])
        dve(lambda e: e.tensor_scalar(out=t1, in0=t0, scalar1=-PI, scalar2=2 * PI, op0=ALU.is_lt, op1=ALU.mult), r=[kk("t0")], w=[kk("t1")])
        dve(lambda e: e.tensor_tensor(out=t0, in0=t0, in1=t1, op=ALU.add), r=[kk("t0"), kk("t1")], w=[kk("t0")])
        dve(lambda e: e.tensor_scalar(out=t0, in0=t0, scalar1=-3.1415925, scalar2=3.1415925, op0=ALU.max, op1=ALU.min), r=[kk("t0")], w=[kk("t0")])
        act(lambda e: e.activation(out=t1, in_=t0, func=AF.Sin), r=[kk("t0")], w=[kk("t1")])
        dve(lambda e: e.scalar_tensor_tensor(out=t0, in0=t0, scalar=-1.0, in1=t0, op0=ALU.mult, op1=ALU.max), r=[kk("t0")], w=[kk("t0")])
        act(lambda e: e.activation(out=t0, in_=t0, func=AF.Sin, scale=-1.0, bias=halfpi), r=[kk("t0"), "consts"], w=[kk("t0")])
        act(lambda e: e.activation(out=t2, in_=ld, func=AF.Exp, scale=float(n)), r=[kk("ld")], w=[kk("t2")])
        dve(lambda e: e.tensor_tensor(out=pr, in0=t2, in1=t0, op=ALU.mult), r=[kk("t2"), kk("t0")], w=[kp])
        dve(lambda e: e.tensor_tensor(out=pi_, in0=t2, in1=t1, op=ALU.mult), r=[kk("t2"), kk("t1")], w=[kp])

    def pw_q(c, p1r, p1i, kp1, qr, qi, kq):
        kk = c["kk"]
        lr, li, t0, t1, t2 = c["lr"], c["li"], c["t0"], c["t1"], c["t2"]
        dve(lambda e: e.tensor_tensor(out=t0, in0=lr, in1=lr, op=ALU.mult), r=[kk("in")], w=[kk("t0")])
        dve(lambda e: e.tensor_tensor(out=t1, in0=li, in1=li, op=ALU.mult), r=[kk("in")], w=[kk("t1")])
        dve(lambda e: e.tensor_tensor(out=t0, in0=t0, in1=t1, op=ALU.add), r=[kk("t0"), kk("t1")], w=[kk("t0")])
        dve(lambda e: e.reciprocal(out=t0, in_=t0), r=[kk("t0")], w=[kk("t0")])
        dve(lambda e: e.tensor_scalar(out=t1, in0=p1r, scalar1=-1.0, scalar2=None, op0=ALU.add), r=[kp1], w=[kk("t1")])
        dve(lambda e: e.tensor_tensor(out=qr, in0=t1, in1=lr, op=ALU.mult), r=[kk("t1"), kk("in")], w=[kq])
        dve(lambda e: e.tensor_tensor(out=t2, in0=p1i, in1=li, op=ALU.mult), r=[kp1, kk("in")], w=[kk("t2")])
        dve(lambda e: e.tensor_tensor(out=qr, in0=qr, in1=t2, op=ALU.add), r=[kq, kk("t2")], w=[kq])
        dve(lambda e: e.tensor_tensor(out=qr, in0=qr, in1=t0, op=ALU.mult), r=[kq, kk("t0")], w=[kq])
        dve(lambda e: e.tensor_tensor(out=qi, in0=p1i, in1=lr, op=ALU.mult), r=[kp1, kk("in")], w=[kq])
        dve(lambda e: e.tensor_tensor(out=t2, in0=t1, in1=li, op=ALU.mult), r=[kk("t1"), kk("in")], w=[kk("t2")])
        dve(lambda e: e.tensor_tensor(out=qi, in0=qi, in1=t2, op=ALU.subtract), r=[kq, kk("t2")], w=[kq])
        dve(lambda e: e.tensor_tensor(out=qi, in0=qi, in1=t0, op=ALU.mult), r=[kq, kk("t0")], w=[kq])

    def s5_stage(l, p, NS, cols, u, g5, last):
        NSQ = NS // 4
        NC = NCHP + NSQ
        s5_mark = A.mark()
        PT = A.f32([1408])
        pt_off = [0]

        def ptake(shape):
            n_ = int(np.prod(shape))
            ap = PT[:, pt_off[0]:pt_off[0] + n_]
            pt_off[0] += n_
            return ap if len(shape) == 1 else ap.rearrange("p (a b) -> p a b", b=shape[1])
        PP = {}
        npi = {}
        Ta = {}
        Tb = {}
        pkeys = []
        for n in (1, 2, 3, 4):
            PP[n] = (ptake([32]), ptake([32]), "P_P%d" % n)
            npi[n] = ptake([32])
            pkeys += ["P_P%d" % n, "P_npi%d" % n]
        for n in (4, 8, 12, 16, 20, 24, 28, 32):
            Ta[n] = ptake([32, 2]); Tb[n] = ptake([32, 2])
            pkeys.append("P_T%d" % n)
        if p == 0:
            pm = A.mark()
            LP = A.f32([3, 32])
            S.op("sp", lambda e: e.dma_start(out=LP, in_=lamp[l]), writes=["P_in"], dma="P_in")
            cP = pw_setup("P", LP[:, 0], LP[:, 1], LP[:, 2], 32)
            tpr = A.f32([32]); tpi = A.f32([32])
            for n in (1, 2, 3, 4, 8, 12, 16, 20, 24, 28, 32):
                if n <= 4:
                    pr, pi_, kpn = PP[n]
                    t = npi[n]
                else:
                    pr, pi_, t = tpr, tpi, None
                    kpn = "P_Pt"
                pw_power(cP, n, pr, pi_, kpn)
                if t is not None:
                    dve(lambda e, t=t, pi_=pi_: e.tensor_scalar(out=t, in0=pi_, scalar1=-1.0, scalar2=None, op0=ALU.mult), r=[kpn], w=["P_npi%d" % n])
                if n >= 4:
                    ta, tb = Ta[n], Tb[n]
                    dve(lambda e, ta=ta, pr=pr: e.tensor_copy(out=ta[:, :, 0], in_=pr), r=[kpn], w=["P_T%d" % n])
                    dve(lambda e, ta=ta, pr=pr: e.tensor_copy(out=ta[:, :, 1], in_=pr), r=[kpn], w=["P_T%d" % n])
                    dve(lambda e, tb=tb, pi_=pi_: e.tensor_scalar(out=tb[:, :, 0], in0=pi_, scalar1=-1.0, scalar2=None, op0=ALU.mult), r=[kpn], w=["P_T%d" % n])
                    dve(lambda e, tb=tb, pi_=pi_: e.tensor_copy(out=tb[:, :, 1], in_=pi_), r=[kpn], w=["P_T%d" % n])
            S.op("sp", lambda e: e.dma_start(out=pt_scr[l], in_=PT), reads=pkeys, writes=[key("ptscr")], dma=key("o"))
        else:
            S.op("sp", lambda e: e.dma_start(out=PT, in_=pt_scr[l]), writes=pkeys, dma="PTld")
        CTb = A.bf16([2, 32, 32])
        S.op("pool", lambda e: e.dma_start(out=CTb, in_=ctp[l]), writes=["CTb"], dma="CTb")
        dve(lambda e: e.tensor_scalar(out=CTb[:, 1], in0=CTb[:, 1], scalar1=-1.0, scalar2=None, op0=ALU.mult), r=["CTb"], w=["CTb"])
        half_mark = A.mark()
        for hf in range(2):
            A.rewind(half_mark)
            barrier()
            XW = A.bf16([4, 2, 4, 128])
            fm = A.mark()
            if p == 0:
                LF = A.f32([3, 4, 128])
                BT = A.f32([2, 4, 128])
                S.op("sp", lambda e, hf=hf: e.dma_start(out=LF, in_=lamf[l][:, :, 4 * hf:4 * hf + 4, :]), writes=["F_in"], dma="F_in")
                S.op("sp", lambda e, hf=hf: e.dma_start(out=BT, in_=btf[l][:, :, 4 * hf:4 * hf + 4, :]), writes=["BT"], dma="BT")
                fl = lambda ap: ap.rearrange("p a b -> p (a b)")
                cF = pw_setup("F", fl(LF[:, 0]), fl(LF[:, 1]), fl(LF[:, 2]), 512)
                t0, t1, t2 = cF["t0"], cF["t1"], cF["t2"]
                btr, bti = fl(BT[:, 0]), fl(BT[:, 1])
                pr = A.f32([512]); pi_ = A.f32([512]); qr = A.f32([512]); qi = A.f32([512])
                vr = A.f32([512]); vi = A.f32([512])
                kq, kp, kv = "F_q", "F_P", "F_V"
                for n in range(4):
                    if n == 0:
                        pw_power(cF, 1, pr, pi_, kp)
                        pw_q(cF, pr, pi_, kp, qr, qi, kq)
                        ur, ui, ku = qr, qi, kq
                    else:
                        if n > 1:
                            pw_power(cF, n, pr, pi_, kp)
                        dve(lambda e: e.tensor_tensor(out=vr, in0=pr, in1=qr, op=ALU.mult), r=[kp, kq], w=[kv])
                        dve(lambda e: e.tensor_tensor(out=t0, in0=pi_, in1=qi, op=ALU.mult), r=[kp, kq], w=["F_t0"])
                        dve(lambda e: e.tensor_tensor(out=vr, in0=vr, in1=t0, op=ALU.subtract), r=[kv, "F_t0"], w=[kv])
                        dve(lambda e: e.tensor_tensor(out=vi, in0=pr, in1=qi, op=ALU.mult), r=[kp, kq], w=[kv])
                        dve(lambda e: e.tensor_tensor(out=t0, in0=pi_, in1=qr, op=ALU.mult), r=[kp, kq], w=["F_t0"])
                        dve(lambda e: e.tensor_tensor(out=vi, in0=vi, in1=t0, op=ALU.add), r=[kv, "F_t0"], w=[kv])
                        ur, ui, ku = vr, vi, kv
                    xr = fl(XW[:, n, 0]); xi = fl(XW[:, n, 1])
                    dve(lambda e, ur=ur: e.tensor_tensor(out=t1, in0=ur, in1=btr, op=ALU.mult), r=[ku, "BT"], w=["F_t1"])
                    dve(lambda e, ui=ui: e.tensor_tensor(out=t2, in0=ui, in1=bti, op=ALU.mult), r=[ku, "BT"], w=["F_t2"])
                    dve(lambda e, xr=xr: e.tensor_tensor(out=xr, in0=t1, in1=t2, op=ALU.subtract), r=["F_t1", "F_t2"], w=["XW"])
                    dve(lambda e, ur=ur: e.tensor_tensor(out=t1, in0=ur, in1=bti, op=ALU.mult), r=[ku, "BT"], w=["F_t1"])
                    dve(lambda e, ui=ui: e.tensor_tensor(out=t2, in0=ui, in1=btr, op=ALU.mult), r=[ku, "BT"], w=["F_t2"])
                    dve(lambda e, xi=xi: e.tensor_tensor(out=xi, in0=t1, in1=t2, op=ALU.add), r=["F_t1", "F_t2"], w=["XW"])
                S.op("sp", lambda e, hf=hf: e.dma_start(out=xw_scr[l, hf], in_=XW.rearrange("p a b c d -> p (a b c d)")), reads=["XW"], writes=[key("xwscr")], dma=key("o"))
            else:
                S.op("sp", lambda e, hf=hf: e.dma_start(out=XW.rearrange("p a b c d -> p (a b c d)"), in_=xw_scr[l, hf]), writes=["XW"], dma="XWld")
            A.rewind(fm)
            SP = A.f32([16, 2, NCHP + 1])
            XS = A.f32([16, 2, NSEQ])
            H0 = A.f32([16, 2, NSEQ])
            SO = A.f32([16, 2, NSEQ])
            CB = A.f32([16, 2, 17])
            st1 = A.f32([16, 2, 16]); st2 = A.f32([16, 2, 16])
            Hf = [A.bf16([4, 2, NCHP + NSEQ]) for _ in range(2)]
            tmpc = A.f32([3, 2, NCHP + NSEQ])
            sws = [A.f32([2, NCHP + NSEQ]) for _ in range(2)]
            ysb = sig
            dve(lambda e: e.memset(SP[:, :, :, 0:1], 0.0), r=["XW", "F_t1", "F_t2", "F_t0", "F_in", "BT", "F_q", "F_V", "F_P", "F_th", "F_ld", "F_dt", "F_ti"],
                w=["SP", "XS", "H0", "SO", "CB", "st1", "st2", "Hf0", "Hf1", "sw0", "sw1", "tmpc0", "tmpc1", "tmpc2"])
            dve(lambda e, hf=hf: e.tensor_copy(out=SP[:, :, :, 0], in_=scar[:, l, 16 * hf:16 * hf + 16, :]), r=["scar"], w=["SP"])
            if NS:
                S.op("sp", lambda e, hf=hf: e.dma_start(out=H0, in_=sin_[l][:, 16 * hf:16 * hf + 16]), writes=["H0"], dma="H0")
            p4r, p4i, kp4 = PP[4]
            sl = slice(16 * hf, 16 * hf + 16)
            for ii in range(16):
                i = 16 * hf + ii
                tl, ip = ii // 4, ii % 4
                t = i // 4
                rows = slice(32 * ip, 32 * ip + 32)
                for ri in range(2):
                    b = (ii * 2 + ri) % 2
                    for s in range(4):
                        pe(lambda e, b=b, s=s, ri=ri, tl=tl, t=t, rows=rows, ip=ip: e.matmul(
                            psb[b][:, 0:NC], XW[rows, 3 - s, ri, tl, :], u[rows, t, s:4 * NC:4],
                            start=(s == 0), stop=(s == 3), tile_position=(32 * ip, 0)),
                           r=["XW", "u"], w=[kps[b]])
                    act(lambda e, b=b, ii=ii, ri=ri: e.activation(out=SP[:, ii, ri, 1:NCHP + 1], in_=psb[b][:, 0:NCHP], func=AF.Copy),
                        r=[kps[b]], w=["SP"])
                    if NS:
                        act(lambda e, b=b, ii=ii, ri=ri: e.activation(out=XS[:, ii, ri, 0:NSQ], in_=psb[b][:, NCHP:NC], func=AF.Copy),
                            r=[kps[b]], w=["XS"])
            SPb = SP[:, :, :, 1:NCHP + 1].rearrange("p a r (b k) -> p a r b k", k=8)
            tab = {n_: Ta[n_][:, sl, :].unsqueeze(3).to_broadcast([128, 16, 2, 16]) for n_ in Ta}
            tbb = {n_: Tb[n_][:, sl, :].unsqueeze(3).to_broadcast([128, 16, 2, 16]) for n_ in Tb}
            for k in range(1, 8):
                dve(lambda e, k=k: e.tensor_tensor(out=st1, in0=SPb[:, :, :, :, k - 1], in1=tab[4], op=ALU.mult), r=["SP", "P_T4"], w=["st1"])
                dve(lambda e, k=k: e.tensor_tensor(out=st2, in0=SPb[:, :, ::-1, :, k - 1], in1=tbb[4], op=ALU.mult), r=["SP", "P_T4"], w=["st2"])
                dve(lambda e, k=k: e.tensor_tensor(out=SPb[:, :, :, :, k], in0=SPb[:, :, :, :, k], in1=st1, op=ALU.add), r=["SP", "st1"], w=["SP"])
                dve(lambda e, k=k: e.tensor_tensor(out=SPb[:, :, :, :, k], in0=SPb[:, :, :, :, k], in1=st2, op=ALU.add), r=["SP", "st2"], w=["SP"])
            dve(lambda e: e.tensor_copy(out=CB[:, :, :, 0], in_=SP[:, :, :, 0]), r=["SP"], w=["CB"])
            for b_ in range(16):
                dve(lambda e, b_=b_: e.tensor_tensor(out=st1[:, :, :, 0], in0=CB[:, :, :, b_], in1=Ta[32][:, sl, :], op=ALU.mult), r=["CB", "P_T32"], w=["st1"])
                dve(lambda e, b_=b_: e.tensor_tensor(out=st2[:, :, :, 0], in0=CB[:, :, ::-1, b_], in1=Tb[32][:, sl, :], op=ALU.mult), r=["CB", "P_T32"], w=["st2"])
                dve(lambda e, b_=b_: e.tensor_tensor(out=st1[:, :, :, 0], in0=st1[:, :, :, 0], in1=st2[:, :, :, 0], op=ALU.add), r=["st1", "st2"], w=["st1"])
                dve(lambda e, b_=b_: e.tensor_tensor(out=CB[:, :, :, b_ + 1], in0=SPb[:, :, :, b_, 7], in1=st1[:, :, :, 0], op=ALU.add), r=["SP", "st1", "CB"], w=["CB"])
            for k in range(8):
                n_ = 4 * (k + 1)
                dve(lambda e, k=k, n_=n_: e.tensor_tensor(out=st1, in0=CB[:, :, :, 0:16], in1=tab[n_], op=ALU.mult), r=["CB", "P_T%d" % n_], w=["st1"])
                dve(lambda e, k=k, n_=n_: e.tensor_tensor(out=st2, in0=CB[:, :, ::-1, 0:16], in1=tbb[n_], op=ALU.mult), r=["CB", "P_T%d" % n_], w=["st2"])
                dve(lambda e, k=k: e.tensor_tensor(out=SPb[:, :, :, :, k], in0=SPb[:, :, :, :, k], in1=st1, op=ALU.add), r=["SP", "st1"], w=["SP"])
                dve(lambda e, k=k: e.tensor_tensor(out=SPb[:, :, :, :, k], in0=SPb[:, :, :, :, k], in1=st2, op=ALU.add), r=["SP", "st2"], w=["SP"])
            dve(lambda e, hf=hf: e.tensor_copy(out=scar[:, l, 16 * hf:16 * hf + 16, :], in_=SP[:, :, :, NCHP]), r=["SP"], w=["scar"])
            if NS:
                for ri in range(2):
                    for ii in range(16):
                        i = 16 * hf + ii
                        dve(lambda e, ii=ii, i=i, ri=ri: e.scalar_tensor_tensor(out=SO[:, ii, ri, :], in0=H0[:, ii, ri, :], scalar=p4r[:, i:i + 1],
                                                                               in1=XS[:, ii, ri, :], op0=ALU.mult, op1=ALU.add),
                            r=["H0", "XS", kp4], w=["SO"])
                        sc = npi[4] if ri == 0 else p4i
                        dve(lambda e, ii=ii, i=i, ri=ri, sc=sc: e.scalar_tensor_tensor(out=SO[:, ii, ri, :], in0=H0[:, ii, 1 - ri, :], scalar=sc[:, i:i + 1],
                                                                                      in1=SO[:, ii, ri, :], op0=ALU.mult, op1=ALU.add),
                            r=["H0", kp4, "P_npi4", "SO"], w=["SO"])
                S.op("sp", lambda e, hf=hf: e.dma_start(out=o_s5s[l][:, 16 * hf:16 * hf + 16], in_=SO), reads=["SO"], writes=[key("o_s5s")], dma=key("o"))
            def pair_ctx(ii):
                tl, ip = ii // 4, ii % 4
                return dict(ii=ii, tl=tl, ip=ip, t=4 * hf + tl, i=16 * hf + ii, rows=slice(32 * ip, 32 * ip + 32),
                            hb=Hf[ii % 2], khf="Hf%d" % (ii % 2), bk0=2 * (ii % 2), ybase=4 + (tl % 2) * 2)

            def emit_hloc(c_):
                ii, tl, ip, t, i, rows, hb, khf, bk0, ybase = (c_[k_] for k_ in ("ii", "tl", "ip", "t", "i", "rows", "hb", "khf", "bk0", "ybase"))
                for ri in range(2):
                    b = bk0 + ri
                    for j in range(3):
                        g0 = j * 144
                        for s_ in range(j + 1):
                            pe(lambda e, b=b, s_=s_, j=j, ri=ri, tl=tl, t=t, rows=rows, ip=ip, g0=g0: e.matmul(
                                psb[b][:, g0:g0 + NC], XW[rows, j - s_, ri, tl, :], u[rows, t, s_:4 * NC:4],
                                start=(s_ == 0), stop=(s_ == j), tile_position=(32 * ip, 0)),
                               r=["XW", "u"], w=[kps[b]])

            def emit_post(c_):
                ii, tl, ip, t, i, rows, hb, khf, bk0, ybase = (c_[k_] for k_ in ("ii", "tl", "ip", "t", "i", "rows", "hb", "khf", "bk0", "ybase"))
                sw = sws[ii % 2]
                ksw = "sw%d" % (ii % 2)
                act(lambda e, ii=ii, sw=sw: e.activation(out=sw[:, 0, 0:NCHP], in_=SP[:, ii, 1, 0:NCHP], func=AF.Identity, scale=-1.0), r=["SP"], w=[ksw])
                act(lambda e, ii=ii, sw=sw: e.activation(out=sw[:, 1, 0:NCHP], in_=SP[:, ii, 0, 0:NCHP], func=AF.Copy), r=["SP"], w=[ksw])
                if NS:
                    act(lambda e, ii=ii, sw=sw: e.activation(out=sw[:, 0, NCHP:NC], in_=H0[:, ii, 1, :], func=AF.Identity, scale=-1.0), r=["H0"], w=[ksw])
                    act(lambda e, ii=ii, sw=sw: e.activation(out=sw[:, 1, NCHP:NC], in_=H0[:, ii, 0, :], func=AF.Copy), r=["H0"], w=[ksw])
                pbank = ps_all[:, bk0 * 512:(bk0 + 2) * 512].rearrange("p (r c) -> p r c", r=2)
                for phase_ in range(2):
                    for j in range(3):
                        pjr, pji, kpj = PP[j + 1]
                        g0 = j * 144
                        groups = [(0, NCHP, SP[:, ii, :, 0:NCHP], "SP")]
                        if NS:
                            groups.append((NCHP, NSQ, H0[:, ii, :, :], "H0"))
                        for (c0, n, sa, ks) in groups:
                            if phase_ == 0:
                                dve(lambda e, c0=c0, n=n, sa=sa, i=i, pjr=pjr, j=j, g0=g0, pbank=pbank: e.scalar_tensor_tensor(
                                    out=tmpc[:, j, :, c0:c0 + n], in0=sa, scalar=pjr[:, i:i + 1], in1=pbank[:, :, g0 + c0:g0 + c0 + n], op0=ALU.mult, op1=ALU.add),
                                    r=[ks, kps[bk0], kps[bk0 + 1], kpj], w=["tmpc%d" % j])
                            else:
                                dve(lambda e, c0=c0, n=n, sw=sw, i=i, pji=pji, hb=hb, j=j: e.scalar_tensor_tensor(
                                    out=hb[:, j, :, c0:c0 + n], in0=sw[:, :, c0:c0 + n], scalar=pji[:, i:i + 1], in1=tmpc[:, j, :, c0:c0 + n], op0=ALU.mult, op1=ALU.add),
                                    r=[ksw, "tmpc%d" % j, kpj], w=[khf])
                for ri in range(2):
                    act(lambda e, ii=ii, ri=ri, hb=hb: e.activation(out=hb[:, 3, ri, 0:NCHP], in_=SP[:, ii, ri, 1:NCHP + 1], func=AF.Copy), r=["SP"], w=[khf])
                    if NS:
                        act(lambda e, ii=ii, ri=ri, hb=hb: e.activation(out=hb[:, 3, ri, NCHP:NC], in_=SO[:, ii, ri, :], func=AF.Copy), r=["SO"], w=[khf])

            def emit_y(c_):
                ii, tl, ip, t, i, rows, hb, khf, bk0, ybase = (c_[k_] for k_ in ("ii", "tl", "ip", "t", "i", "rows", "hb", "khf", "bk0", "ybase"))
                for j in range(4):
                    for ri in range(2):
                        pe(lambda e, ip=ip, j=j, ri=ri, i=i, hb=hb, ybase=ybase: e.matmul(
                            psb[ybase][32 * ip:32 * ip + 32, j:NPT:4], CTb[:, ri, i, :], hb[:, j, ri, 0:NCHP],
                            start=(ri == 0), stop=(ri == 1), tile_position=(0, 32 * ip)),
                           r=["CTb", khf], w=[kps[ybase]])
                        if NS:
                            pe(lambda e, ip=ip, j=j, ri=ri, i=i, hb=hb, ybase=ybase: e.matmul(
                                psb[ybase + 1][32 * ip:32 * ip + 32, j:NS:4], CTb[:, ri, i, :], hb[:, j, ri, NCHP:NC],
                                start=(ri == 0), stop=(ri == 1), tile_position=(0, 32 * ip)),
                               r=["CTb", khf], w=[kps[ybase + 1]])

            def emit_evac(tl):
                t = 4 * hf + tl
                ybase = 4 + (tl % 2) * 2
                for gi, (c0, n) in enumerate(cols):
                    dve(lambda e, t=t, c0=c0, n=n, b=ybase + gi: e.scalar_tensor_tensor(
                        out=ysb[:, c0:c0 + n], in0=u[:, t, c0:c0 + n], scalar=PAR[:, O_D + t:O_D + t + 1], in1=psb[b][:, 0:n], op0=ALU.mult, op1=ALU.add),
                        r=["u", "PAR", kps[ybase + gi]], w=["sig"])
                    act(lambda e, t=t, c0=c0, n=n: e.activation(out=g5[:, t, c0:c0 + n], in_=ysb[:, c0:c0 + n], func=AF.Gelu_apprx_tanh),
                        r=["sig"], w=["g5"])

            ctxs = [pair_ctx(ii) for ii in range(16)]
            emit_hloc(ctxs[0])
            for ii in range(16):
                if ii + 1 < 16:
                    emit_hloc(ctxs[ii + 1])
                emit_post(ctxs[ii])
                emit_y(ctxs[ii])
                if ii % 4 == 3:
                    emit_evac(ii // 4)
        if last:
            S.op("sp", lambda e: e.dma_start(out=o_s5p[l], in_=scar[:, l]), reads=["scar"], writes=[key("o_s5p")], dma=key("o"))
        A.rewind(s5_mark)

    for p in range(npass):
        last = (p == npass - 1)
        NS = NSMP if last else 0
        NT = NPT + NS
        cols = [(0, NPT)] + ([(NPT, NS)] if NS else [])
        S.op("sp", lambda e, p=p: e.dma_start(out=X, in_=xin[p]), writes=[kX], dma="X")
        dve(lambda e: e.tensor_copy(out=xb, in_=X), r=[kX], w=[kxb])
        for l in range(depth):
            S.op("sp", lambda e, l=l: e.dma_start(out=PAR, in_=par[l]), writes=["PAR"], dma="PAR")
            A.rewind(base_mark)
            u = A.bf16([8, NTMAX])
            g5 = A.bf16([8, NTMAX])
            for ci in range(2):
                wt, kw = load_w(w_in[l][:, ci * 512:(ci + 1) * 512], KT, 512)
                for mi in range(4):
                    m = ci * 4 + mi
                    banks = matmul_tiles(wt, kw, mi, KT, xb, kxb, cols, m)
                    for gi, (c0, n) in enumerate(cols):
                        b = banks[gi]
                        act(lambda e, b=b, m=m, c0=c0, n=n: e.activation(out=u[:, m, c0:c0 + n], in_=psb[b][:, 0:n], func=AF.Copy),
                            r=[kps[b]], w=["u"])
            s5_stage(l, p, NS, cols, u, g5, last)
            if debug and last and l == 0:
                S.op("pool", lambda e: e.dma_start(out=dbg_g5, in_=g5), reads=["g5"], writes=[key("o_dbg")], dma=key("o"))
                S.op("pool", lambda e: e.dma_start(out=dbg_u, in_=u), reads=["u"], writes=[key("o_dbg")], dma=key("o"))
            for ci in range(2):
                wt, kw = load_w(w_glu[l][:, ci * 512:(ci + 1) * 512], 8, 512)
                for mi in range(4):
                    m = ci * 4 + mi
                    banks = matmul_tiles(wt, kw, mi, 8, g5, "g5", cols, m)
                    for gi, (c0, n) in enumerate(cols):
                        b = banks[gi]
                        act(lambda e, b=b, c0=c0, n=n: e.activation(out=sig[:, c0:c0 + n], in_=psb[b][:, 0:n], func=AF.Sigmoid),
                            r=[kps[b]], w=["sig"])
                        dve(lambda e, m=m, c0=c0, n=n: e.tensor_tensor(out=mixcat[:, m, c0:c0 + n], in0=g5[:, m, c0:c0 + n], in1=sig[:, c0:c0 + n], op=ALU.mult),
                            r=["g5", "sig"], w=["mixcat"])
            barrier()

            A.rewind(base_mark)
            cv = A.f32([8, NTMAX])
            convy = A.f32([8, NTMAX])
            cbp = A.bf16([8, 30 + NPT])
            cbs = A.bf16([8, NSEQ, 34])
            csn = A.f32([8, NSEQ, 4])
            dg = [A.bf16([31, 128]) for _ in range(2)]
            dve(lambda e, l=l: e.tensor_copy(out=cbp[:, :, 0:30], in_=chist[:, l]), r=["chist"], w=["cbp"])
            if NS:
                for hh in range(2):
                    S.op("pool", lambda e, l=l, hh=hh: e.dma_start(out=cbs[:, 4 * hh:4 * hh + 4, :, 0:30], in_=ccf[l][:, 4 * hh:4 * hh + 4]),
                         writes=["cbs"], dma="cbs%d" % hh)
            for ci in range(2, 6):
                wt, kw = load_w(w_in[l][:, ci * 512:(ci + 1) * 512], KT, 512)
                for mi in range(4):
                    m = ci * 4 + mi
                    banks = matmul_tiles(wt, kw, mi, KT, xb, kxb, cols, m)
                    for gi, (c0, n) in enumerate(cols):
                        b = banks[gi]
                        if m < 16:
                            act(lambda e, b=b, m=m, c0=c0, n=n: e.activation(out=cv[:, m - 8, c0:c0 + n], in_=psb[b][:, 0:n], func=AF.Copy),
                                r=[kps[b]], w=["cv"])
                        else:
                            mm = m - 16
                            act(lambda e, b=b, c0=c0, n=n: e.activation(out=sig[:, c0:c0 + n], in_=psb[b][:, 0:n], func=AF.Sigmoid),
                                r=[kps[b]], w=["sig"])
                            if gi == 0:
                                dve(lambda e, mm=mm: e.tensor_tensor(out=cbp[:, mm, 30:30 + NPT], in0=cv[:, mm, 0:NPT], in1=sig[:, 0:NPT], op=ALU.mult),
                                    r=["cv", "sig"], w=["cbp"])
                                dve(lambda e, mm=mm, l=l: e.tensor_tensor(out=chist[:, l, mm, :], in0=cv[:, mm, NPT - 30:NPT], in1=sig[:, NPT - 30:NPT], op=ALU.mult),
                                    r=["cv", "sig", "cbp"], w=["chist"])
                            else:
                                dve(lambda e, mm=mm: e.tensor_tensor(out=csn[:, mm],
                                                                     in0=cv[:, mm, NPT:NPT + NS].rearrange("p (s t) -> p s t", t=4),
                                                                     in1=sig[:, NPT:NPT + NS].rearrange("p (s t) -> p s t", t=4), op=ALU.mult),
                                    r=["cv", "sig"], w=["csn"])
                                dve(lambda e, mm=mm: e.tensor_copy(out=cbs[:, mm, :, 30:34], in_=csn[:, mm]), r=["csn"], w=["cbs"])
            if last:
                S.op("sp", lambda e, l=l: e.dma_start(out=o_cvp[l], in_=chist[:, l]), reads=["chist"], writes=[key("o_cvp")], dma=key("o"))
                S.op("sp", lambda e, l=l: e.dma_start(out=o_cvsn[l], in_=csn), reads=["csn"], writes=[key("o_cvsn")], dma=key("o"))
                S.op("sp", lambda e, l=l: e.dma_start(out=o_cvso[l], in_=ccr[l][:, 4:30, :]), writes=[key("o_cvso")], dma=key("o"))
            for m in range(8):
                d_ = dg[m % 2]
                kd = "dg%d" % (m % 2)
                for k in range(31):
                    if k % 2 == 0:
                        act(lambda e, d_=d_, m=m, k=k: e.activation(out=d_[:, k, :], in_=ident, func=AF.Identity,
                                                                    scale=PAR[:, O_CW + m * 31 + k:O_CW + m * 31 + k + 1]),
                            r=["ident", "PAR"], w=[kd + "a"])
                    else:
                        dve(lambda e, d_=d_, m=m, k=k: e.tensor_scalar(out=d_[:, k, :], in0=ident, scalar1=PAR[:, O_CW + m * 31 + k:O_CW + m * 31 + k + 1],
                                                                       scalar2=None, op0=ALU.mult),
                            r=["ident", "PAR"], w=[kd + "b"])
                for gi, (c0, n) in enumerate(cols):
                    b = (m % 2) * 2 + gi
                    for k in range(31):
                        if gi == 0:
                            pe(lambda e, b=b, k=k, m=m, d_=d_: e.matmul(psb[b][:, 0:NPT], d_[:, k, :], cbp[:, m, k:k + NPT], start=(k == 0), stop=(k == 30)),
                               r=[kd + "a", kd + "b", "cbp"], w=[kps[b]])
                        else:
                            pe(lambda e, b=b, k=k, m=m, d_=d_: e.matmul(psb[b][:, 0:NS].rearrange("p (s t) -> p s t", t=4), d_[:, k, :], cbs[:, m, :, k:k + 4],
                                                                        start=(k == 0), stop=(k == 30)),
                               r=[kd + "a", kd + "b", "cbs"], w=[kps[b]])
                    act(lambda e, b=b, m=m, c0=c0, n=n: e.activation(out=convy[:, m, c0:c0 + n], in_=psb[b][:, 0:n], func=AF.Identity,
                                                                    bias=PAR[:, O_CB + m:O_CB + m + 1]),
                        r=[kps[b], "PAR"], w=["convy"])
            layernorm(convy, "convy", cols, O_CLG, O_CLB, 8, None, None, mixcat[:, 8:16, :], "mixcat", func=AF.Silu)

            def proj_residual(wsrc, nk, rhs, krhs):
                for ci in range(4):
                    wt, kw = load_w(wsrc[:, ci * 512:(ci + 1) * 512], nk, 512)
                    for mi in range(4):
                        m = ci * 4 + mi
                        banks = matmul_tiles(wt, kw, mi, nk, rhs, krhs, cols, m)
                        for gi, (c0, n) in enumerate(cols):
                            b = banks[gi]
                            dve(lambda e, b=b, m=m, c0=c0, n=n: e.scalar_tensor_tensor(out=X[:, m, c0:c0 + n], in0=X[:, m, c0:c0 + n], scalar=ALPHA,
                                                                                      in1=psb[b][:, 0:n], op0=ALU.mult, op1=ALU.add),
                                r=[kX, kps[b]], w=[kX])
            if debug and last and l == 0:
                S.op("pool", lambda e: e.dma_start(out=dbg_mix, in_=mixcat), reads=["mixcat"], writes=[key("o_dbg")], dma=key("o"))
            proj_residual(w_out[l], KT, mixcat, "mixcat")
            layernorm(X, kX, cols, O_LN1G, O_LN1B, KT, X, kX, xb, kxb)
            if debug and last and l == 0:
                S.op("sp", lambda e: e.dma_start(out=dbg_x1, in_=X), reads=[kX], writes=[key("o_dbg")], dma=key("o"))
            barrier()

            A.rewind(base0_mark)
            actb = A.bf16([FT, NTMAX])
            extp = [A.f32([2 + NPT]) for _ in range(2)]
            exts = [A.f32([NSEQ, 6]) for _ in range(2)]
            hy = [A.f32([NTMAX]) for _ in range(2)]
            cfs = [A.f32([NSEQ, 2]) for _ in range(2)]
            ffs_out = A.f32([88, NSEQ, 2])
            pb = A.bf16([2, NTMAX])
            wpe = A.bf16([2, D])
            wv = w_up[l].rearrange("(k p) c -> p k c", p=128)
            for j in range(FT):
                jl = j % 2
                if jl == 0:
                    s = ring_i[0] % 2
                    ring_i[0] += 1
                    wt = wring[s]
                    kwa, kwb = "wr%da" % s, "wr%db" % s
                    S.op("pool", lambda e, wt=wt, j=j: e.dma_start(out=wt[:, :, 0:256], in_=wv[:, :, j * 128:(j + 2) * 128]), writes=[kwa], dma=kwa, nobar=True)
                    S.op("pool", lambda e, wt=wt, j=j: e.dma_start(out=wt[:, :, 256:512], in_=wv[:, :, DFF + j * 128:DFF + (j + 2) * 128]), writes=[kwb], dma=kwb, nobar=True)
                for h in range(2):
                    ft = h * FT + j
                    kwh = [kwa, kwb][h]
                    for gi, (c0, n) in enumerate(cols):
                        b = h * 2 + gi
                        for k in range(KT):
                            pe(lambda e, b=b, k=k, h=h, c0=c0, n=n, wt=wt, jl=jl: e.matmul(psb[b][:, 0:n], wt[:, k, h * 256 + jl * 128:h * 256 + (jl + 1) * 128], xb[:, k, c0:c0 + n],
                                                                                     start=(k == 0), stop=(k == KT - 1)),
                               r=[kwh, kxb], w=[kps[b]])
                    wf = lambda k, ft=ft: PAR[:, O_FCW + ft * 3 + k:O_FCW + ft * 3 + k + 1]
                    bf = PAR[:, O_FCB + ft:O_FCB + ft + 1]
                    ke = "extp%d" % h
                    kh = "hy%d" % h
                    dve(lambda e, h=h, ft=ft, l=l: e.tensor_copy(out=extp[h][:, 0:2], in_=fhist[:, l, ft, :]), r=["fhist"], w=[ke])
                    act(lambda e, h=h: e.activation(out=extp[h][:, 2:2 + NPT], in_=psb[h * 2][:, 0:NPT], func=AF.Copy), r=[kps[h * 2]], w=[ke])
                    dve(lambda e, h=h, ft=ft, l=l: e.tensor_copy(out=fhist[:, l, ft, :], in_=extp[h][:, NPT:NPT + 2]), r=[ke], w=["fhist"])
                    dve(lambda e, h=h, wf=wf, bf=bf: e.tensor_scalar(out=hy[h][:, 0:NPT], in0=extp[h][:, 2:2 + NPT], scalar1=wf(2), scalar2=bf, op0=ALU.mult, op1=ALU.add),
                        r=[ke, "PAR"], w=[kh])
                    for k in range(2):
                        dve(lambda e, h=h, k=k, wf=wf: e.scalar_tensor_tensor(out=hy[h][:, 0:NPT], in0=extp[h][:, k:k + NPT], scalar=wf(k), in1=hy[h][:, 0:NPT],
                                                                             op0=ALU.mult, op1=ALU.add),
                            r=[ke, "PAR", kh], w=[kh])
                    if NS:
                        kes = "exts%d" % h
                        kcf = "cfs%d" % h
                        hys = hy[h][:, NPT:NPT + NS].rearrange("p (s t) -> p s t", t=4)
                        S.op("sp", lambda e, h=h, ft=ft, l=l: e.dma_start(out=cfs[h], in_=cff[l][:, ft]), writes=[kcf], dma=kcf)
                        dve(lambda e, h=h: e.tensor_copy(out=exts[h][:, :, 0:2], in_=cfs[h]), r=[kcf], w=[kes])
                        act(lambda e, h=h: e.activation(out=exts[h][:, :, 2:6], in_=psb[h * 2 + 1][:, 0:NS].rearrange("p (s t) -> p s t", t=4), func=AF.Copy),
                            r=[kps[h * 2 + 1]], w=[kes])
                        dve(lambda e, h=h, ft=ft: e.tensor_copy(out=ffs_out[:, ft], in_=exts[h][:, :, 4:6]), r=[kes], w=["ffs_out"])
                        dve(lambda e, h=h, wf=wf, bf=bf, hys=hys: e.tensor_scalar(out=hys, in0=exts[h][:, :, 2:6], scalar1=wf(2), scalar2=bf, op0=ALU.mult, op1=ALU.add),
                            r=[kes, "PAR"], w=[kh])
                        for k in range(2):
                            dve(lambda e, h=h, k=k, wf=wf, hys=hys: e.scalar_tensor_tensor(out=hys, in0=exts[h][:, :, k:k + 4], scalar=wf(k), in1=hys,
                                                                                          op0=ALU.mult, op1=ALU.add),
                                r=[kes, "PAR", kh], w=[kh])
                act(lambda e: e.activation(out=hy[0][:, 0:NT], in_=hy[0][:, 0:NT], func=AF.Silu), r=["hy0"], w=["hy0"])
                dve(lambda e, j=j: e.tensor_tensor(out=actb[:, j, 0:NT], in0=hy[0][:, 0:NT], in1=hy[1][:, 0:NT], op=ALU.mult), r=["hy0", "hy1"], w=["actb"])
            if last:
                S.op("sp", lambda e, l=l: e.dma_start(out=o_ffp[l], in_=fhist[:, l]), reads=["fhist"], writes=[key("o_ffp")], dma=key("o"))
                S.op("sp", lambda e, l=l: e.dma_start(out=o_ffs[l], in_=ffs_out), reads=["ffs_out"], writes=[key("o_ffs")], dma=key("o"))
            wdv = w_down[l].rearrange("(k p) c -> p k c", p=128)
            for m in range(KT):
                s_ = ring_i[0] % 2
                ring_i[0] += 1
                wd = wring[s_].rearrange("p k c -> p (k c)")[:, 0:FT * 128].rearrange("p (k c) -> p k c", k=FT)
                kw = ["wr%da" % s_, "wr%db" % s_]
                S.op("pool", lambda e, wd=wd, m=m: e.dma_start(out=wd, in_=wdv[:, :, m * 128:(m + 1) * 128]), writes=kw, dma=kw[0], nobar=True)
                for gi, (c0, n) in enumerate(cols):
                    b = (m % 2) * 2 + gi
                    for k in range(FT):
                        pe(lambda e, b=b, k=k, wd=wd, c0=c0, n=n: e.matmul(psb[b][:, 0:n], wd[:, k, :], actb[:, k, c0:c0 + n], start=(k == 0), stop=(k == FT - 1)),
                           r=kw + ["actb"], w=[kps[b]])
                    dve(lambda e, b=b, m=m, c0=c0, n=n: e.scalar_tensor_tensor(out=X[:, m, c0:c0 + n], in0=X[:, m, c0:c0 + n], scalar=ALPHA,
                                                                              in1=psb[b][:, 0:n], op0=ALU.mult, op1=ALU.add),
                        r=[kX, kps[b]], w=[kX])
            layernorm(X, kX, cols, O_LN2G, O_LN2B, KT, X, kX, xb, kxb)

            S.op("pool", lambda e, l=l, p=p: e.dma_start(out=pb, in_=pin[l, p]), writes=["pb"], dma="pb")
            S.op("pool", lambda e, l=l: e.dma_start(out=wpe, in_=w_pe[l].rearrange("(k p) c -> p k c", p=128)), writes=["wpe"], dma="wpe")
            for ci in range(4):
                wt, kw = load_w(w_gate[l][:, ci * 512:(ci + 1) * 512], KT, 512)
                for mi in range(4):
                    m = ci * 4 + mi
                    banks = matmul_tiles(wt, kw, mi, KT, xb, kxb, cols, m)
                    for gi, (c0, n) in enumerate(cols):
                        b = banks[gi]
                        be = 4 + gi
                        for k in range(2):
                            pe(lambda e, be=be, k=k, m=m, c0=c0, n=n: e.matmul(psb[be][:, 0:n], wpe[:, k, m * 128:(m + 1) * 128], pb[:, k, c0:c0 + n],
                                                                                 start=(k == 0), stop=(k == 1)),
                               r=["wpe", "pb"], w=[kps[be]])
                        act(lambda e, b=b, c0=c0, n=n: e.activation(out=sig[:, c0:c0 + n], in_=psb[b][:, 0:n], func=AF.Sigmoid), r=[kps[b]], w=["sig"])
                        dve(lambda e, be=be, c0=c0, n=n: e.tensor_tensor(out=sig[:, c0:c0 + n], in0=sig[:, c0:c0 + n], in1=psb[be][:, 0:n], op=ALU.mult),
                            r=["sig", kps[be]], w=["sig"])
                        dve(lambda e, m=m, c0=c0, n=n: e.scalar_tensor_tensor(out=X[:, m, c0:c0 + n], in0=X[:, m, c0:c0 + n], scalar=ALPHA,
                                                                             in1=sig[:, c0:c0 + n], op0=ALU.mult, op1=ALU.add),
                            r=[kX, "sig"], w=[kX])
            layernorm(X, kX, cols, O_LN3G, O_LN3B, KT, X, kX, xb, kxb)
            barrier()
        S.op("sp", lambda e, p=p: e.dma_start(out=o_y[p], in_=X), reads=[kX], writes=[key("o_y")], dma="o_y")
    S.barrier(lambda e: e.memset(dummy, 0.0))
    S.emit(nc, st)
    st.close()
    return nc, S, A


def make_core_inputs(inp, b, s0, depth=DEPTH, npass=NPASS):
    f = np.float32
    L = depth
    m = {}
    xin = np.zeros((npass, 128, KT, NTMAX), f)
    pin = np.zeros((L, npass, 128, 2, NTMAX), f)
    for p in range(npass):
        xs = inp["x_prompt"][b, p * NPT:(p + 1) * NPT]
        xin[p, :, :, :NPT] = xs.reshape(NPT, KT, 128).transpose(2, 1, 0)
        ps = inp["p_prompt"][:L, b, p * NPT:(p + 1) * NPT]
        pin[:, p, :, :, :NPT] = ps.reshape(L, NPT, 2, 128).transpose(0, 3, 2, 1)
    xs = inp["x_sample"][s0:s0 + NSEQ].reshape(NSMP, D)
    xin[npass - 1, :, :, NPT:] = xs.reshape(NSMP, KT, 128).transpose(2, 1, 0)
    ps = inp["p_sample"][:L, s0:s0 + NSEQ].reshape(L, NSMP, 256)
    pin[:, npass - 1, :, :, NPT:] = ps.reshape(L, NSMP, 2, 128).transpose(0, 3, 2, 1)
    m["xin"], m["pin"] = xin, pin

    par = np.zeros((L, 128, NPAR), f)

    def put(off, v, ntile):
        par[:, :, off:off + ntile] = v.reshape(L, ntile, 128).transpose(0, 2, 1)
    put(O_LN1G, inp["ln1_g"][:L], 16); put(O_LN1B, inp["ln1_b"][:L], 16)
    put(O_LN2G, inp["ln2_g"][:L], 16); put(O_LN2B, inp["ln2_b"][:L], 16)
    put(O_LN3G, inp["ln3_g"][:L], 16); put(O_LN3B, inp["ln3_b"][:L], 16)
    put(O_CB, inp["conv_b"][:L], 8); put(O_CLG, inp["conv_ln_g"][:L], 8); put(O_CLB, inp["conv_ln_b"][:L], 8)
    put(O_D, inp["s5_d"][:L].reshape(L, 1024), 8)
    par[:, :, O_CW:O_CW + 248] = inp["conv_w"][:L].reshape(L, 31, 8, 128).transpose(0, 3, 2, 1).reshape(L, 128, 248)
    par[:, :, O_FCW:O_FCW + 264] = inp["ffn_conv_w"][:L].reshape(L, 3, 88, 128).transpose(0, 3, 2, 1).reshape(L, 128, 264)
    put(O_FCB, inp["ffn_conv_b"][:L], 88)
    m["par"] = par

    def lay_p(v):
        return v.reshape(L, 32, 2, 64).transpose(0, 2, 3, 1).reshape(L, 128, 32)
    ldt = np.broadcast_to(inp["s5_log_dt"][:L, :, None], (L, 64, 64))
    m["lamp"] = np.ascontiguousarray(np.stack([lay_p(inp["s5_lam_re"][:L]), lay_p(inp["s5_lam_im"][:L]), lay_p(ldt)], axis=2))
    ctp = np.zeros((L, 128, 2, 32, 32), f)
    for ri, nm in enumerate(("s5_c_re", "s5_c_im")):
        c = inp[nm][:L].reshape(L, 32, 2, 16, 64)
        for g2 in range(2):
            ctp[:, g2 * 64:(g2 + 1) * 64, ri, :, g2 * 16:(g2 + 1) * 16] = c[:, :, g2].transpose(0, 3, 1, 2)
    m["ctp"] = ctp
    def lay_f(v):
        w = v.reshape(L, 8, 4, 2, 64)
        w = w.transpose(0, 2, 1, 3, 4).reshape(L, 4, 1, 8, 128)
        return np.broadcast_to(w, (L, 4, 32, 8, 128)).reshape(L, 128, 8, 128)
    m["lamf"] = np.ascontiguousarray(np.stack([lay_f(inp["s5_lam_re"][:L]), lay_f(inp["s5_lam_im"][:L]), lay_f(ldt)], axis=2))
    btf = np.zeros((L, 4, 2, 16, 2, 8, 2, 64), f)
    for ri, nm in enumerate(("s5_b_re", "s5_b_im")):
        bb = inp[nm][:L].reshape(L, 8, 4, 2, 64, 16)
        for g2 in range(2):
            btf[:, :, g2, :, ri, :, g2, :] = bb[:, :, :, g2].transpose(0, 2, 4, 1, 3)
    m["btf"] = btf.reshape(L, 128, 2, 8, 128)
    sin_ = np.zeros((L, 128, 32, 2, NSEQ), f)
    for ri, nm in enumerate(("state_s5_re", "state_s5_im")):
        sv = inp[nm][:L, s0:s0 + NSEQ].reshape(L, NSEQ, 32, 2, 64)
        sin_[:, :, :, ri, :] = sv.transpose(0, 3, 4, 2, 1).reshape(L, 128, 32, NSEQ)
    m["sin"] = sin_
    cc = inp["cache_conv"][:L, s0:s0 + NSEQ]
    m["ccr"] = np.ascontiguousarray(cc)
    m["ccf"] = np.ascontiguousarray(cc.reshape(L, NSEQ, 30, 8, 128).transpose(0, 4, 3, 1, 2))
    cf = inp["cache_ffn_conv"][:L, s0:s0 + NSEQ]
    m["cff"] = np.ascontiguousarray(cf.reshape(L, NSEQ, 2, 88, 128).transpose(0, 4, 3, 1, 2))
    m["w_in"] = inp["w_in"][:L]; m["w_glu"] = inp["s5_w_glu"][:L]; m["w_out"] = inp["w_out"][:L]
    m["w_up"] = inp["ffn_w_up"][:L]; m["w_down"] = inp["ffn_w_down"][:L]
    m["w_pe"] = inp["pe_w"][:L]; m["w_gate"] = inp["pe_w_gate"][:L]
    return m


def unpack_core(res, depth=DEPTH, npass=NPASS):
    L = depth
    o = {}
    y = res["o_y"]
    yp = y[:, :, :, :NPT].transpose(0, 3, 2, 1).reshape(npass * NPT, D)
    ys = y[npass - 1, :, :, NPT:].transpose(2, 1, 0).reshape(NSEQ, 4, D)
    o["y_prompt"], o["y_sample"] = yp, ys
    sp = res["o_s5p"].reshape(L, 2, 64, 32, 2)
    sp = sp.transpose(0, 4, 3, 1, 2).reshape(L, 2, 64, 64)
    o["s5_re_prompt"], o["s5_im_prompt"] = sp[:, 0], sp[:, 1]
    ss = res["o_s5s"].reshape(L, 2, 64, 32, 2, NSEQ)
    ss = ss.transpose(0, 4, 5, 3, 1, 2).reshape(L, 2, NSEQ, 64, 64)
    o["s5_re_sample"], o["s5_im_sample"] = ss[:, 0], ss[:, 1]
    o["conv_prompt"] = res["o_cvp"].transpose(0, 3, 2, 1).reshape(L, 30, 1024)
    new = res["o_cvsn"].transpose(0, 3, 4, 2, 1).reshape(L, NSEQ, 4, 1024)
    o["conv_sample"] = np.concatenate([res["o_cvso"], new], axis=2)
    o["ffn_conv_prompt"] = res["o_ffp"].transpose(0, 3, 2, 1).reshape(L, 2, 11264)
    o["ffn_conv_sample"] = res["o_ffs"].transpose(0, 3, 4, 2, 1).reshape(L, NSEQ, 2, 11264)
    return o


_PROG = {}


def kernel(**inputs):
    inp = {k: np.asarray(v) for k, v in inputs.items()}
    if "nc" not in _PROG:
        _PROG["nc"] = build_program()[0]
    nc = _PROG["nc"]
    n = 8
    in_maps = [make_core_inputs(inp, c % 4, 16 * c) for c in range(n)]
    res = run_bass_kernel_spmd(nc, in_maps, core_ids=list(range(n)))
    outs = [unpack_core(r) for r in res.results]
    f = np.float32
    y_prompt = np.stack([outs[b]["y_prompt"] for b in range(4)]).astype(f)
    y_sample = np.concatenate([outs[c]["y_sample"] for c in range(n)]).astype(f)

    def pstack(nm):
        return np.stack([outs[b][nm] for b in range(4)], axis=1).astype(f)

    def sstack(nm):
        return np.concatenate([outs[c][nm] for c in range(n)], axis=1).astype(f)

    return (y_prompt, y_sample,
            pstack("s5_re_prompt"), pstack("s5_im_prompt"), pstack("conv_prompt"), pstack("ffn_conv_prompt"),
            sstack("s5_re_sample"), sstack("s5_im_sample"), sstack("conv_sample"), sstack("ffn_conv_sample"))
```

```python
import contextlib
import math
import types
import numpy as np
import concourse.bass as bass
import concourse.mybir as mybir
from concourse.bass_utils import run_bass_kernel_spmd

F32 = mybir.dt.float32
BF16 = mybir.dt.bfloat16
I32 = mybir.dt.int32
AF = mybir.ActivationFunctionType
ALU = mybir.AluOpType

DEPTH = 4
D = 2048
KT = 16
NPASS = 4
NPT = 512
NSEQ = 16
NSMP = 64
NTMAX = NPT + NSMP
NCHP = 128
DFF = 5632
FT = 44
ALPHA = (2.0 * DEPTH) ** 0.25
EPS = 1e-5
O_LN1G, O_LN1B, O_LN2G, O_LN2B, O_LN3G, O_LN3B = 0, 16, 32, 48, 64, 80
O_CB, O_CLG, O_CLB, O_D, O_CW, O_FCW, O_FCB = 96, 104, 112, 120, 128, 376, 640
NPAR = 728
ENGS = ("pe", "act", "dve", "pool", "sp")
PI = math.pi


class Op:
    __slots__ = ("eng", "fn", "deps", "dma", "signal", "ev", "pos", "epoch")

    def __init__(self, eng, fn, deps, dma):
        self.eng, self.fn, self.deps, self.dma = eng, fn, deps, dma
        self.signal, self.ev, self.pos, self.epoch = False, None, 0, 0


def _freeze(fn):
    if fn.__closure__ is None:
        return fn
    cells = []
    for c in fn.__closure__:
        try:
            cells.append(types.CellType(c.cell_contents))
        except ValueError:
            cells.append(c)
    g = types.FunctionType(fn.__code__, fn.__globals__, fn.__name__, fn.__defaults__, tuple(cells))
    g.__kwdefaults__ = fn.__kwdefaults__
    return g


class Sched:
    def __init__(self):
        self.ops = []
        self.last_writer = {}
        self.readers = {}
        self.queues = {e: [] for e in ENGS}
        self.bar = None
        self.dmas_since_bar = []
        self.epoch = 0

    def op(self, eng, fn, reads=(), writes=(), dma=None, nobar=False):
        deps = set()
        if self.bar is not None and not nobar:
            deps.add(self.bar)
        for k in reads:
            w = self.last_writer.get(k)
            if w is not None:
                deps.add(w)
        for k in writes:
            w = self.last_writer.get(k)
            if w is not None:
                deps.add(w)
            deps.update(self.readers.get(k, ()))
        idx = len(self.ops)
        o = Op(eng, _freeze(fn), deps, dma)
        o.pos = len(self.queues[eng])
        o.epoch = self.epoch
        self.ops.append(o)
        self.queues[eng].append(idx)
        if dma is not None:
            self.dmas_since_bar.append(idx)
        for k in reads:
            self.readers.setdefault(k, []).append(idx)
        for k in writes:
            self.last_writer[k] = idx
            self.readers[k] = []
        return idx

    def barrier(self, fn):
        deps = set(self.dmas_since_bar)
        for e in ENGS:
            if self.queues[e]:
                deps.add(self.queues[e][-1])
        if self.bar is not None:
            deps.add(self.bar)
        idx = len(self.ops)
        o = Op("dve", fn, deps, None)
        o.pos = len(self.queues["dve"])
        o.epoch = self.epoch
        self.ops.append(o)
        self.queues["dve"].append(idx)
        self.bar = idx
        self.dmas_since_bar = []
        self.last_writer = {k: v for k, v in self.last_writer.items() if k.startswith("wr")}
        self.readers = {k: v for k, v in self.readers.items() if k.startswith("wr")}
        self.epoch += 1

    def emit(self, nc, stack):
        ops = self.ops
        need = [[] for _ in ops]
        for i, o in enumerate(ops):
            for d in o.deps:
                p = ops[d]
                if p.dma is None and p.eng == o.eng:
                    if o.eng in ("pe", "sp"):
                        continue
                    if o.pos - p.pos > 3:
                        continue
                if p.dma is None:
                    p.signal = True
                need[i].append(d)
        cnt = {}
        dcnt = {}
        for o in ops:
            if o.dma is not None:
                dcnt[o.dma] = dcnt.get(o.dma, 0) + 16
                o.ev = (("dma", o.dma), dcnt[o.dma])
            elif o.signal:
                k = ("eng", o.eng, o.epoch // 12)
                cnt[k] = cnt.get(k, 0) + 1
                o.ev = (k, cnt[k])
        sems = {}
        for k in list(cnt.keys()) + [("dma", c) for c in dcnt]:
            sems[k] = stack.enter_context(nc.semaphore("s%d" % len(sems)))
        self.nsems = len(sems)
        block = stack.enter_context(nc.Block())

        def run(engname):
            def body(eng):
                waited = {}
                for idx in self.queues[engname]:
                    o = ops[idx]
                    w = {}
                    for d in need[idx]:
                        sk, v = ops[d].ev
                        if waited.get(sk, 0) >= v:
                            continue
                        if w.get(sk, 0) < v:
                            w[sk] = v
                    for sk, v in w.items():
                        eng.wait_ge(sems[sk], v)
                        waited[sk] = v
                    ins = o.fn(eng)
                    if ins is None:
                        continue
                    if o.dma is not None:
                        ins.then_inc(sems[o.ev[0]], 16)
                    elif o.signal:
                        ins.then_inc(sems[o.ev[0]], 1)
            return body

        block.tensor(run("pe"))
        block.scalar(run("act"))
        block.vector(run("dve"))
        block.gpsimd(run("pool"))
        block.sync(run("sp"))


class Arena:
    def __init__(self, t, nwords):
        self.t, self.n, self.off, self.peak = t, nwords, 0, 0

    def mark(self):
        return self.off

    def rewind(self, m):
        self.off = m

    def _take(self, nw):
        assert self.off + nw <= self.n, ("arena overflow", self.off, nw, self.n)
        ap = self.t[:, self.off:self.off + nw]
        self.off += nw
        self.peak = max(self.peak, self.off)
        return ap

    def f32(self, shape, dt=None):
        n = int(np.prod(shape))
        ap = self._take(n)
        if dt is not None:
            ap = ap.bitcast(dt)
        return self._shape(ap, shape)

    def bf16(self, shape):
        n = int(np.prod(shape))
        ap = self._take((n + 1) // 2).bitcast(BF16)[:, 0:n]
        return self._shape(ap, shape)

    @staticmethod
    def _shape(ap, shape):
        if len(shape) == 1:
            return ap
        names = " ".join("d%d" % i for i in range(len(shape)))
        kw = {"d%d" % i: s for i, s in enumerate(shape)}
        return ap.rearrange("p (%s) -> p %s" % (names, names), **kw)


def build_program(depth=DEPTH, npass=NPASS, debug=False):
    nc = bass.Bass("TRN2", target_bir_lowering=False)
    S = Sched()

    def din(name, shape):
        return nc.dram_tensor(name, list(shape), F32, kind="ExternalInput").ap()

    def dout(name, shape):
        return nc.dram_tensor(name, list(shape), F32, kind="ExternalOutput").ap()

    xin = din("xin", [npass, 128, KT, NTMAX])
    pin = din("pin", [depth, npass, 128, 2, NTMAX])
    par = din("par", [depth, 128, NPAR])
    lamp = din("lamp", [depth, 128, 3, 32])
    ctp = din("ctp", [depth, 128, 2, 32, 32])
    lamf = din("lamf", [depth, 128, 3, 8, 128])
    btf = din("btf", [depth, 128, 2, 8, 128])
    sin_ = din("sin", [depth, 128, 32, 2, NSEQ])
    ccf = din("ccf", [depth, 128, 8, NSEQ, 30])
    ccr = din("ccr", [depth, NSEQ, 30, 1024])
    cff = din("cff", [depth, 128, 88, NSEQ, 2])
    w_in = din("w_in", [depth, D, 3072])
    w_glu = din("w_glu", [depth, 1024, 1024])
    w_out = din("w_out", [depth, D, D])
    w_up = din("w_up", [depth, D, 2 * DFF])
    w_down = din("w_down", [depth, DFF, D])
    w_pe = din("w_pe", [depth, 256, D])
    w_gate = din("w_gate", [depth, D, D])

    o_y = dout("o_y", [npass, 128, KT, NTMAX])
    o_s5p = dout("o_s5p", [depth, 128, 32, 2])
    o_s5s = dout("o_s5s", [depth, 128, 32, 2, NSEQ])
    o_cvp = dout("o_cvp", [depth, 128, 8, 30])
    o_cvsn = dout("o_cvsn", [depth, 128, 8, NSEQ, 4])
    o_cvso = dout("o_cvso", [depth, NSEQ, 26, 1024])
    o_ffp = dout("o_ffp", [depth, 128, 88, 2])
    o_ffs = dout("o_ffs", [depth, 128, 88, NSEQ, 2])
    if debug:
        dbg_g5 = dout("dbg_g5", [128, 8, NTMAX])
        dbg_u = dout("dbg_u", [128, 8, NTMAX])
        dbg_mix = dout("dbg_mix", [128, KT, NTMAX])
        dbg_x1 = dout("dbg_x1", [128, KT, NTMAX])

    xw_scr = nc.dram_tensor("xw_scr", [depth, 2, 128, 4096], BF16, kind="Internal").ap()
    pt_scr = nc.dram_tensor("pt_scr", [depth, 128, 1408], F32, kind="Internal").ap()
    st = contextlib.ExitStack()
    NW = 52000
    arena_t = st.enter_context(nc.sbuf_tensor("arena", [128, NW], F32))
    A = Arena(arena_t, NW)
    ps_all = st.enter_context(nc.psum_tensor("ps_all", [128, 8 * 512], F32))
    psb = [ps_all[:, i * 512:(i + 1) * 512] for i in range(8)]
    kps = ["ps%d" % i for i in range(8)]

    uid = [0]

    def key(prefix):
        if prefix == "o":
            uid[0] += 1
            return "oshared%d" % (uid[0] % 4)
        uid[0] += 1
        return "%s#%d" % (prefix, uid[0])

    def dve(fn, r=(), w=()):
        S.op("dve", fn, r, w)

    def act(fn, r=(), w=()):
        S.op("act", fn, r, w)

    def pe(fn, r=(), w=()):
        S.op("pe", fn, r, w)

    X = A.f32([KT, NTMAX]);   kX = "X"
    xb = A.bf16([KT, NTMAX]); kxb = "xb"
    ones = A.f32([128])
    scar = A.f32([depth, 32, 2])
    chist = A.f32([depth, 8, 30])
    fhist = A.f32([depth, 88, 2])
    dummy = A.f32([2])
    consts = A.f32([4])
    PAR = A.f32([NPAR])
    wring = [A.bf16([KT, 512]) for _ in range(2)]
    sig = A.f32([NTMAX])
    lnt = tuple(A.f32([512]) for _ in range(4))
    ident = A.f32([128])
    base0_mark = A.mark()
    mixcat = A.bf16([KT, NTMAX])
    base_mark = A.mark()

    dve(lambda e: e.memset(ones, 1.0), w=["ones"])
    dve(lambda e: e.memset(consts[:, 0:1], EPS), w=["consts"])
    dve(lambda e: e.memset(consts[:, 1:2], PI / 2), w=["consts"])
    dve(lambda e: e.memset(scar, 0.0), w=["scar"])
    dve(lambda e: e.memset(chist, 0.0), w=["chist"])
    dve(lambda e: e.memset(fhist, 0.0), w=["fhist"])
    dve(lambda e: e.memset(dummy, 0.0), w=["dummy"])
    epsb = consts[:, 0:1]
    halfpi = consts[:, 1:2]
    ident_i = lnt[0][:, 0:128].bitcast(I32)
    S.op("pool", lambda e: e.iota(out=ident_i, pattern=[[1, 128]], base=0, channel_multiplier=-1), writes=["ident_i"])
    dve(lambda e: e.tensor_scalar(out=ident, in0=ident_i, scalar1=0.0, scalar2=None, op0=ALU.is_equal), r=["ident_i"], w=["ident"])
    S.barrier(lambda e: e.memset(dummy, 0.0))

    ring_i = [0]

    def load_w(src_ap, nk, ncols):
        s = ring_i[0] % 2
        ring_i[0] += 1
        dst = wring[s].rearrange("p k c -> p (k c)")[:, 0:nk * ncols].rearrange("p (k c) -> p k c", k=nk)
        srcv = src_ap.rearrange("(k p) c -> p k c", p=128)
        S.op("pool", lambda e: e.dma_start(out=dst, in_=srcv), writes=["wr%da" % s, "wr%db" % s], dma="wr%da" % s, nobar=True)
        return dst, ["wr%da" % s, "wr%db" % s]

    def barrier():
        S.barrier(lambda e: e.memset(dummy, 0.0))

    def layernorm(src, ksrc, cols_list, goff, boff, ntile, dst_f32, kdst, dst_bf, kdstb, func=None, extra=None):
        mean, rstd, sq0, t10 = lnt
        sqs = [sq0, extra[0] if extra else sq0]
        t1s = [t10, extra[1] if extra else t10]
        ksq = ["ln_sq0", "ln_sq1" if extra else "ln_sq0"]
        kt1 = ["ln_t10", "ln_t11" if extra else "ln_t10"]
        inv = 1.0 / (ntile * 128)
        for (c0, n) in cols_list:
            ps_s, ps_q = psb[6], psb[7]
            for m in range(ntile):
                pe(lambda e, m=m: e.matmul(ps_s[:, 0:n], ones, src[:, m, c0:c0 + n], start=(m == 0), stop=(m == ntile - 1)),
                   r=[ksrc, "ones"], w=["ps6"])
            for m in range(ntile):
                sq = sqs[m % 2]
                act(lambda e, m=m, sq=sq: e.activation(out=sq[:, 0:n], in_=src[:, m, c0:c0 + n], func=AF.Square), r=[ksrc], w=[ksq[m % 2]])
                pe(lambda e, m=m, sq=sq: e.matmul(ps_q[:, 0:n], ones, sq[:, 0:n], start=(m == 0), stop=(m == ntile - 1)),
                   r=[ksq[m % 2], "ones"], w=["ps7"])
            t1 = t1s[0]
            act(lambda e: e.activation(out=mean[:, 0:n], in_=ps_s[:, 0:n], func=AF.Identity, scale=inv), r=["ps6"], w=["ln_mean"])
            dve(lambda e: e.tensor_tensor(out=t1[:, 0:n], in0=mean[:, 0:n], in1=mean[:, 0:n], op=ALU.mult), r=["ln_mean"], w=[kt1[0]])
            dve(lambda e: e.scalar_tensor_tensor(out=rstd[:, 0:n], in0=ps_q[:, 0:n], scalar=inv, in1=t1[:, 0:n], op0=ALU.mult, op1=ALU.subtract),
                r=["ps7", kt1[0]], w=["ln_rstd"])
            act(lambda e: e.activation(out=rstd[:, 0:n], in_=rstd[:, 0:n], func=AF.Ln, bias=epsb), r=["ln_rstd", "consts"], w=["ln_rstd"])
            act(lambda e: e.activation(out=rstd[:, 0:n], in_=rstd[:, 0:n], func=AF.Exp, scale=-0.5), r=["ln_rstd"], w=["ln_rstd"])
            for m in range(ntile):
                t1 = t1s[m % 2]
                k1 = kt1[m % 2]
                dve(lambda e, m=m, t1=t1: e.tensor_tensor(out=t1[:, 0:n], in0=src[:, m, c0:c0 + n], in1=mean[:, 0:n], op=ALU.subtract),
                    r=[ksrc, "ln_mean"], w=[k1])
                dve(lambda e, m=m, t1=t1: e.tensor_tensor(out=t1[:, 0:n], in0=t1[:, 0:n], in1=rstd[:, 0:n], op=ALU.mult),
                    r=[k1, "ln_rstd"], w=[k1])
                if dst_f32 is not None:
                    act(lambda e, m=m, t1=t1: e.activation(out=dst_f32[:, m, c0:c0 + n], in_=t1[:, 0:n], func=AF.Identity,
                                                           scale=PAR[:, goff + m:goff + m + 1], bias=PAR[:, boff + m:boff + m + 1]),
                        r=[k1, "PAR"], w=[kdst])
                    act(lambda e, m=m, t1=t1: e.activation(out=dst_bf[:, m, c0:c0 + n], in_=t1[:, 0:n], func=AF.Identity,
                                                           scale=PAR[:, goff + m:goff + m + 1], bias=PAR[:, boff + m:boff + m + 1]),
                        r=[k1, "PAR"], w=[kdstb])
                else:
                    act(lambda e, m=m, t1=t1: e.activation(out=dst_bf[:, m, c0:c0 + n], in_=t1[:, 0:n], func=func,
                                                           scale=PAR[:, goff + m:goff + m + 1], bias=PAR[:, boff + m:boff + m + 1]),
                        r=[k1, "PAR"], w=[kdstb])

    def matmul_tiles(wt, kw, mi, nk, rhs, krhs, cols, m):
        banks = []
        for gi, (c0, n) in enumerate(cols):
            b = (m % 2) * 2 + gi
            banks.append(b)
            for k in range(nk):
                pe(lambda e, b=b, k=k, c0=c0, n=n: e.matmul(psb[b][:, 0:n], wt[:, k, mi * 128:(mi + 1) * 128], rhs[:, k, c0:c0 + n],
                                                            start=(k == 0), stop=(k == nk - 1)),
                   r=kw + [krhs], w=[kps[b]])
        return banks

    def pw_setup(tag, lr, li, ldt, F):
        kk = lambda s_: "%s_%s" % (tag, s_)
        c = dict(tag=tag, lr=lr, li=li, F=F, kk=kk)
        c["dt"] = A.f32([F]); c["th"] = A.f32([F]); c["ld"] = A.f32([F])
        c["t0"] = A.f32([F]); c["t1"] = A.f32([F]); c["t2"] = A.f32([F]); c["ti"] = A.f32([F], dt=I32)
        dt, th, ld = c["dt"], c["th"], c["ld"]
        act(lambda e: e.activation(out=dt, in_=ldt, func=AF.Exp), r=[kk("in")], w=[kk("dt")])
        dve(lambda e: e.tensor_tensor(out=th, in0=li, in1=dt, op=ALU.mult), r=[kk("in"), kk("dt")], w=[kk("th")])
        dve(lambda e: e.tensor_tensor(out=ld, in0=lr, in1=dt, op=ALU.mult), r=[kk("in"), kk("dt")], w=[kk("ld")])
        return c

    def pw_power(c, n, pr, pi_, kp):
        kk = c["kk"]
        th, ld, t0, t1, t2, ti = c["th"], c["ld"], c["t0"], c["t1"], c["t2"], c["ti"]
        dve(lambda e: e.tensor_scalar(out=t0, in0=th, scalar1=float(n), scalar2=None, op0=ALU.mult), r=[kk("th")], w=[kk("t0")])
        dve(lambda e: e.tensor_scalar(out=t1, in0=t0, scalar1=1.0 / (2 * PI), scalar2=None, op0=ALU.mult), r=[kk("t0")], w=[kk("t1")])
        dve(lambda e: e.tensor_copy(out=ti, in_=t1), r=[kk("t1")], w=[kk("ti")])
        dve(lambda e: e.tensor_copy(out=t1, in_=ti), r=[kk("ti")], w=[kk("t1")])
        dve(lambda e: e.scalar_tensor_tensor(out=t0, in0=t1, scalar=-2 * PI, in1=t0, op0=ALU.mult, op1=ALU.add), r=[kk("t1"), kk("t0")], w=[kk("t0")])
        dve(lambda e: e.tensor_scalar(out=t1, in0=t0, scalar1=PI, scalar2=-2 * PI, op0=ALU.is_gt, op1=ALU.mult), r=[kk("t0")], w=[kk("t1")])
        dve(lambda e: e.tensor_tensor(out=t0, in0=t0, in1=t1, op=ALU.add), r=[kk("t0"), kk("t1")], w=[kk("t0")])
        dve(lambda e: e.tensor_scalar(out=t1, in0=t0, scalar1=-PI, scalar2=2 * PI, op0=ALU.is_lt, op1=ALU.mult), r=[kk("t0")], w=[kk("t1")])
        dve(lambda e: e.tensor_tensor(out=t0, in0=t0, in1=t1, op=ALU.add), r=[kk("t0"), kk("t1")], w=[kk("t0")])
        dve(lambda e: e.tensor_scalar(out=t0, in0=t0, scalar1=-3.1415925, scalar2=3.1415925, op0=ALU.max, op1=ALU.min), r=[kk("t0")], w=[kk("t0")])
        act(lambda e: e.activation(out=t1, in_=t0, func=AF.Sin), r=[kk("t0")], w=[kk("t1")])
        dve(lambda e: e.scalar_tensor_tensor(out=t0, in0=t0, scalar=-1.0, in1=t0, op0=ALU.mult, op1=ALU.max), r=[kk("t0")], w=[kk("t0")])
        act(lambda e: e.activation(out=t0, in_=t0, func=AF.Sin, scale=-1.0, bias=halfpi), r=[kk("t0"), "consts"], w=[kk("t0")])
        act(lambda e: e.activation(out=t2, in_=ld, func=AF.Exp, scale=float(n)), r=[kk("ld")], w=[kk("t2")])
        dve(lambda e: e.tensor_tensor(out=pr, in0=t2, in1=t0, op=ALU.mult), r=[kk("t2"), kk("t0")], w=[kp])
        dve(lambda e: e.tensor_tensor(out=pi_, in0=t2, in1=t1, op=ALU.mult), r=[kk("t2"), kk("t1")], w=[kp])

    def pw_q(c, p1r, p1i, kp1, qr, qi, kq):
        kk = c["kk"]
        lr, li, t0, t1, t2 = c["lr"], c["li"], c["t0"], c["t1"], c["t2"]
        dve(lambda e: e.tensor_tensor(out=t0, in0=lr, in1=lr, op=ALU.mult), r=[kk("in")], w=[kk("t0")])
        dve(lambda e: e.tensor_tensor(out=t1, in0=li, in1=li, op=ALU.mult), r=[kk("in")], w=[kk("t1")])
        dve(lambda e: e.tensor_tensor(out=t0, in0=t0, in1=t1, op=ALU.add), r=[kk("t0"), kk("t1")], w=[kk("t0")])
        dve(lambda e: e.reciprocal(out=t0, in_=t0), r=[kk("t0")], w=[kk("t0")])
        dve(lambda e: e.tensor_scalar(out=t1, in0=p1r, scalar1=-1.0, scalar2=None, op0=ALU.add), r=[kp1], w=[kk("t1")])
        dve(lambda e: e.tensor_tensor(out=qr, in0=t1, in1=lr, op=ALU.mult), r=[kk("t1"), kk("in")], w=[kq])
        dve(lambda e: e.tensor_tensor(out=t2, in0=p1i, in1=li, op=ALU.mult), r=[kp1, kk("in")], w=[kk("t2")])
        dve(lambda e: e.tensor_tensor(out=qr, in0=qr, in1=t2, op=ALU.add), r=[kq, kk("t2")], w=[kq])
        dve(lambda e: e.tensor_tensor(out=qr, in0=qr, in1=t0, op=ALU.mult), r=[kq, kk("t0")], w=[kq])
        dve(lambda e: e.tensor_tensor(out=qi, in0=p1i, in1=lr, op=ALU.mult), r=[kp1, kk("in")], w=[kq])
        dve(lambda e: e.tensor_tensor(out=t2, in0=t1, in1=li, op=ALU.mult), r=[kk("t1"), kk("in")], w=[kk("t2")])
        dve(lambda e: e.tensor_tensor(out=qi, in0=qi, in1=t2, op=ALU.subtract), r=[kq, kk("t2")], w=[kq])
        dve(lambda e: e.tensor_tensor(out=qi, in0=qi, in1=t0, op=ALU.mult), r=[kq, kk("t0")], w=[kq])

    def s5_stage(l, p, NS, cols, u, g5, last):
        NSQ = NS // 4
        NC = NCHP + NSQ
        s5_mark = A.mark()
        PT = A.f32([1408])
        pt_off = [0]

        def ptake(shape):
            n_ = int(np.prod(shape))
            ap = PT[:, pt_off[0]:pt_off[0] + n_]
            pt_off[0] += n_
            return ap if len(shape) == 1 else ap.rearrange("p (a b) -> p a b", b=shape[1])
        PP = {}
        npi = {}
        Ta = {}
        Tb = {}
        pkeys = []
        for n in (1, 2, 3, 4):
            PP[n] = (ptake([32]), ptake([32]), "P_P%d" % n)
            npi[n] = ptake([32])
            pkeys += ["P_P%d" % n, "P_npi%d" % n]
        for n in (4, 8, 12, 16, 20, 24, 28, 32):
            Ta[n] = ptake([32, 2]); Tb[n] = ptake([32, 2])
            pkeys.append("P_T%d" % n)
        if p == 0:
            pm = A.mark()
            LP = A.f32([3, 32])
            S.op("sp", lambda e: e.dma_start(out=LP, in_=lamp[l]), writes=["P_in"], dma="P_in")
            cP = pw_setup("P", LP[:, 0], LP[:, 1], LP[:, 2], 32)
            tpr = A.f32([32]); tpi = A.f32([32])
            for n in (1, 2, 3, 4, 8, 12, 16, 20, 24, 28, 32):
                if n <= 4:
                    pr, pi_, kpn = PP[n]
                    t = npi[n]
                else:
                    pr, pi_, t = tpr, tpi, None
                    kpn = "P_Pt"
                pw_power(cP, n, pr, pi_, kpn)
                if t is not None:
                    dve(lambda e, t=t, pi_=pi_: e.tensor_scalar(out=t, in0=pi_, scalar1=-1.0, scalar2=None, op0=ALU.mult), r=[kpn], w=["P_npi%d" % n])
                if n >= 4:
                    ta, tb = Ta[n], Tb[n]
                    dve(lambda e, ta=ta, pr=pr: e.tensor_copy(out=ta[:, :, 0], in_=pr), r=[kpn], w=["P_T%d" % n])
                    dve(lambda e, ta=ta, pr=pr: e.tensor_copy(out=ta[:, :, 1], in_=pr), r=[kpn], w=["P_T%d" % n])
                    dve(lambda e, tb=tb, pi_=pi_: e.tensor_scalar(out=tb[:, :, 0], in0=pi_, scalar1=-1.0, scalar2=None, op0=ALU.mult), r=[kpn], w=["P_T%d" % n])
                    dve(lambda e, tb=tb, pi_=pi_: e.tensor_copy(out=tb[:, :, 1], in_=pi_), r=[kpn], w=["P_T%d" % n])
            S.op("sp", lambda e: e.dma_start(out=pt_scr[l], in_=PT), reads=pkeys, writes=[key("ptscr")], dma=key("o"))
        else:
            S.op("sp", lambda e: e.dma_start(out=PT, in_=pt_scr[l]), writes=pkeys, dma="PTld")
        CTb = A.bf16([2, 32, 32])
        S.op("pool", lambda e: e.dma_start(out=CTb, in_=ctp[l]), writes=["CTb"], dma="CTb")
        dve(lambda e: e.tensor_scalar(out=CTb[:, 1], in0=CTb[:, 1], scalar1=-1.0, scalar2=None, op0=ALU.mult), r=["CTb"], w=["CTb"])
        half_mark = A.mark()
        for hf in range(2):
            A.rewind(half_mark)
            if hf:
                barrier()
            XW = A.bf16([4, 2, 4, 128])
            fm = A.mark()
            if p == 0:
                LF = A.f32([3, 4, 128])
                BT = A.f32([2, 4, 128])
                S.op("sp", lambda e, hf=hf: e.dma_start(out=LF, in_=lamf[l][:, :, 4 * hf:4 * hf + 4, :]), writes=["F_in"], dma="F_in")
                S.op("sp", lambda e, hf=hf: e.dma_start(out=BT, in_=btf[l][:, :, 4 * hf:4 * hf + 4, :]), writes=["BT"], dma="BT")
                fl = lambda ap: ap.rearrange("p a b -> p (a b)")
                cF = pw_setup("F", fl(LF[:, 0]), fl(LF[:, 1]), fl(LF[:, 2]), 512)
                t0, t1, t2 = cF["t0"], cF["t1"], cF["t2"]
                btr, bti = fl(BT[:, 0]), fl(BT[:, 1])
                pr = A.f32([512]); pi_ = A.f32([512]); qr = A.f32([512]); qi = A.f32([512])
                vr = A.f32([512]); vi = A.f32([512])
                kq, kp, kv = "F_q", "F_P", "F_V"
                for n in range(4):
                    if n == 0:
                        pw_power(cF, 1, pr, pi_, kp)
                        pw_q(cF, pr, pi_, kp, qr, qi, kq)
                        ur, ui, ku = qr, qi, kq
                    else:
                        if n > 1:
                            pw_power(cF, n, pr, pi_, kp)
                        dve(lambda e: e.tensor_tensor(out=vr, in0=pr, in1=qr, op=ALU.mult), r=[kp, kq], w=[kv])
                        dve(lambda e: e.tensor_tensor(out=t0, in0=pi_, in1=qi, op=ALU.mult), r=[kp, kq], w=["F_t0"])
                        dve(lambda e: e.tensor_tensor(out=vr, in0=vr, in1=t0, op=ALU.subtract), r=[kv, "F_t0"], w=[kv])
                        dve(lambda e: e.tensor_tensor(out=vi, in0=pr, in1=qi, op=ALU.mult), r=[kp, kq], w=[kv])
                        dve(lambda e: e.tensor_tensor(out=t0, in0=pi_, in1=qr, op=ALU.mult), r=[kp, kq], w=["F_t0"])
                        dve(lambda e: e.tensor_tensor(out=vi, in0=vi, in1=t0, op=ALU.add), r=[kv, "F_t0"], w=[kv])
                        ur, ui, ku = vr, vi, kv
                    xr = fl(XW[:, n, 0]); xi = fl(XW[:, n, 1])
                    dve(lambda e, ur=ur: e.tensor_tensor(out=t1, in0=ur, in1=btr, op=ALU.mult), r=[ku, "BT"], w=["F_t1"])
                    dve(lambda e, ui=ui: e.tensor_tensor(out=t2, in0=ui, in1=bti, op=ALU.mult), r=[ku, "BT"], w=["F_t2"])
                    dve(lambda e, xr=xr: e.tensor_tensor(out=xr, in0=t1, in1=t2, op=ALU.subtract), r=["F_t1", "F_t2"], w=["XW"])
                    dve(lambda e, ur=ur: e.tensor_tensor(out=t1, in0=ur, in1=bti, op=ALU.mult), r=[ku, "BT"], w=["F_t1"])
                    dve(lambda e, ui=ui: e.tensor_tensor(out=t2, in0=ui, in1=btr, op=ALU.mult), r=[ku, "BT"], w=["F_t2"])
                    dve(lambda e, xi=xi: e.tensor_tensor(out=xi, in0=t1, in1=t2, op=ALU.add), r=["F_t1", "F_t2"], w=["XW"])
                S.op("sp", lambda e, hf=hf: e.dma_start(out=xw_scr[l, hf], in_=XW.rearrange("p a b c d -> p (a b c d)")), reads=["XW"], writes=[key("xwscr")], dma=key("o"))
            else:
                S.op("sp", lambda e, hf=hf: e.dma_start(out=XW.rearrange("p a b c d -> p (a b c d)"), in_=xw_scr[l, hf]), writes=["XW"], dma="XWld")
            A.rewind(fm)
            SP = A.f32([16, 2, NCHP + 1])
            XS = A.f32([16, 2, NSEQ])
            H0 = A.f32([16, 2, NSEQ])
            SO = A.f32([16, 2, NSEQ])
            CB = A.f32([16, 2, 17])
            st1 = A.f32([16, 2, 16]); st2 = A.f32([16, 2, 16])
            Hf = [A.bf16([4, 2, NCHP + NSEQ]) for _ in range(2)]
            tmpc = A.f32([3, 2, NCHP + NSEQ])
            sws = [A.f32([2, NCHP + NSEQ]) for _ in range(2)]
            ysb = sig
            dve(lambda e: e.memset(SP[:, :, :, 0:1], 0.0), r=["XW", "F_t1", "F_t2", "F_t0", "F_in", "BT", "F_q", "F_V", "F_P", "F_th", "F_ld", "F_dt", "F_ti"],
                w=["SP", "XS", "H0", "SO", "CB", "st1", "st2", "Hf0", "Hf1", "sw0", "sw1", "tmpc0", "tmpc1", "tmpc2"])
            dve(lambda e, hf=hf: e.tensor_copy(out=SP[:, :, :, 0], in_=scar[:, l, 16 * hf:16 * hf + 16, :]), r=["scar"], w=["SP"])
            if NS:
                S.op("sp", lambda e, hf=hf: e.dma_start(out=H0, in_=sin_[l][:, 16 * hf:16 * hf + 16]), writes=["H0"], dma="H0")
            p4r, p4i, kp4 = PP[4]
            sl = slice(16 * hf, 16 * hf + 16)
            for ii in range(16):
                i = 16 * hf + ii
                tl, ip = ii // 4, ii % 4
                t = i // 4
                rows = slice(32 * ip, 32 * ip + 32)
                for ri in range(2):
                    b = (ii * 2 + ri) % 2
                    for s in range(4):
                        pe(lambda e, b=b, s=s, ri=ri, tl=tl, t=t, rows=rows, ip=ip: e.matmul(
                            psb[b][:, 0:NC], XW[rows, 3 - s, ri, tl, :], u[rows, t, s:4 * NC:4],
                            start=(s == 0), stop=(s == 3), tile_position=(32 * ip, 0)),
                           r=["XW", "u"], w=[kps[b]])
                    act(lambda e, b=b, ii=ii, ri=ri: e.activation(out=SP[:, ii, ri, 1:NCHP + 1], in_=psb[b][:, 0:NCHP], func=AF.Copy),
                        r=[kps[b]], w=["SP"])
                    if NS:
                        act(lambda e, b=b, ii=ii, ri=ri: e.activation(out=XS[:, ii, ri, 0:NSQ], in_=psb[b][:, NCHP:NC], func=AF.Copy),
                            r=[kps[b]], w=["XS"])
            SPb = SP[:, :, :, 1:NCHP + 1].rearrange("p a r (b k) -> p a r b k", k=8)
            tab = {n_: Ta[n_][:, sl, :].unsqueeze(3).to_broadcast([128, 16, 2, 16]) for n_ in Ta}
            tbb = {n_: Tb[n_][:, sl, :].unsqueeze(3).to_broadcast([128, 16, 2, 16]) for n_ in Tb}
            for k in range(1, 8):
                dve(lambda e, k=k: e.tensor_tensor(out=st1, in0=SPb[:, :, :, :, k - 1], in1=tab[4], op=ALU.mult), r=["SP", "P_T4"], w=["st1"])
                dve(lambda e, k=k: e.tensor_tensor(out=st2, in0=SPb[:, :, ::-1, :, k - 1], in1=tbb[4], op=ALU.mult), r=["SP", "P_T4"], w=["st2"])
                dve(lambda e, k=k: e.tensor_tensor(out=SPb[:, :, :, :, k], in0=SPb[:, :, :, :, k], in1=st1, op=ALU.add), r=["SP", "st1"], w=["SP"])
                dve(lambda e, k=k: e.tensor_tensor(out=SPb[:, :, :, :, k], in0=SPb[:, :, :, :, k], in1=st2, op=ALU.add), r=["SP", "st2"], w=["SP"])
            dve(lambda e: e.tensor_copy(out=CB[:, :, :, 0], in_=SP[:, :, :, 0]), r=["SP"], w=["CB"])
            for b_ in range(16):
                dve(lambda e, b_=b_: e.tensor_tensor(out=st1[:, :, :, 0], in0=CB[:, :, :, b_], in1=Ta[32][:, sl, :], op=ALU.mult), r=["CB", "P_T32"], w=["st1"])
                dve(lambda e, b_=b_: e.tensor_tensor(out=st2[:, :, :, 0], in0=CB[:, :, ::-1, b_], in1=Tb[32][:, sl, :], op=ALU.mult), r=["CB", "P_T32"], w=["st2"])
                dve(lambda e, b_=b_: e.tensor_tensor(out=st1[:, :, :, 0], in0=st1[:, :, :, 0], in1=st2[:, :, :, 0], op=ALU.add), r=["st1", "st2"], w=["st1"])
                dve(lambda e, b_=b_: e.tensor_tensor(out=CB[:, :, :, b_ + 1], in0=SPb[:, :, :, b_, 7], in1=st1[:, :, :, 0], op=ALU.add), r=["SP", "st1", "CB"], w=["CB"])
            for k in range(8):
                n_ = 4 * (k + 1)
                dve(lambda e, k=k, n_=n_: e.tensor_tensor(out=st1, in0=CB[:, :, :, 0:16], in1=tab[n_], op=ALU.mult), r=["CB", "P_T%d" % n_], w=["st1"])
                dve(lambda e, k=k, n_=n_: e.tensor_tensor(out=st2, in0=CB[:, :, ::-1, 0:16], in1=tbb[n_], op=ALU.mult), r=["CB", "P_T%d" % n_], w=["st2"])
                dve(lambda e, k=k: e.tensor_tensor(out=SPb[:, :, :, :, k], in0=SPb[:, :, :, :, k], in1=st1, op=ALU.add), r=["SP", "st1"], w=["SP"])
                dve(lambda e, k=k: e.tensor_tensor(out=SPb[:, :, :, :, k], in0=SPb[:, :, :, :, k], in1=st2, op=ALU.add), r=["SP", "st2"], w=["SP"])
            dve(lambda e, hf=hf: e.tensor_copy(out=scar[:, l, 16 * hf:16 * hf + 16, :], in_=SP[:, :, :, NCHP]), r=["SP"], w=["scar"])
            if NS:
                for ri in range(2):
                    for ii in range(16):
                        i = 16 * hf + ii
                        dve(lambda e, ii=ii, i=i, ri=ri: e.scalar_tensor_tensor(out=SO[:, ii, ri, :], in0=H0[:, ii, ri, :], scalar=p4r[:, i:i + 1],
                                                                               in1=XS[:, ii, ri, :], op0=ALU.mult, op1=ALU.add),
                            r=["H0", "XS", kp4], w=["SO"])
                        sc = npi[4] if ri == 0 else p4i
                        dve(lambda e, ii=ii, i=i, ri=ri, sc=sc: e.scalar_tensor_tensor(out=SO[:, ii, ri, :], in0=H0[:, ii, 1 - ri, :], scalar=sc[:, i:i + 1],
                                                                                      in1=SO[:, ii, ri, :], op0=ALU.mult, op1=ALU.add),
                            r=["H0", kp4, "P_npi4", "SO"], w=["SO"])
                S.op("sp", lambda e, hf=hf: e.dma_start(out=o_s5s[l][:, 16 * hf:16 * hf + 16], in_=SO), reads=["SO"], writes=[key("o_s5s")], dma=key("o"))
            def pair_ctx(ii):
                tl, ip = ii // 4, ii % 4
                return dict(ii=ii, tl=tl, ip=ip, t=4 * hf + tl, i=16 * hf + ii, rows=slice(32 * ip, 32 * ip + 32),
                            hb=Hf[ii % 2], khf="Hf%d" % (ii % 2), bk0=2 * (ii % 2), ybase=4 + (tl % 2) * 2)

            def emit_hloc(c_):
                ii, tl, ip, t, i, rows, hb, khf, bk0, ybase = (c_[k_] for k_ in ("ii", "tl", "ip", "t", "i", "rows", "hb", "khf", "bk0", "ybase"))
                for ri in range(2):
                    b = bk0 + ri
                    for j in range(3):
                        g0 = j * 144
                        for s_ in range(j + 1):
                            pe(lambda e, b=b, s_=s_, j=j, ri=ri, tl=tl, t=t, rows=rows, ip=ip, g0=g0: e.matmul(
                                psb[b][:, g0:g0 + NC], XW[rows, j - s_, ri, tl, :], u[rows, t, s_:4 * NC:4],
                                start=(s_ == 0), stop=(s_ == j), tile_position=(32 * ip, 0)),
                               r=["XW", "u"], w=[kps[b]])

            def emit_post(c_):
                ii, tl, ip, t, i, rows, hb, khf, bk0, ybase = (c_[k_] for k_ in ("ii", "tl", "ip", "t", "i", "rows", "hb", "khf", "bk0", "ybase"))
                sw = sws[ii % 2]
                ksw = "sw%d" % (ii % 2)
                act(lambda e, ii=ii, sw=sw: e.activation(out=sw[:, 0, 0:NCHP], in_=SP[:, ii, 1, 0:NCHP], func=AF.Identity, scale=-1.0), r=["SP"], w=[ksw])
                act(lambda e, ii=ii, sw=sw: e.activation(out=sw[:, 1, 0:NCHP], in_=SP[:, ii, 0, 0:NCHP], func=AF.Copy), r=["SP"], w=[ksw])
                if NS:
                    act(lambda e, ii=ii, sw=sw: e.activation(out=sw[:, 0, NCHP:NC], in_=H0[:, ii, 1, :], func=AF.Identity, scale=-1.0), r=["H0"], w=[ksw])
                    act(lambda e, ii=ii, sw=sw: e.activation(out=sw[:, 1, NCHP:NC], in_=H0[:, ii, 0, :], func=AF.Copy), r=["H0"], w=[ksw])
                pbank = ps_all[:, bk0 * 512:(bk0 + 2) * 512].rearrange("p (r c) -> p r c", r=2)
                for phase_ in range(2):
                    for j in range(3):
                        pjr, pji, kpj = PP[j + 1]
                        g0 = j * 144
                        groups = [(0, NCHP, SP[:, ii, :, 0:NCHP], "SP")]
                        if NS:
                            groups.append((NCHP, NSQ, H0[:, ii, :, :], "H0"))
                        for (c0, n, sa, ks) in groups:
                            if phase_ == 0:
                                dve(lambda e, c0=c0, n=n, sa=sa, i=i, pjr=pjr, j=j, g0=g0, pbank=pbank: e.scalar_tensor_tensor(
                                    out=tmpc[:, j, :, c0:c0 + n], in0=sa, scalar=pjr[:, i:i + 1], in1=pbank[:, :, g0 + c0:g0 + c0 + n], op0=ALU.mult, op1=ALU.add),
                                    r=[ks, kps[bk0], kps[bk0 + 1], kpj], w=["tmpc%d" % j])
                            else:
                                dve(lambda e, c0=c0, n=n, sw=sw, i=i, pji=pji, hb=hb, j=j: e.scalar_tensor_tensor(
                                    out=hb[:, j, :, c0:c0 + n], in0=sw[:, :, c0:c0 + n], scalar=pji[:, i:i + 1], in1=tmpc[:, j, :, c0:c0 + n], op0=ALU.mult, op1=ALU.add),
                                    r=[ksw, "tmpc%d" % j, kpj], w=[khf])
                for ri in range(2):
                    act(lambda e, ii=ii, ri=ri, hb=hb: e.activation(out=hb[:, 3, ri, 0:NCHP], in_=SP[:, ii, ri, 1:NCHP + 1], func=AF.Copy), r=["SP"], w=[khf])
                    if NS:
                        act(lambda e, ii=ii, ri=ri, hb=hb: e.activation(out=hb[:, 3, ri, NCHP:NC], in_=SO[:, ii, ri, :], func=AF.Copy), r=["SO"], w=[khf])

            def emit_y(c_):
                ii, tl, ip, t, i, rows, hb, khf, bk0, ybase = (c_[k_] for k_ in ("ii", "tl", "ip", "t", "i", "rows", "hb", "khf", "bk0", "ybase"))
                for j in range(4):
                    for ri in range(2):
                        pe(lambda e, ip=ip, j=j, ri=ri, i=i, hb=hb, ybase=ybase: e.matmul(
                            psb[ybase][32 * ip:32 * ip + 32, j:NPT:4], CTb[:, ri, i, :], hb[:, j, ri, 0:NCHP],
                            start=(ri == 0), stop=(ri == 1), tile_position=(0, 32 * ip)),
                           r=["CTb", khf], w=[kps[ybase]])
                        if NS:
                            pe(lambda e, ip=ip, j=j, ri=ri, i=i, hb=hb, ybase=ybase: e.matmul(
                                psb[ybase + 1][32 * ip:32 * ip + 32, j:NS:4], CTb[:, ri, i, :], hb[:, j, ri, NCHP:NC],
                                start=(ri == 0), stop=(ri == 1), tile_position=(0, 32 * ip)),
                               r=["CTb", khf], w=[kps[ybase + 1]])

            def emit_evac(tl):
                t = 4 * hf + tl
                ybase = 4 + (tl % 2) * 2
                for gi, (c0, n) in enumerate(cols):
                    dve(lambda e, t=t, c0=c0, n=n, b=ybase + gi: e.scalar_tensor_tensor(
                        out=ysb[:, c0:c0 + n], in0=u[:, t, c0:c0 + n], scalar=PAR[:, O_D + t:O_D + t + 1], in1=psb[b][:, 0:n], op0=ALU.mult, op1=ALU.add),
                        r=["u", "PAR", kps[ybase + gi]], w=["sig"])
                    act(lambda e, t=t, c0=c0, n=n: e.activation(out=g5[:, t, c0:c0 + n], in_=ysb[:, c0:c0 + n], func=AF.Gelu_apprx_tanh),
                        r=["sig"], w=["g5"])

            ctxs = [pair_ctx(ii) for ii in range(16)]
            emit_hloc(ctxs[0])
            for ii in range(16):
                if ii + 1 < 16:
                    emit_hloc(ctxs[ii + 1])
                emit_post(ctxs[ii])
                emit_y(ctxs[ii])
                if ii % 4 == 3:
                    emit_evac(ii // 4)
        if last:
            S.op("sp", lambda e: e.dma_start(out=o_s5p[l], in_=scar[:, l]), reads=["scar"], writes=[key("o_s5p")], dma=key("o"))
        A.rewind(s5_mark)

    for p in range(npass):
        last = (p == npass - 1)
        NS = NSMP if last else 0
        NT = NPT + NS
        cols = [(0, NPT)] + ([(NPT, NS)] if NS else [])
        S.op("sp", lambda e, p=p: e.dma_start(out=X, in_=xin[p]), writes=[kX], dma="X")
        dve(lambda e: e.tensor_copy(out=xb, in_=X), r=[kX], w=[kxb])
        for l in range(depth):
            S.op("sp", lambda e, l=l: e.dma_start(out=PAR, in_=par[l]), writes=["PAR"], dma="PAR")
            A.rewind(base_mark)
            u = A.bf16([8, NTMAX])
            g5 = A.bf16([8, NTMAX])
            for ci in range(2):
                wt, kw = load_w(w_in[l][:, ci * 512:(ci + 1) * 512], KT, 512)
                for mi in range(4):
                    m = ci * 4 + mi
                    banks = matmul_tiles(wt, kw, mi, KT, xb, kxb, cols, m)
                    for gi, (c0, n) in enumerate(cols):
                        b = banks[gi]
                        act(lambda e, b=b, m=m, c0=c0, n=n: e.activation(out=u[:, m, c0:c0 + n], in_=psb[b][:, 0:n], func=AF.Copy),
                            r=[kps[b]], w=["u"])
            s5_stage(l, p, NS, cols, u, g5, last)
            if debug and last and l == 0:
                S.op("pool", lambda e: e.dma_start(out=dbg_g5, in_=g5), reads=["g5"], writes=[key("o_dbg")], dma=key("o"))
                S.op("pool", lambda e: e.dma_start(out=dbg_u, in_=u), reads=["u"], writes=[key("o_dbg")], dma=key("o"))
            for ci in range(2):
                wt, kw = load_w(w_glu[l][:, ci * 512:(ci + 1) * 512], 8, 512)
                for mi in range(4):
                    m = ci * 4 + mi
                    banks = matmul_tiles(wt, kw, mi, 8, g5, "g5", cols, m)
                    for gi, (c0, n) in enumerate(cols):
                        b = banks[gi]
                        act(lambda e, b=b, c0=c0, n=n: e.activation(out=sig[:, c0:c0 + n], in_=psb[b][:, 0:n], func=AF.Sigmoid),
                            r=[kps[b]], w=["sig"])
                        dve(lambda e, m=m, c0=c0, n=n: e.tensor_tensor(out=mixcat[:, m, c0:c0 + n], in0=g5[:, m, c0:c0 + n], in1=sig[:, c0:c0 + n], op=ALU.mult),
                            r=["g5", "sig"], w=["mixcat"])
            barrier()

            A.rewind(base_mark)
            cv = A.f32([8, NTMAX])
            convy = A.f32([8, NTMAX])
            cbp = A.bf16([8, 30 + NPT])
            cbs = A.bf16([8, NSEQ, 34])
            csn = A.f32([8, NSEQ, 4])
            dg = [A.bf16([31, 128]) for _ in range(2)]
            lnx = (A.f32([512]), A.f32([512]))
            dve(lambda e, l=l: e.tensor_copy(out=cbp[:, :, 0:30], in_=chist[:, l]), r=["chist"], w=["cbp"])
            if NS:
                for hh in range(2):
                    S.op("pool", lambda e, l=l, hh=hh: e.dma_start(out=cbs[:, 4 * hh:4 * hh + 4, :, 0:30], in_=ccf[l][:, 4 * hh:4 * hh + 4]),
                         writes=["cbs"], dma="cbs%d" % hh)
            for ci in range(2, 6):
                wt, kw = load_w(w_in[l][:, ci * 512:(ci + 1) * 512], KT, 512)
                for mi in range(4):
                    m = ci * 4 + mi
                    banks = matmul_tiles(wt, kw, mi, KT, xb, kxb, cols, m)
                    for gi, (c0, n) in enumerate(cols):
                        b = banks[gi]
                        if m < 16:
                            act(lambda e, b=b, m=m, c0=c0, n=n: e.activation(out=cv[:, m - 8, c0:c0 + n], in_=psb[b][:, 0:n], func=AF.Copy),
                                r=[kps[b]], w=["cv"])
                        else:
                            mm = m - 16
                            act(lambda e, b=b, c0=c0, n=n: e.activation(out=sig[:, c0:c0 + n], in_=psb[b][:, 0:n], func=AF.Sigmoid),
                                r=[kps[b]], w=["sig"])
                            if gi == 0:
                                dve(lambda e, mm=mm: e.tensor_tensor(out=cbp[:, mm, 30:30 + NPT], in0=cv[:, mm, 0:NPT], in1=sig[:, 0:NPT], op=ALU.mult),
                                    r=["cv", "sig"], w=["cbp"])
                                dve(lambda e, mm=mm, l=l: e.tensor_tensor(out=chist[:, l, mm, :], in0=cv[:, mm, NPT - 30:NPT], in1=sig[:, NPT - 30:NPT], op=ALU.mult),
                                    r=["cv", "sig", "cbp"], w=["chist"])
                            else:
                                dve(lambda e, mm=mm: e.tensor_tensor(out=csn[:, mm],
                                                                     in0=cv[:, mm, NPT:NPT + NS].rearrange("p (s t) -> p s t", t=4),
                                                                     in1=sig[:, NPT:NPT + NS].rearrange("p (s t) -> p s t", t=4), op=ALU.mult),
                                    r=["cv", "sig"], w=["csn"])
                                dve(lambda e, mm=mm: e.tensor_copy(out=cbs[:, mm, :, 30:34], in_=csn[:, mm]), r=["csn"], w=["cbs"])
            if last:
                S.op("sp", lambda e, l=l: e.dma_start(out=o_cvp[l], in_=chist[:, l]), reads=["chist"], writes=[key("o_cvp")], dma=key("o"))
                S.op("sp", lambda e, l=l: e.dma_start(out=o_cvsn[l], in_=csn), reads=["csn"], writes=[key("o_cvsn")], dma=key("o"))
                S.op("sp", lambda e, l=l: e.dma_start(out=o_cvso[l], in_=ccr[l][:, 4:30, :]), writes=[key("o_cvso")], dma=key("o"))
            for m in range(8):
                d_ = dg[m % 2]
                kd = "dg%d" % (m % 2)
                for k in range(31):
                    if k % 2 == 0:
                        act(lambda e, d_=d_, m=m, k=k: e.activation(out=d_[:, k, :], in_=ident, func=AF.Identity,
                                                                    scale=PAR[:, O_CW + m * 31 + k:O_CW + m * 31 + k + 1]),
                            r=["ident", "PAR"], w=[kd + "a"])
                    else:
                        dve(lambda e, d_=d_, m=m, k=k: e.tensor_scalar(out=d_[:, k, :], in0=ident, scalar1=PAR[:, O_CW + m * 31 + k:O_CW + m * 31 + k + 1],
                                                                       scalar2=None, op0=ALU.mult),
                            r=["ident", "PAR"], w=[kd + "b"])
                for gi, (c0, n) in enumerate(cols):
                    b = (m % 2) * 2 + gi
                    for k in range(31):
                        if gi == 0:
                            pe(lambda e, b=b, k=k, m=m, d_=d_: e.matmul(psb[b][:, 0:NPT], d_[:, k, :], cbp[:, m, k:k + NPT], start=(k == 0), stop=(k == 30)),
                               r=[kd + "a", kd + "b", "cbp"], w=[kps[b]])
                        else:
                            pe(lambda e, b=b, k=k, m=m, d_=d_: e.matmul(psb[b][:, 0:NS].rearrange("p (s t) -> p s t", t=4), d_[:, k, :], cbs[:, m, :, k:k + 4],
                                                                        start=(k == 0), stop=(k == 30)),
                               r=[kd + "a", kd + "b", "cbs"], w=[kps[b]])
                    act(lambda e, b=b, m=m, c0=c0, n=n: e.activation(out=convy[:, m, c0:c0 + n], in_=psb[b][:, 0:n], func=AF.Identity,
                                                                    bias=PAR[:, O_CB + m:O_CB + m + 1]),
                        r=[kps[b], "PAR"], w=["convy"])
            layernorm(convy, "convy", cols, O_CLG, O_CLB, 8, None, None, mixcat[:, 8:16, :], "mixcat", func=AF.Silu, extra=lnx)

            def proj_residual(wsrc, nk, rhs, krhs):
                for ci in range(4):
                    wt, kw = load_w(wsrc[:, ci * 512:(ci + 1) * 512], nk, 512)
                    for mi in range(4):
                        m = ci * 4 + mi
                        banks = matmul_tiles(wt, kw, mi, nk, rhs, krhs, cols, m)
                        for gi, (c0, n) in enumerate(cols):
                            b = banks[gi]
                            dve(lambda e, b=b, m=m, c0=c0, n=n: e.scalar_tensor_tensor(out=X[:, m, c0:c0 + n], in0=X[:, m, c0:c0 + n], scalar=ALPHA,
                                                                                      in1=psb[b][:, 0:n], op0=ALU.mult, op1=ALU.add),
                                r=[kX, kps[b]], w=[kX])
            if debug and last and l == 0:
                S.op("pool", lambda e: e.dma_start(out=dbg_mix, in_=mixcat), reads=["mixcat"], writes=[key("o_dbg")], dma=key("o"))
            proj_residual(w_out[l], KT, mixcat, "mixcat")
            layernorm(X, kX, cols, O_LN1G, O_LN1B, KT, X, kX, xb, kxb, extra=lnx)
            if debug and last and l == 0:
                S.op("sp", lambda e: e.dma_start(out=dbg_x1, in_=X), reads=[kX], writes=[key("o_dbg")], dma=key("o"))
            barrier()

            A.rewind(base0_mark)
            actb = A.bf16([FT, NTMAX])
            extp = [A.f32([2 + NPT]) for _ in range(2)]
            exts = [A.f32([NSEQ, 6]) for _ in range(2)]
            hy = [A.f32([NTMAX]) for _ in range(2)]
            cfs = [A.f32([NSEQ, 2]) for _ in range(2)]
            ffs_out = A.f32([88, NSEQ, 2])
            pb = A.bf16([2, NTMAX])
            wpe = A.bf16([2, D])
            lnx = (A.f32([512]), A.f32([512]))
            wv = w_up[l].rearrange("(k p) c -> p k c", p=128)
            for j in range(FT):
                jl = j % 2
                if jl == 0:
                    s = ring_i[0] % 2
                    ring_i[0] += 1
                    wt = wring[s]
                    kwa, kwb = "wr%da" % s, "wr%db" % s
                    S.op("pool", lambda e, wt=wt, j=j: e.dma_start(out=wt[:, :, 0:256], in_=wv[:, :, j * 128:(j + 2) * 128]), writes=[kwa], dma=kwa, nobar=True)
                    S.op("pool", lambda e, wt=wt, j=j: e.dma_start(out=wt[:, :, 256:512], in_=wv[:, :, DFF + j * 128:DFF + (j + 2) * 128]), writes=[kwb], dma=kwb, nobar=True)
                for h in range(2):
                    ft = h * FT + j
                    kwh = [kwa, kwb][h]
                    for gi, (c0, n) in enumerate(cols):
                        b = jl * 4 + h * 2 + gi
                        for k in range(KT):
                            pe(lambda e, b=b, k=k, h=h, c0=c0, n=n, wt=wt, jl=jl: e.matmul(psb[b][:, 0:n], wt[:, k, h * 256 + jl * 128:h * 256 + (jl + 1) * 128], xb[:, k, c0:c0 + n],
                                                                                     start=(k == 0), stop=(k == KT - 1)),
                               r=[kwh, kxb], w=[kps[b]])
                    wf = lambda k, ft=ft: PAR[:, O_FCW + ft * 3 + k:O_FCW + ft * 3 + k + 1]
                    bf = PAR[:, O_FCB + ft:O_FCB + ft + 1]
                    ke = "extp%d" % h
                    kh = "hy%d" % h
                    dve(lambda e, h=h, ft=ft, l=l: e.tensor_copy(out=extp[h][:, 0:2], in_=fhist[:, l, ft, :]), r=["fhist"], w=[ke])
                    act(lambda e, h=h, jl=jl: e.activation(out=extp[h][:, 2:2 + NPT], in_=psb[jl * 4 + h * 2][:, 0:NPT], func=AF.Copy), r=[kps[jl * 4 + h * 2]], w=[ke])
                    dve(lambda e, h=h, ft=ft, l=l: e.tensor_copy(out=fhist[:, l, ft, :], in_=extp[h][:, NPT:NPT + 2]), r=[ke], w=["fhist"])
                    dve(lambda e, h=h, wf=wf, bf=bf: e.tensor_scalar(out=hy[h][:, 0:NPT], in0=extp[h][:, 2:2 + NPT], scalar1=wf(2), scalar2=bf, op0=ALU.mult, op1=ALU.add),
                        r=[ke, "PAR"], w=[kh])
                    for k in range(2):
                        dve(lambda e, h=h, k=k, wf=wf: e.scalar_tensor_tensor(out=hy[h][:, 0:NPT], in0=extp[h][:, k:k + NPT], scalar=wf(k), in1=hy[h][:, 0:NPT],
                                                                             op0=ALU.mult, op1=ALU.add),
                            r=[ke, "PAR", kh], w=[kh])
                    if NS:
                        kes = "exts%d" % h
                        kcf = "cfs%d" % h
                        hys = hy[h][:, NPT:NPT + NS].rearrange("p (s t) -> p s t", t=4)
                        S.op("sp", lambda e, h=h, ft=ft, l=l: e.dma_start(out=cfs[h], in_=cff[l][:, ft]), writes=[kcf], dma=kcf)
                        dve(lambda e, h=h: e.tensor_copy(out=exts[h][:, :, 0:2], in_=cfs[h]), r=[kcf], w=[kes])
                        act(lambda e, h=h, jl=jl: e.activation(out=exts[h][:, :, 2:6], in_=psb[jl * 4 + h * 2 + 1][:, 0:NS].rearrange("p (s t) -> p s t", t=4), func=AF.Copy),
                            r=[kps[jl * 4 + h * 2 + 1]], w=[kes])
                        dve(lambda e, h=h, ft=ft: e.tensor_copy(out=ffs_out[:, ft], in_=exts[h][:, :, 4:6]), r=[kes], w=["ffs_out"])
                        dve(lambda e, h=h, wf=wf, bf=bf, hys=hys: e.tensor_scalar(out=hys, in0=exts[h][:, :, 2:6], scalar1=wf(2), scalar2=bf, op0=ALU.mult, op1=ALU.add),
                            r=[kes, "PAR"], w=[kh])
                        for k in range(2):
                            dve(lambda e, h=h, k=k, wf=wf, hys=hys: e.scalar_tensor_tensor(out=hys, in0=exts[h][:, :, k:k + 4], scalar=wf(k), in1=hys,
                                                                                          op0=ALU.mult, op1=ALU.add),
                                r=[kes, "PAR", kh], w=[kh])
                act(lambda e: e.activation(out=hy[0][:, 0:NT], in_=hy[0][:, 0:NT], func=AF.Silu), r=["hy0"], w=["hy0"])
                dve(lambda e, j=j: e.tensor_tensor(out=actb[:, j, 0:NT], in0=hy[0][:, 0:NT], in1=hy[1][:, 0:NT], op=ALU.mult), r=["hy0", "hy1"], w=["actb"])
            if last:
                S.op("sp", lambda e, l=l: e.dma_start(out=o_ffp[l], in_=fhist[:, l]), reads=["fhist"], writes=[key("o_ffp")], dma=key("o"))
                S.op("sp", lambda e, l=l: e.dma_start(out=o_ffs[l], in_=ffs_out), reads=["ffs_out"], writes=[key("o_ffs")], dma=key("o"))
            wdv = w_down[l].rearrange("(k p) c -> p k c", p=128)
            for m in range(KT):
                s_ = ring_i[0] % 2
                ring_i[0] += 1
                wd = wring[s_].rearrange("p k c -> p (k c)")[:, 0:FT * 128].rearrange("p (k c) -> p k c", k=FT)
                kw = ["wr%da" % s_, "wr%db" % s_]
                S.op("pool", lambda e, wd=wd, m=m: e.dma_start(out=wd, in_=wdv[:, :, m * 128:(m + 1) * 128]), writes=kw, dma=kw[0], nobar=True)
                for gi, (c0, n) in enumerate(cols):
                    b = (m % 2) * 2 + gi
                    for k in range(FT):
                        pe(lambda e, b=b, k=k, wd=wd, c0=c0, n=n: e.matmul(psb[b][:, 0:n], wd[:, k, :], actb[:, k, c0:c0 + n], start=(k == 0), stop=(k == FT - 1)),
                           r=kw + ["actb"], w=[kps[b]])
                    dve(lambda e, b=b, m=m, c0=c0, n=n: e.scalar_tensor_tensor(out=X[:, m, c0:c0 + n], in0=X[:, m, c0:c0 + n], scalar=ALPHA,
                                                                              in1=psb[b][:, 0:n], op0=ALU.mult, op1=ALU.add),
                        r=[kX, kps[b]], w=[kX])
            layernorm(X, kX, cols, O_LN2G, O_LN2B, KT, X, kX, xb, kxb, extra=lnx)

            S.op("pool", lambda e, l=l, p=p: e.dma_start(out=pb, in_=pin[l, p]), writes=["pb"], dma="pb")
            S.op("pool", lambda e, l=l: e.dma_start(out=wpe, in_=w_pe[l].rearrange("(k p) c -> p k c", p=128)), writes=["wpe"], dma="wpe")
            for ci in range(4):
                wt, kw = load_w(w_gate[l][:, ci * 512:(ci + 1) * 512], KT, 512)
                for mi in range(4):
                    m = ci * 4 + mi
                    banks = matmul_tiles(wt, kw, mi, KT, xb, kxb, cols, m)
                    for gi, (c0, n) in enumerate(cols):
                        b = banks[gi]
                        be = 4 + gi
                        for k in range(2):
                            pe(lambda e, be=be, k=k, m=m, c0=c0, n=n: e.matmul(psb[be][:, 0:n], wpe[:, k, m * 128:(m + 1) * 128], pb[:, k, c0:c0 + n],
                                                                                 start=(k == 0), stop=(k == 1)),
                               r=["wpe", "pb"], w=[kps[be]])
                        act(lambda e, b=b, c0=c0, n=n: e.activation(out=sig[:, c0:c0 + n], in_=psb[b][:, 0:n], func=AF.Sigmoid), r=[kps[b]], w=["sig"])
                        dve(lambda e, be=be, c0=c0, n=n: e.tensor_tensor(out=sig[:, c0:c0 + n], in0=sig[:, c0:c0 + n], in1=psb[be][:, 0:n], op=ALU.mult),
                            r=["sig", kps[be]], w=["sig"])
                        dve(lambda e, m=m, c0=c0, n=n: e.scalar_tensor_tensor(out=X[:, m, c0:c0 + n], in0=X[:, m, c0:c0 + n], scalar=ALPHA,
                                                                             in1=sig[:, c0:c0 + n], op0=ALU.mult, op1=ALU.add),
                            r=[kX, "sig"], w=[kX])
            layernorm(X, kX, cols, O_LN3G, O_LN3B, KT, X, kX, xb, kxb, extra=lnx)
            barrier()
        S.op("sp", lambda e, p=p: e.dma_start(out=o_y[p], in_=X), reads=[kX], writes=[key("o_y")], dma="o_y")
    S.barrier(lambda e: e.memset(dummy, 0.0))
    S.emit(nc, st)
    st.close()
    return nc, S, A


def make_core_inputs(inp, b, s0, depth=DEPTH, npass=NPASS):
    f = np.float32
    L = depth
    m = {}
    xin = np.zeros((npass, 128, KT, NTMAX), f)
    pin = np.zeros((L, npass, 128, 2, NTMAX), f)
    for p in range(npass):
        xs = inp["x_prompt"][b, p * NPT:(p + 1) * NPT]
        xin[p, :, :, :NPT] = xs.reshape(NPT, KT, 128).transpose(2, 1, 0)
        ps = inp["p_prompt"][:L, b, p * NPT:(p + 1) * NPT]
        pin[:, p, :, :, :NPT] = ps.reshape(L, NPT, 2, 128).transpose(0, 3, 2, 1)
    xs = inp["x_sample"][s0:s0 + NSEQ].reshape(NSMP, D)
    xin[npass - 1, :, :, NPT:] = xs.reshape(NSMP, KT, 128).transpose(2, 1, 0)
    ps = inp["p_sample"][:L, s0:s0 + NSEQ].reshape(L, NSMP, 256)
    pin[:, npass - 1, :, :, NPT:] = ps.reshape(L, NSMP, 2, 128).transpose(0, 3, 2, 1)
    m["xin"], m["pin"] = xin, pin

    par = np.zeros((L, 128, NPAR), f)

    def put(off, v, ntile):
        par[:, :, off:off + ntile] = v.reshape(L, ntile, 128).transpose(0, 2, 1)
    put(O_LN1G, inp["ln1_g"][:L], 16); put(O_LN1B, inp["ln1_b"][:L], 16)
    put(O_LN2G, inp["ln2_g"][:L], 16); put(O_LN2B, inp["ln2_b"][:L], 16)
    put(O_LN3G, inp["ln3_g"][:L], 16); put(O_LN3B, inp["ln3_b"][:L], 16)
    put(O_CB, inp["conv_b"][:L], 8); put(O_CLG, inp["conv_ln_g"][:L], 8); put(O_CLB, inp["conv_ln_b"][:L], 8)
    put(O_D, inp["s5_d"][:L].reshape(L, 1024), 8)
    par[:, :, O_CW:O_CW + 248] = inp["conv_w"][:L].reshape(L, 31, 8, 128).transpose(0, 3, 2, 1).reshape(L, 128, 248)
    par[:, :, O_FCW:O_FCW + 264] = inp["ffn_conv_w"][:L].reshape(L, 3, 88, 128).transpose(0, 3, 2, 1).reshape(L, 128, 264)
    put(O_FCB, inp["ffn_conv_b"][:L], 88)
    m["par"] = par

    def lay_p(v):
        return v.reshape(L, 32, 2, 64).transpose(0, 2, 3, 1).reshape(L, 128, 32)
    ldt = np.broadcast_to(inp["s5_log_dt"][:L, :, None], (L, 64, 64))
    m["lamp"] = np.ascontiguousarray(np.stack([lay_p(inp["s5_lam_re"][:L]), lay_p(inp["s5_lam_im"][:L]), lay_p(ldt)], axis=2))
    ctp = np.zeros((L, 128, 2, 32, 32), f)
    for ri, nm in enumerate(("s5_c_re", "s5_c_im")):
        c = inp[nm][:L].reshape(L, 32, 2, 16, 64)
        for g2 in range(2):
            ctp[:, g2 * 64:(g2 + 1) * 64, ri, :, g2 * 16:(g2 + 1) * 16] = c[:, :, g2].transpose(0, 3, 1, 2)
    m["ctp"] = ctp
    def lay_f(v):
        w = v.reshape(L, 8, 4, 2, 64)
        w = w.transpose(0, 2, 1, 3, 4).reshape(L, 4, 1, 8, 128)
        return np.broadcast_to(w, (L, 4, 32, 8, 128)).reshape(L, 128, 8, 128)
    m["lamf"] = np.ascontiguousarray(np.stack([lay_f(inp["s5_lam_re"][:L]), lay_f(inp["s5_lam_im"][:L]), lay_f(ldt)], axis=2))
    btf = np.zeros((L, 4, 2, 16, 2, 8, 2, 64), f)
    for ri, nm in enumerate(("s5_b_re", "s5_b_im")):
        bb = inp[nm][:L].reshape(L, 8, 4, 2, 64, 16)
        for g2 in range(2):
            btf[:, :, g2, :, ri, :, g2, :] = bb[:, :, :, g2].transpose(0, 2, 4, 1, 3)
    m["btf"] = btf.reshape(L, 128, 2, 8, 128)
    sin_ = np.zeros((L, 128, 32, 2, NSEQ), f)
    for ri, nm in enumerate(("state_s5_re", "state_s5_im")):
        sv = inp[nm][:L, s0:s0 + NSEQ].reshape(L, NSEQ, 32, 2, 64)
        sin_[:, :, :, ri, :] = sv.transpose(0, 3, 4, 2, 1).reshape(L, 128, 32, NSEQ)
    m["sin"] = sin_
    cc = inp["cache_conv"][:L, s0:s0 + NSEQ]
    m["ccr"] = np.ascontiguousarray(cc)
    m["ccf"] = np.ascontiguousarray(cc.reshape(L, NSEQ, 30, 8, 128).transpose(0, 4, 3, 1, 2))
    cf = inp["cache_ffn_conv"][:L, s0:s0 + NSEQ]
    m["cff"] = np.ascontiguousarray(cf.reshape(L, NSEQ, 2, 88, 128).transpose(0, 4, 3, 1, 2))
    m["w_in"] = inp["w_in"][:L]; m["w_glu"] = inp["s5_w_glu"][:L]; m["w_out"] = inp["w_out"][:L]
    m["w_up"] = inp["ffn_w_up"][:L]; m["w_down"] = inp["ffn_w_down"][:L]
    m["w_pe"] = inp["pe_w"][:L]; m["w_gate"] = inp["pe_w_gate"][:L]
    return m


def unpack_core(res, depth=DEPTH, npass=NPASS):
    L = depth
    o = {}
    y = res["o_y"]
    yp = y[:, :, :, :NPT].transpose(0, 3, 2, 1).reshape(npass * NPT, D)
    ys = y[npass - 1, :, :, NPT:].transpose(2, 1, 0).reshape(NSEQ, 4, D)
    o["y_prompt"], o["y_sample"] = yp, ys
    sp = res["o_s5p"].reshape(L, 2, 64, 32, 2)
    sp = sp.transpose(0, 4, 3, 1, 2).reshape(L, 2, 64, 64)
    o["s5_re_prompt"], o["s5_im_prompt"] = sp[:, 0], sp[:, 1]
    ss = res["o_s5s"].reshape(L, 2, 64, 32, 2, NSEQ)
    ss = ss.transpose(0, 4, 5, 3, 1, 2).reshape(L, 2, NSEQ, 64, 64)
    o["s5_re_sample"], o["s5_im_sample"] = ss[:, 0], ss[:, 1]
    o["conv_prompt"] = res["o_cvp"].transpose(0, 3, 2, 1).reshape(L, 30, 1024)
    new = res["o_cvsn"].transpose(0, 3, 4, 2, 1).reshape(L, NSEQ, 4, 1024)
    o["conv_sample"] = np.concatenate([res["o_cvso"], new], axis=2)
    o["ffn_conv_prompt"] = res["o_ffp"].transpose(0, 3, 2, 1).reshape(L, 2, 11264)
    o["ffn_conv_sample"] = res["o_ffs"].transpose(0, 3, 4, 2, 1).reshape(L, NSEQ, 2, 11264)
    return o


_PROG = {}


def kernel(**inputs):
    inp = {k: np.asarray(v) for k, v in inputs.items()}
    if "nc" not in _PROG:
        _PROG["nc"] = build_program()[0]
    nc = _PROG["nc"]
    n = 8
    in_maps = [make_core_inputs(inp, c % 4, 16 * c) for c in range(n)]
    res = run_bass_kernel_spmd(nc, in_maps, core_ids=list(range(n)))
    outs = [unpack_core(r) for r in res.results]
    f = np.float32
    y_prompt = np.stack([outs[b]["y_prompt"] for b in range(4)]).astype(f)
    y_sample = np.concatenate([outs[c]["y_sample"] for c in range(n)]).astype(f)

    def pstack(nm):
        return np.stack([outs[b][nm] for b in range(4)], axis=1).astype(f)

    def sstack(nm):
        return np.concatenate([outs[c][nm] for c in range(n)], axis=1).astype(f)

    return (y_prompt, y_sample,
            pstack("s5_re_prompt"), pstack("s5_im_prompt"), pstack("conv_prompt"), pstack("ffn_conv_prompt"),
            sstack("s5_re_sample"), sstack("s5_im_sample"), sstack("conv_sample"), sstack("ffn_conv_sample"))
```

```python
import contextlib
import math
import types
import numpy as np
import concourse.bass as bass
import concourse.mybir as mybir
from concourse.bass_utils import run_bass_kernel_spmd

F32 = mybir.dt.float32
BF16 = mybir.dt.bfloat16
I32 = mybir.dt.int32
AF = mybir.ActivationFunctionType
ALU = mybir.AluOpType

DEPTH = 4
D = 2048
KT = 16
NPASS = 4
NPT = 512
NSEQ = 16
NSMP = 64
NTMAX = NPT + NSMP
NCHP = 128
DFF = 5632
FT = 44
ALPHA = (2.0 * DEPTH) ** 0.25
EPS = 1e-5
O_LN1G, O_LN1B, O_LN2G, O_LN2B, O_LN3G, O_LN3B = 0, 16, 32, 48, 64, 80
O_CB, O_CLG, O_CLB, O_D, O_CW, O_FCW, O_FCB = 96, 104, 112, 120, 128, 376, 640
NPAR = 728
ENGS = ("pe", "act", "dve", "pool", "sp")
PI = math.pi


class Op:
    __slots__ = ("eng", "fn", "deps", "dma", "signal", "ev", "pos", "epoch")

    def __init__(self, eng, fn, deps, dma):
        self.eng, self.fn, self.deps, self.dma = eng, fn, deps, dma
        self.signal, self.ev, self.pos, self.epoch = False, None, 0, 0


def _freeze(fn):
    if fn.__closure__ is None:
        return fn
    cells = []
    for c in fn.__closure__:
        try:
            cells.append(types.CellType(c.cell_contents))
        except ValueError:
            cells.append(c)
    g = types.FunctionType(fn.__code__, fn.__globals__, fn.__name__, fn.__defaults__, tuple(cells))
    g.__kwdefaults__ = fn.__kwdefaults__
    return g


class Sched:
    def __init__(self):
        self.ops = []
        self.last_writer = {}
        self.readers = {}
        self.queues = {e: [] for e in ENGS}
        self.bar = None
        self.dmas_since_bar = []
        self.epoch = 0

    def op(self, eng, fn, reads=(), writes=(), dma=None, nobar=False):
        deps = set()
        if self.bar is not None and not nobar:
            deps.add(self.bar)
        for k in reads:
            w = self.last_writer.get(k)
            if w is not None:
                deps.add(w)
        for k in writes:
            w = self.last_writer.get(k)
            if w is not None:
                deps.add(w)
            deps.update(self.readers.get(k, ()))
        idx = len(self.ops)
        o = Op(eng, _freeze(fn), deps, dma)
        o.pos = len(self.queues[eng])
        o.epoch = self.epoch
        self.ops.append(o)
        self.queues[eng].append(idx)
        if dma is not None:
            self.dmas_since_bar.append(idx)
        for k in reads:
            self.readers.setdefault(k, []).append(idx)
        for k in writes:
            self.last_writer[k] = idx
            self.readers[k] = []
        return idx

    def barrier(self, fn):
        deps = set(self.dmas_since_bar)
        for e in ENGS:
            if self.queues[e]:
                deps.add(self.queues[e][-1])
        if self.bar is not None:
            deps.add(self.bar)
        idx = len(self.ops)
        o = Op("dve", fn, deps, None)
        o.pos = len(self.queues["dve"])
        o.epoch = self.epoch
        self.ops.append(o)
        self.queues["dve"].append(idx)
        self.bar = idx
        self.dmas_since_bar = []
        self.last_writer = {k: v for k, v in self.last_writer.items() if k.startswith("wr")}
        self.readers = {k: v for k, v in self.readers.items() if k.startswith("wr")}
        self.epoch += 1

    def emit(self, nc, stack):
        ops = self.ops
        need = [[] for _ in ops]
        for i, o in enumerate(ops):
            for d in o.deps:
                p = ops[d]
                if p.dma is None and p.eng == o.eng:
                    if o.eng in ("pe", "sp"):
                        continue
                    if o.pos - p.pos > 3:
                        continue
                if p.dma is None:
                    p.signal = True
                need[i].append(d)
        cnt = {}
        dcnt = {}
        for o in ops:
            if o.dma is not None:
                dcnt[o.dma] = dcnt.get(o.dma, 0) + 16
                o.ev = (("dma", o.dma), dcnt[o.dma])
            elif o.signal:
                k = ("eng", o.eng, o.epoch // 12)
                cnt[k] = cnt.get(k, 0) + 1
                o.ev = (k, cnt[k])
        sems = {}
        for k in list(cnt.keys()) + [("dma", c) for c in dcnt]:
            sems[k] = stack.enter_context(nc.semaphore("s%d" % len(sems)))
        self.nsems = len(sems)
        block = stack.enter_context(nc.Block())

        def run(engname):
            def body(eng):
                waited = {}
                for idx in self.queues[engname]:
                    o = ops[idx]
                    w = {}
                    for d in need[idx]:
                        sk, v = ops[d].ev
                        if waited.get(sk, 0) >= v:
                            continue
                        if w.get(sk, 0) < v:
                            w[sk] = v
                    for sk, v in w.items():
                        eng.wait_ge(sems[sk], v)
                        waited[sk] = v
                    ins = o.fn(eng)
                    if ins is None:
                        continue
                    if o.dma is not None:
                        ins.then_inc(sems[o.ev[0]], 16)
                    elif o.signal:
                        ins.then_inc(sems[o.ev[0]], 1)
            return body

        block.tensor(run("pe"))
        block.scalar(run("act"))
        block.vector(run("dve"))
        block.gpsimd(run("pool"))
        block.sync(run("sp"))


class Arena:
    def __init__(self, t, nwords):
        self.t, self.n, self.off, self.peak = t, nwords, 0, 0

    def mark(self):
        return self.off

    def rewind(self, m):
        self.off = m

    def _take(self, nw):
        assert self.off + nw <= self.n, ("arena overflow", self.off, nw, self.n)
        ap = self.t[:, self.off:self.off + nw]
        self.off += nw
        self.peak = max(self.peak, self.off)
        return ap

    def f32(self, shape, dt=None):
        n = int(np.prod(shape))
        ap = self._take(n)
        if dt is not None:
            ap = ap.bitcast(dt)
        return self._shape(ap, shape)

    def bf16(self, shape):
        n = int(np.prod(shape))
        ap = self._take((n + 1) // 2).bitcast(BF16)[:, 0:n]
        return self._shape(ap, shape)

    @staticmethod
    def _shape(ap, shape):
        if len(shape) == 1:
            return ap
        names = " ".join("d%d" % i for i in range(len(shape)))
        kw = {"d%d" % i: s for i, s in enumerate(shape)}
        return ap.rearrange("p (%s) -> p %s" % (names, names), **kw)


def build_program(depth=DEPTH, npass=NPASS, debug=False):
    nc = bass.Bass("TRN2", target_bir_lowering=False)
    S = Sched()

    def din(name, shape):
        return nc.dram_tensor(name, list(shape), F32, kind="ExternalInput").ap()

    def dout(name, shape):
        return nc.dram_tensor(name, list(shape), F32, kind="ExternalOutput").ap()

    xin = din("xin", [npass, 128, KT, NTMAX])
    pin = din("pin", [depth, npass, 128, 2, NTMAX])
    par = din("par", [depth, 128, NPAR])
    lamp = din("lamp", [depth, 128, 3, 32])
    ctp = din("ctp", [depth, 128, 2, 32, 32])
    lamf = din("lamf", [depth, 128, 3, 8, 128])
    btf = din("btf", [depth, 128, 2, 8, 128])
    sin_ = din("sin", [depth, 128, 32, 2, NSEQ])
    ccf = din("ccf", [depth, 128, 8, NSEQ, 30])
    ccr = din("ccr", [depth, NSEQ, 30, 1024])
    cff = din("cff", [depth, 128, 88, NSEQ, 2])
    w_in = din("w_in", [depth, D, 3072])
    w_glu = din("w_glu", [depth, 1024, 1024])
    w_out = din("w_out", [depth, D, D])
    w_up = din("w_up", [depth, D, 2 * DFF])
    w_down = din("w_down", [depth, DFF, D])
    w_pe = din("w_pe", [depth, 256, D])
    w_gate = din("w_gate", [depth, D, D])

    o_y = dout("o_y", [npass, 128, KT, NTMAX])
    o_s5p = dout("o_s5p", [depth, 128, 32, 2])
    o_s5s = dout("o_s5s", [depth, 128, 32, 2, NSEQ])
    o_cvp = dout("o_cvp", [depth, 128, 8, 30])
    o_cvsn = dout("o_cvsn", [depth, 128, 8, NSEQ, 4])
    o_cvso = dout("o_cvso", [depth, NSEQ, 26, 1024])
    o_ffp = dout("o_ffp", [depth, 128, 88, 2])
    o_ffs = dout("o_ffs", [depth, 128, 88, NSEQ, 2])
    if debug:
        dbg_g5 = dout("dbg_g5", [128, 8, NTMAX])
        dbg_u = dout("dbg_u", [128, 8, NTMAX])
        dbg_mix = dout("dbg_mix", [128, KT, NTMAX])
        dbg_x1 = dout("dbg_x1", [128, KT, NTMAX])

    xw_scr = nc.dram_tensor("xw_scr", [depth, 2, 128, 4096], BF16, kind="Internal").ap()
    pt_scr = nc.dram_tensor("pt_scr", [depth, 128, 1408], F32, kind="Internal").ap()
    st = contextlib.ExitStack()
    NW = 52000
    arena_t = st.enter_context(nc.sbuf_tensor("arena", [128, NW], F32))
    A = Arena(arena_t, NW)
    ps_all = st.enter_context(nc.psum_tensor("ps_all", [128, 8 * 512], F32))
    psb = [ps_all[:, i * 512:(i + 1) * 512] for i in range(8)]
    kps = ["ps%d" % i for i in range(8)]

    uid = [0]

    def key(prefix):
        if prefix == "o":
            uid[0] += 1
            return "oshared%d" % (uid[0] % 4)
        uid[0] += 1
        return "%s#%d" % (prefix, uid[0])

    def dve(fn, r=(), w=()):
        S.op("dve", fn, r, w)

    def act(fn, r=(), w=()):
        S.op("act", fn, r, w)

    def pe(fn, r=(), w=()):
        S.op("pe", fn, r, w)

    X = A.f32([KT, NTMAX]);   kX = "X"
    xb = A.bf16([KT, NTMAX]); kxb = "xb"
    ones = A.f32([128])
    scar = A.f32([depth, 32, 2])
    chist = A.f32([depth, 8, 30])
    fhist = A.f32([depth, 88, 2])
    dummy = A.f32([2])
    consts = A.f32([4])
    PAR = A.f32([NPAR])
    wring = [A.bf16([KT, 256]) for _ in range(4)]
    sig = A.f32([NTMAX])
    lnt = tuple(A.f32([512]) for _ in range(4))
    ident = A.f32([128])
    base0_mark = A.mark()
    mixcat = A.bf16([KT, NTMAX])
    base_mark = A.mark()

    dve(lambda e: e.memset(ones, 1.0), w=["ones"])
    dve(lambda e: e.memset(consts[:, 0:1], EPS), w=["consts"])
    dve(lambda e: e.memset(consts[:, 1:2], PI / 2), w=["consts"])
    dve(lambda e: e.memset(scar, 0.0), w=["scar"])
    dve(lambda e: e.memset(chist, 0.0), w=["chist"])
    dve(lambda e: e.memset(fhist, 0.0), w=["fhist"])
    dve(lambda e: e.memset(dummy, 0.0), w=["dummy"])
    epsb = consts[:, 0:1]
    halfpi = consts[:, 1:2]
    ident_i = lnt[0][:, 0:128].bitcast(I32)
    S.op("pool", lambda e: e.iota(out=ident_i, pattern=[[1, 128]], base=0, channel_multiplier=-1), writes=["ident_i"])
    dve(lambda e: e.tensor_scalar(out=ident, in0=ident_i, scalar1=0.0, scalar2=None, op0=ALU.is_equal), r=["ident_i"], w=["ident"])
    S.barrier(lambda e: e.memset(dummy, 0.0))

    ring_i = [0]

    def load_slot(src_ap, nk):
        sl_ = ring_i[0] % 4
        ring_i[0] += 1
        dst = wring[sl_].rearrange("p k c -> p (k c)")[:, 0:nk * 256].rearrange("p (k c) -> p k c", k=nk)
        srcv = src_ap.rearrange("(k p) c -> p k c", p=128)
        S.op("pool", lambda e: e.dma_start(out=dst, in_=srcv), writes=["wr%d" % sl_], dma="wr%d" % sl_, nobar=True)
        return dst, "wr%d" % sl_

    def load_w(src_ap, nk, ncols):
        assert ncols == 512
        wa, ka = load_slot(src_ap[:, 0:256], nk)
        wb, kb = load_slot(src_ap[:, 256:512], nk)
        return [wa, wb], [ka, kb]

    def barrier():
        S.barrier(lambda e: e.memset(dummy, 0.0))

    def layernorm(src, ksrc, cols_list, goff, boff, ntile, dst_f32, kdst, dst_bf, kdstb, func=None, extra=None):
        mean, rstd, sq0, t10 = lnt
        sqs = [sq0, extra[0] if extra else sq0]
        t1s = [t10, extra[1] if extra else t10]
        ksq = ["ln_sq0", "ln_sq1" if extra else "ln_sq0"]
        kt1 = ["ln_t10", "ln_t11" if extra else "ln_t10"]
        inv = 1.0 / (ntile * 128)
        for (c0, n) in cols_list:
            ps_s, ps_q = psb[6], psb[7]
            for m in range(ntile):
                pe(lambda e, m=m: e.matmul(ps_s[:, 0:n], ones, src[:, m, c0:c0 + n], start=(m == 0), stop=(m == ntile - 1)),
                   r=[ksrc, "ones"], w=["ps6"])
            for m in range(ntile):
                sq = sqs[m % 2]
                act(lambda e, m=m, sq=sq: e.activation(out=sq[:, 0:n], in_=src[:, m, c0:c0 + n], func=AF.Square), r=[ksrc], w=[ksq[m % 2]])
                pe(lambda e, m=m, sq=sq: e.matmul(ps_q[:, 0:n], ones, sq[:, 0:n], start=(m == 0), stop=(m == ntile - 1)),
                   r=[ksq[m % 2], "ones"], w=["ps7"])
            t1 = t1s[0]
            act(lambda e: e.activation(out=mean[:, 0:n], in_=ps_s[:, 0:n], func=AF.Identity, scale=inv), r=["ps6"], w=["ln_mean"])
            dve(lambda e: e.tensor_tensor(out=t1[:, 0:n], in0=mean[:, 0:n], in1=mean[:, 0:n], op=ALU.mult), r=["ln_mean"], w=[kt1[0]])
            dve(lambda e: e.scalar_tensor_tensor(out=rstd[:, 0:n], in0=ps_q[:, 0:n], scalar=inv, in1=t1[:, 0:n], op0=ALU.mult, op1=ALU.subtract),
                r=["ps7", kt1[0]], w=["ln_rstd"])
            act(lambda e: e.activation(out=rstd[:, 0:n], in_=rstd[:, 0:n], func=AF.Ln, bias=epsb), r=["ln_rstd", "consts"], w=["ln_rstd"])
            act(lambda e: e.activation(out=rstd[:, 0:n], in_=rstd[:, 0:n], func=AF.Exp, scale=-0.5), r=["ln_rstd"], w=["ln_rstd"])
            for m in range(ntile):
                t1 = t1s[m % 2]
                k1 = kt1[m % 2]
                dve(lambda e, m=m, t1=t1: e.tensor_tensor(out=t1[:, 0:n], in0=src[:, m, c0:c0 + n], in1=mean[:, 0:n], op=ALU.subtract),
                    r=[ksrc, "ln_mean"], w=[k1])
                dve(lambda e, m=m, t1=t1: e.tensor_tensor(out=t1[:, 0:n], in0=t1[:, 0:n], in1=rstd[:, 0:n], op=ALU.mult),
                    r=[k1, "ln_rstd"], w=[k1])
                if dst_f32 is not None:
                    act(lambda e, m=m, t1=t1: e.activation(out=dst_f32[:, m, c0:c0 + n], in_=t1[:, 0:n], func=AF.Identity,
                                                           scale=PAR[:, goff + m:goff + m + 1], bias=PAR[:, boff + m:boff + m + 1]),
                        r=[k1, "PAR"], w=[kdst])
                    act(lambda e, m=m, t1=t1: e.activation(out=dst_bf[:, m, c0:c0 + n], in_=t1[:, 0:n], func=AF.Identity,
                                                           scale=PAR[:, goff + m:goff + m + 1], bias=PAR[:, boff + m:boff + m + 1]),
                        r=[k1, "PAR"], w=[kdstb])
                else:
                    act(lambda e, m=m, t1=t1: e.activation(out=dst_bf[:, m, c0:c0 + n], in_=t1[:, 0:n], func=func,
                                                           scale=PAR[:, goff + m:goff + m + 1], bias=PAR[:, boff + m:boff + m + 1]),
                        r=[k1, "PAR"], w=[kdstb])

    def matmul_tiles(wt, kw, mi, nk, rhs, krhs, cols, m):
        banks = []
        for gi, (c0, n) in enumerate(cols):
            b = (m % 2) * 2 + gi
            banks.append(b)
            for k in range(nk):
                pe(lambda e, b=b, k=k, c0=c0, n=n: e.matmul(psb[b][:, 0:n], wt[mi // 2][:, k, (mi % 2) * 128:(mi % 2 + 1) * 128], rhs[:, k, c0:c0 + n],
                                                            start=(k == 0), stop=(k == nk - 1)),
                   r=[kw[mi // 2], krhs], w=[kps[b]])
        return banks

    def pw_setup(tag, lr, li, ldt, F):
        kk = lambda s_: "%s_%s" % (tag, s_)
        c = dict(tag=tag, lr=lr, li=li, F=F, kk=kk)
        c["dt"] = A.f32([F]); c["th"] = A.f32([F]); c["ld"] = A.f32([F])
        c["t0"] = A.f32([F]); c["t1"] = A.f32([F]); c["t2"] = A.f32([F]); c["ti"] = A.f32([F], dt=I32)
        dt, th, ld = c["dt"], c["th"], c["ld"]
        act(lambda e: e.activation(out=dt, in_=ldt, func=AF.Exp), r=[kk("in")], w=[kk("dt")])
        dve(lambda e: e.tensor_tensor(out=th, in0=li, in1=dt, op=ALU.mult), r=[kk("in"), kk("dt")], w=[kk("th")])
        dve(lambda e: e.tensor_tensor(out=ld, in0=lr, in1=dt, op=ALU.mult), r=[kk("in"), kk("dt")], w=[kk("ld")])
        return c

    def pw_power(c, n, pr, pi_, kp):
        kk = c["kk"]
        th, ld, t0, t1, t2, ti = c["th"], c["ld"], c["t0"], c["t1"], c["t2"], c["ti"]
        dve(lambda e: e.tensor_scalar(out=t0, in0=th, scalar1=float(n), scalar2=None, op0=ALU.mult), r=[kk("th")], w=[kk("t0")])
        dve(lambda e: e.tensor_scalar(out=t1, in0=t0, scalar1=1.0 / (2 * PI), scalar2=None, op0=ALU.mult), r=[kk("t0")], w=[kk("t1")])
        dve(lambda e: e.tensor_copy(out=ti, in_=t1), r=[kk("t1")], w=[kk("ti")])
        dve(lambda e: e.tensor_copy(out=t1, in_=ti), r=[kk("ti")], w=[kk("t1")])
        dve(lambda e: e.scalar_tensor_tensor(out=t0, in0=t1, scalar=-2 * PI, in1=t0, op0=ALU.mult, op1=ALU.add), r=[kk("t1"), kk("t0")], w=[kk("t0")])
        dve(lambda e: e.tensor_scalar(out=t1, in0=t0, scalar1=PI, scalar2=-2 * PI, op0=ALU.is_gt, op1=ALU.mult), r=[kk("t0")], w=[kk("t1")])
        dve(lambda e: e.tensor_tensor(out=t0, in0=t0, in1=t1, op=ALU.add), r=[kk("t0"), kk("t1")], w=[kk("t0")])
        dve(lambda e: e.tensor_scalar(out=t1, in0=t0, scalar1=-PI, scalar2=2 * PI, op0=ALU.is_lt, op1=ALU.mult), r=[kk("t0")], w=[kk("t1")])
        dve(lambda e: e.tensor_tensor(out=t0, in0=t0, in1=t1, op=ALU.add), r=[kk("t0"), kk("t1")], w=[kk("t0")])
        dve(lambda e: e.tensor_scalar(out=t0, in0=t0, scalar1=-3.1415925, scalar2=3.1415925, op0=ALU.max, op1=ALU.min), r=[kk("t0")], w=[kk("t0")])
        act(lambda e: e.activation(out=t1, in_=t0, func=AF.Sin), r=[kk("t0")], w=[kk("t1")])
        dve(lambda e: e.scalar_tensor_tensor(out=t0, in0=t0, scalar=-1.0, in1=t0, op0=ALU.mult, op1=ALU.max), r=[kk("t0")], w=[kk("t0")])
        act(lambda e: e.activation(out=t0, in_=t0, func=AF.Sin, scale=-1.0, bias=halfpi), r=[kk("t0"), "consts"], w=[kk("t0")])
        act(lambda e: e.activation(out=t2, in_=ld, func=AF.Exp, scale=float(n)), r=[kk("ld")], w=[kk("t2")])
        dve(lambda e: e.tensor_tensor(out=pr, in0=t2, in1=t0, op=ALU.mult), r=[kk("t2"), kk("t0")], w=[kp])
        dve(lambda e: e.tensor_tensor(out=pi_, in0=t2, in1=t1, op=ALU.mult), r=[kk("t2"), kk("t1")], w=[kp])

    def pw_q(c, p1r, p1i, kp1, qr, qi, kq):
        kk = c["kk"]
        lr, li, t0, t1, t2 = c["lr"], c["li"], c["t0"], c["t1"], c["t2"]
        dve(lambda e: e.tensor_tensor(out=t0, in0=lr, in1=lr, op=ALU.mult), r=[kk("in")], w=[kk("t0")])
        dve(lambda e: e.tensor_tensor(out=t1, in0=li, in1=li, op=ALU.mult), r=[kk("in")], w=[kk("t1")])
        dve(lambda e: e.tensor_tensor(out=t0, in0=t0, in1=t1, op=ALU.add), r=[kk("t0"), kk("t1")], w=[kk("t0")])
        dve(lambda e: e.reciprocal(out=t0, in_=t0), r=[kk("t0")], w=[kk("t0")])
        dve(lambda e: e.tensor_scalar(out=t1, in0=p1r, scalar1=-1.0, scalar2=None, op0=ALU.add), r=[kp1], w=[kk("t1")])
        dve(lambda e: e.tensor_tensor(out=qr, in0=t1, in1=lr, op=ALU.mult), r=[kk("t1"), kk("in")], w=[kq])
        dve(lambda e: e.tensor_tensor(out=t2, in0=p1i, in1=li, op=ALU.mult), r=[kp1, kk("in")], w=[kk("t2")])
        dve(lambda e: e.tensor_tensor(out=qr, in0=qr, in1=t2, op=ALU.add), r=[kq, kk("t2")], w=[kq])
        dve(lambda e: e.tensor_tensor(out=qr, in0=qr, in1=t0, op=ALU.mult), r=[kq, kk("t0")], w=[kq])
        dve(lambda e: e.tensor_tensor(out=qi, in0=p1i, in1=lr, op=ALU.mult), r=[kp1, kk("in")], w=[kq])
        dve(lambda e: e.tensor_tensor(out=t2, in0=t1, in1=li, op=ALU.mult), r=[kk("t1"), kk("in")], w=[kk("t2")])
        dve(lambda e: e.tensor_tensor(out=qi, in0=qi, in1=t2, op=ALU.subtract), r=[kq, kk("t2")], w=[kq])
        dve(lambda e: e.tensor_tensor(out=qi, in0=qi, in1=t0, op=ALU.mult), r=[kq, kk("t0")], w=[kq])

    def s5_stage(l, p, NS, cols, u, g5, last):
        NSQ = NS // 4
        NC = NCHP + NSQ
        s5_mark = A.mark()
        PT = A.f32([1408])
        pt_off = [0]

        def ptake(shape):
            n_ = int(np.prod(shape))
            ap = PT[:, pt_off[0]:pt_off[0] + n_]
            pt_off[0] += n_
            return ap if len(shape) == 1 else ap.rearrange("p (a b) -> p a b", b=shape[1])
        PP = {}
        npi = {}
        Ta = {}
        Tb = {}
        pkeys = []
        for n in (1, 2, 3, 4):
            PP[n] = (ptake([32]), ptake([32]), "P_P%d" % n)
            npi[n] = ptake([32])
            pkeys += ["P_P%d" % n, "P_npi%d" % n]
        for n in (4, 8, 12, 16, 20, 24, 28, 32):
            Ta[n] = ptake([32, 2]); Tb[n] = ptake([32, 2])
            pkeys.append("P_T%d" % n)
        if p == 0:
            pm = A.mark()
            LP = A.f32([3, 32])
            S.op("sp", lambda e: e.dma_start(out=LP, in_=lamp[l]), writes=["P_in"], dma="P_in")
            cP = pw_setup("P", LP[:, 0], LP[:, 1], LP[:, 2], 32)
            tpr = A.f32([32]); tpi = A.f32([32])
            for n in (1, 2, 3, 4, 8, 12, 16, 20, 24, 28, 32):
                if n <= 4:
                    pr, pi_, kpn = PP[n]
                    t = npi[n]
                else:
                    pr, pi_, t = tpr, tpi, None
                    kpn = "P_Pt"
                pw_power(cP, n, pr, pi_, kpn)
                if t is not None:
                    dve(lambda e, t=t, pi_=pi_: e.tensor_scalar(out=t, in0=pi_, scalar1=-1.0, scalar2=None, op0=ALU.mult), r=[kpn], w=["P_npi%d" % n])
                if n >= 4:
                    ta, tb = Ta[n], Tb[n]
                    dve(lambda e, ta=ta, pr=pr: e.tensor_copy(out=ta[:, :, 0], in_=pr), r=[kpn], w=["P_T%d" % n])
                    dve(lambda e, ta=ta, pr=pr: e.tensor_copy(out=ta[:, :, 1], in_=pr), r=[kpn], w=["P_T%d" % n])
                    dve(lambda e, tb=tb, pi_=pi_: e.tensor_scalar(out=tb[:, :, 0], in0=pi_, scalar1=-1.0, scalar2=None, op0=ALU.mult), r=[kpn], w=["P_T%d" % n])
                    dve(lambda e, tb=tb, pi_=pi_: e.tensor_copy(out=tb[:, :, 1], in_=pi_), r=[kpn], w=["P_T%d" % n])
            S.op("sp", lambda e: e.dma_start(out=pt_scr[l], in_=PT), reads=pkeys, writes=[key("ptscr")], dma=key("o"))
        else:
            S.op("sp", lambda e: e.dma_start(out=PT, in_=pt_scr[l]), writes=pkeys, dma="PTld")
        CTb = A.bf16([2, 32, 32])
        S.op("pool", lambda e: e.dma_start(out=CTb, in_=ctp[l]), writes=["CTb"], dma="CTb")
        dve(lambda e: e.tensor_scalar(out=CTb[:, 1], in0=CTb[:, 1], scalar1=-1.0, scalar2=None, op0=ALU.mult), r=["CTb"], w=["CTb"])
        half_mark = A.mark()
        for hf in range(2):
            A.rewind(half_mark)
            if hf:
                barrier()
            XW = A.bf16([4, 2, 4, 128])
            fm = A.mark()
            if p == 0:
                LF = A.f32([3, 4, 128])
                BT = A.f32([2, 4, 128])
                S.op("sp", lambda e, hf=hf: e.dma_start(out=LF, in_=lamf[l][:, :, 4 * hf:4 * hf + 4, :]), writes=["F_in"], dma="F_in")
                S.op("sp", lambda e, hf=hf: e.dma_start(out=BT, in_=btf[l][:, :, 4 * hf:4 * hf + 4, :]), writes=["BT"], dma="BT")
                fl = lambda ap: ap.rearrange("p a b -> p (a b)")
                cF = pw_setup("F", fl(LF[:, 0]), fl(LF[:, 1]), fl(LF[:, 2]), 512)
                t0, t1, t2 = cF["t0"], cF["t1"], cF["t2"]
                btr, bti = fl(BT[:, 0]), fl(BT[:, 1])
                pr = A.f32([512]); pi_ = A.f32([512]); qr = A.f32([512]); qi = A.f32([512])
                vr = A.f32([512]); vi = A.f32([512])
                kq, kp, kv = "F_q", "F_P", "F_V"
                for n in range(4):
                    if n == 0:
                        pw_power(cF, 1, pr, pi_, kp)
                        pw_q(cF, pr, pi_, kp, qr, qi, kq)
                        ur, ui, ku = qr, qi, kq
                    else:
                        if n > 1:
                            pw_power(cF, n, pr, pi_, kp)
                        dve(lambda e: e.tensor_tensor(out=vr, in0=pr, in1=qr, op=ALU.mult), r=[kp, kq], w=[kv])
                        dve(lambda e: e.tensor_tensor(out=t0, in0=pi_, in1=qi, op=ALU.mult), r=[kp, kq], w=["F_t0"])
                        dve(lambda e: e.tensor_tensor(out=vr, in0=vr, in1=t0, op=ALU.subtract), r=[kv, "F_t0"], w=[kv])
                        dve(lambda e: e.tensor_tensor(out=vi, in0=pr, in1=qi, op=ALU.mult), r=[kp, kq], w=[kv])
                        dve(lambda e: e.tensor_tensor(out=t0, in0=pi_, in1=qr, op=ALU.mult), r=[kp, kq], w=["F_t0"])
                        dve(lambda e: e.tensor_tensor(out=vi, in0=vi, in1=t0, op=ALU.add), r=[kv, "F_t0"], w=[kv])
                        ur, ui, ku = vr, vi, kv
                    xr = fl(XW[:, n, 0]); xi = fl(XW[:, n, 1])
                    dve(lambda e, ur=ur: e.tensor_tensor(out=t1, in0=ur, in1=btr, op=ALU.mult), r=[ku, "BT"], w=["F_t1"])
                    dve(lambda e, ui=ui: e.tensor_tensor(out=t2, in0=ui, in1=bti, op=ALU.mult), r=[ku, "BT"], w=["F_t2"])
                    dve(lambda e, xr=xr: e.tensor_tensor(out=xr, in0=t1, in1=t2, op=ALU.subtract), r=["F_t1", "F_t2"], w=["XW"])
                    dve(lambda e, ur=ur: e.tensor_tensor(out=t1, in0=ur, in1=bti, op=ALU.mult), r=[ku, "BT"], w=["F_t1"])
                    dve(lambda e, ui=ui: e.tensor_tensor(out=t2, in0=ui, in1=btr, op=ALU.mult), r=[ku, "BT"], w=["F_t2"])
                    dve(lambda e, xi=xi: e.tensor_tensor(out=xi, in0=t1, in1=t2, op=ALU.add), r=["F_t1", "F_t2"], w=["XW"])
                S.op("sp", lambda e, hf=hf: e.dma_start(out=xw_scr[l, hf], in_=XW.rearrange("p a b c d -> p (a b c d)")), reads=["XW"], writes=[key("xwscr")], dma=key("o"))
            else:
                S.op("sp", lambda e, hf=hf: e.dma_start(out=XW.rearrange("p a b c d -> p (a b c d)"), in_=xw_scr[l, hf]), writes=["XW"], dma="XWld")
            A.rewind(fm)
            SP = A.f32([16, 2, NCHP + 1])
            XS = A.f32([16, 2, NSEQ])
            H0 = A.f32([16, 2, NSEQ])
            SO = A.f32([16, 2, NSEQ])
            CB = A.f32([16, 2, 17])
            st1 = A.f32([16, 2, 16]); st2 = A.f32([16, 2, 16])
            Hf = [A.bf16([4, 2, NCHP + NSEQ]) for _ in range(2)]
            tmpc = A.f32([3, 2, NCHP + NSEQ])
            sws = [A.f32([2, NCHP + NSEQ]) for _ in range(2)]
            ysb = sig
            dve(lambda e: e.memset(SP[:, :, :, 0:1], 0.0), r=["XW", "F_t1", "F_t2", "F_t0", "F_in", "BT", "F_q", "F_V", "F_P", "F_th", "F_ld", "F_dt", "F_ti"],
                w=["SP", "XS", "H0", "SO", "CB", "st1", "st2", "Hf0", "Hf1", "sw0", "sw1", "tmpc0", "tmpc1", "tmpc2"])
            dve(lambda e, hf=hf: e.tensor_copy(out=SP[:, :, :, 0], in_=scar[:, l, 16 * hf:16 * hf + 16, :]), r=["scar"], w=["SP"])
            if NS:
                S.op("sp", lambda e, hf=hf: e.dma_start(out=H0, in_=sin_[l][:, 16 * hf:16 * hf + 16]), writes=["H0"], dma="H0")
            p4r, p4i, kp4 = PP[4]
            sl = slice(16 * hf, 16 * hf + 16)
            for ii in range(16):
                i = 16 * hf + ii
                tl, ip = ii // 4, ii % 4
                t = i // 4
                rows = slice(32 * ip, 32 * ip + 32)
                for ri in range(2):
                    b = (ii * 2 + ri) % 2
                    for s in range(4):
                        pe(lambda e, b=b, s=s, ri=ri, tl=tl, t=t, rows=rows, ip=ip: e.matmul(
                            psb[b][:, 0:NC], XW[rows, 3 - s, ri, tl, :], u[rows, t, s:4 * NC:4],
                            start=(s == 0), stop=(s == 3), tile_position=(32 * ip, 0)),
                           r=["XW", "u"], w=[kps[b]])
                    act(lambda e, b=b, ii=ii, ri=ri: e.activation(out=SP[:, ii, ri, 1:NCHP + 1], in_=psb[b][:, 0:NCHP], func=AF.Copy),
                        r=[kps[b]], w=["SP"])
                    if NS:
                        act(lambda e, b=b, ii=ii, ri=ri: e.activation(out=XS[:, ii, ri, 0:NSQ], in_=psb[b][:, NCHP:NC], func=AF.Copy),
                            r=[kps[b]], w=["XS"])
            SPb = SP[:, :, :, 1:NCHP + 1].rearrange("p a r (b k) -> p a r b k", k=8)
            tab = {n_: Ta[n_][:, sl, :].unsqueeze(3).to_broadcast([128, 16, 2, 16]) for n_ in Ta}
            tbb = {n_: Tb[n_][:, sl, :].unsqueeze(3).to_broadcast([128, 16, 2, 16]) for n_ in Tb}
            for k in range(1, 8):
                dve(lambda e, k=k: e.tensor_tensor(out=st1, in0=SPb[:, :, :, :, k - 1], in1=tab[4], op=ALU.mult), r=["SP", "P_T4"], w=["st1"])
                dve(lambda e, k=k: e.tensor_tensor(out=st2, in0=SPb[:, :, ::-1, :, k - 1], in1=tbb[4], op=ALU.mult), r=["SP", "P_T4"], w=["st2"])
                dve(lambda e, k=k: e.tensor_tensor(out=SPb[:, :, :, :, k], in0=SPb[:, :, :, :, k], in1=st1, op=ALU.add), r=["SP", "st1"], w=["SP"])
                dve(lambda e, k=k: e.tensor_tensor(out=SPb[:, :, :, :, k], in0=SPb[:, :, :, :, k], in1=st2, op=ALU.add), r=["SP", "st2"], w=["SP"])
            dve(lambda e: e.tensor_copy(out=CB[:, :, :, 0], in_=SP[:, :, :, 0]), r=["SP"], w=["CB"])
            for b_ in range(16):
                dve(lambda e, b_=b_: e.tensor_tensor(out=st1[:, :, :, 0], in0=CB[:, :, :, b_], in1=Ta[32][:, sl, :], op=ALU.mult), r=["CB", "P_T32"], w=["st1"])
                dve(lambda e, b_=b_: e.tensor_tensor(out=st2[:, :, :, 0], in0=CB[:, :, ::-1, b_], in1=Tb[32][:, sl, :], op=ALU.mult), r=["CB", "P_T32"], w=["st2"])
                dve(lambda e, b_=b_: e.tensor_tensor(out=st1[:, :, :, 0], in0=st1[:, :, :, 0], in1=st2[:, :, :, 0], op=ALU.add), r=["st1", "st2"], w=["st1"])
                dve(lambda e, b_=b_: e.tensor_tensor(out=CB[:, :, :, b_ + 1], in0=SPb[:, :, :, b_, 7], in1=st1[:, :, :, 0], op=ALU.add), r=["SP", "st1", "CB"], w=["CB"])
            for k in range(8):
                n_ = 4 * (k + 1)
                dve(lambda e, k=k, n_=n_: e.tensor_tensor(out=st1, in0=CB[:, :, :, 0:16], in1=tab[n_], op=ALU.mult), r=["CB", "P_T%d" % n_], w=["st1"])
                dve(lambda e, k=k, n_=n_: e.tensor_tensor(out=st2, in0=CB[:, :, ::-1, 0:16], in1=tbb[n_], op=ALU.mult), r=["CB", "P_T%d" % n_], w=["st2"])
                dve(lambda e, k=k: e.tensor_tensor(out=SPb[:, :, :, :, k], in0=SPb[:, :, :, :, k], in1=st1, op=ALU.add), r=["SP", "st1"], w=["SP"])
                dve(lambda e, k=k: e.tensor_tensor(out=SPb[:, :, :, :, k], in0=SPb[:, :, :, :, k], in1=st2, op=ALU.add), r=["SP", "st2"], w=["SP"])
            dve(lambda e, hf=hf: e.tensor_copy(out=scar[:, l, 16 * hf:16 * hf + 16, :], in_=SP[:, :, :, NCHP]), r=["SP"], w=["scar"])
            if NS:
                for ri in range(2):
                    for ii in range(16):
                        i = 16 * hf + ii
                        dve(lambda e, ii=ii, i=i, ri=ri: e.scalar_tensor_tensor(out=SO[:, ii, ri, :], in0=H0[:, ii, ri, :], scalar=p4r[:, i:i + 1],
                                                                               in1=XS[:, ii, ri, :], op0=ALU.mult, op1=ALU.add),
                            r=["H0", "XS", kp4], w=["SO"])
                        sc = npi[4] if ri == 0 else p4i
                        dve(lambda e, ii=ii, i=i, ri=ri, sc=sc: e.scalar_tensor_tensor(out=SO[:, ii, ri, :], in0=H0[:, ii, 1 - ri, :], scalar=sc[:, i:i + 1],
                                                                                      in1=SO[:, ii, ri, :], op0=ALU.mult, op1=ALU.add),
                            r=["H0", kp4, "P_npi4", "SO"], w=["SO"])
                S.op("sp", lambda e, hf=hf: e.dma_start(out=o_s5s[l][:, 16 * hf:16 * hf + 16], in_=SO), reads=["SO"], writes=[key("o_s5s")], dma=key("o"))
            def pair_ctx(ii):
                tl, ip = ii // 4, ii % 4
                return dict(ii=ii, tl=tl, ip=ip, t=4 * hf + tl, i=16 * hf + ii, rows=slice(32 * ip, 32 * ip + 32),
                            hb=Hf[ii % 2], khf="Hf%d" % (ii % 2), bk0=2 * (ii % 2), ybase=4 + (tl % 2) * 2)

            def emit_hloc(c_):
                ii, tl, ip, t, i, rows, hb, khf, bk0, ybase = (c_[k_] for k_ in ("ii", "tl", "ip", "t", "i", "rows", "hb", "khf", "bk0", "ybase"))
                for ri in range(2):
                    b = bk0 + ri
                    for j in range(3):
                        g0 = j * 144
                        for s_ in range(j + 1):
                            pe(lambda e, b=b, s_=s_, j=j, ri=ri, tl=tl, t=t, rows=rows, ip=ip, g0=g0: e.matmul(
                                psb[b][:, g0:g0 + NC], XW[rows, j - s_, ri, tl, :], u[rows, t, s_:4 * NC:4],
                                start=(s_ == 0), stop=(s_ == j), tile_position=(32 * ip, 0)),
                               r=["XW", "u"], w=[kps[b]])

            def emit_post(c_):
                ii, tl, ip, t, i, rows, hb, khf, bk0, ybase = (c_[k_] for k_ in ("ii", "tl", "ip", "t", "i", "rows", "hb", "khf", "bk0", "ybase"))
                sw = sws[ii % 2]
                ksw = "sw%d" % (ii % 2)
                act(lambda e, ii=ii, sw=sw: e.activation(out=sw[:, 0, 0:NCHP], in_=SP[:, ii, 1, 0:NCHP], func=AF.Identity, scale=-1.0), r=["SP"], w=[ksw])
                act(lambda e, ii=ii, sw=sw: e.activation(out=sw[:, 1, 0:NCHP], in_=SP[:, ii, 0, 0:NCHP], func=AF.Copy), r=["SP"], w=[ksw])
                if NS:
                    act(lambda e, ii=ii, sw=sw: e.activation(out=sw[:, 0, NCHP:NC], in_=H0[:, ii, 1, :], func=AF.Identity, scale=-1.0), r=["H0"], w=[ksw])
                    act(lambda e, ii=ii, sw=sw: e.activation(out=sw[:, 1, NCHP:NC], in_=H0[:, ii, 0, :], func=AF.Copy), r=["H0"], w=[ksw])
                pbank = ps_all[:, bk0 * 512:(bk0 + 2) * 512].rearrange("p (r c) -> p r c", r=2)
                for phase_ in range(2):
                    for j in range(3):
                        pjr, pji, kpj = PP[j + 1]
                        g0 = j * 144
                        groups = [(0, NCHP, SP[:, ii, :, 0:NCHP], "SP")]
                        if NS:
                            groups.append((NCHP, NSQ, H0[:, ii, :, :], "H0"))
                        for (c0, n, sa, ks) in groups:
                            if phase_ == 0:
                                dve(lambda e, c0=c0, n=n, sa=sa, i=i, pjr=pjr, j=j, g0=g0, pbank=pbank: e.scalar_tensor_tensor(
                                    out=tmpc[:, j, :, c0:c0 + n], in0=sa, scalar=pjr[:, i:i + 1], in1=pbank[:, :, g0 + c0:g0 + c0 + n], op0=ALU.mult, op1=ALU.add),
                                    r=[ks, kps[bk0], kps[bk0 + 1], kpj], w=["tmpc%d" % j])
                            else:
                                dve(lambda e, c0=c0, n=n, sw=sw, i=i, pji=pji, hb=hb, j=j: e.scalar_tensor_tensor(
                                    out=hb[:, j, :, c0:c0 + n], in0=sw[:, :, c0:c0 + n], scalar=pji[:, i:i + 1], in1=tmpc[:, j, :, c0:c0 + n], op0=ALU.mult, op1=ALU.add),
                                    r=[ksw, "tmpc%d" % j, kpj], w=[khf])
                for ri in range(2):
                    act(lambda e, ii=ii, ri=ri, hb=hb: e.activation(out=hb[:, 3, ri, 0:NCHP], in_=SP[:, ii, ri, 1:NCHP + 1], func=AF.Copy), r=["SP"], w=[khf])
                    if NS:
                        act(lambda e, ii=ii, ri=ri, hb=hb: e.activation(out=hb[:, 3, ri, NCHP:NC], in_=SO[:, ii, ri, :], func=AF.Copy), r=["SO"], w=[khf])

            def emit_y(c_):
                ii, tl, ip, t, i, rows, hb, khf, bk0, ybase = (c_[k_] for k_ in ("ii", "tl", "ip", "t", "i", "rows", "hb", "khf", "bk0", "ybase"))
                for j in range(4):
                    for ri in range(2):
                        pe(lambda e, ip=ip, j=j, ri=ri, i=i, hb=hb, ybase=ybase: e.matmul(
                            psb[ybase][32 * ip:32 * ip + 32, j:NPT:4], CTb[:, ri, i, :], hb[:, j, ri, 0:NCHP],
                            start=(ri == 0), stop=(ri == 1), tile_position=(0, 32 * ip)),
                           r=["CTb", khf], w=[kps[ybase]])
                        if NS:
                            pe(lambda e, ip=ip, j=j, ri=ri, i=i, hb=hb, ybase=ybase: e.matmul(
                                psb[ybase + 1][32 * ip:32 * ip + 32, j:NS:4], CTb[:, ri, i, :], hb[:, j, ri, NCHP:NC],
                                start=(ri == 0), stop=(ri == 1), tile_position=(0, 32 * ip)),
                               r=["CTb", khf], w=[kps[ybase + 1]])

            def emit_evac(tl):
                t = 4 * hf + tl
                ybase = 4 + (tl % 2) * 2
                for gi, (c0, n) in enumerate(cols):
                    dve(lambda e, t=t, c0=c0, n=n, b=ybase + gi: e.scalar_tensor_tensor(
                        out=ysb[:, c0:c0 + n], in0=u[:, t, c0:c0 + n], scalar=PAR[:, O_D + t:O_D + t + 1], in1=psb[b][:, 0:n], op0=ALU.mult, op1=ALU.add),
                        r=["u", "PAR", kps[ybase + gi]], w=["sig"])
                    act(lambda e, t=t, c0=c0, n=n: e.activation(out=g5[:, t, c0:c0 + n], in_=ysb[:, c0:c0 + n], func=AF.Gelu_apprx_tanh),
                        r=["sig"], w=["g5"])

            ctxs = [pair_ctx(ii) for ii in range(16)]
            emit_hloc(ctxs[0])
            for ii in range(16):
                if ii + 1 < 16:
                    emit_hloc(ctxs[ii + 1])
                emit_post(ctxs[ii])
                emit_y(ctxs[ii])
                if ii % 4 == 3:
                    emit_evac(ii // 4)
        if last:
            S.op("sp", lambda e: e.dma_start(out=o_s5p[l], in_=scar[:, l]), reads=["scar"], writes=[key("o_s5p")], dma=key("o"))
        A.rewind(s5_mark)

    for p in range(npass):
        last = (p == npass - 1)
        NS = NSMP if last else 0
        NT = NPT + NS
        cols = [(0, NPT)] + ([(NPT, NS)] if NS else [])
        S.op("sp", lambda e, p=p: e.dma_start(out=X, in_=xin[p]), writes=[kX], dma="X")
        dve(lambda e: e.tensor_copy(out=xb, in_=X), r=[kX], w=[kxb])
        for l in range(depth):
            S.op("sp", lambda e, l=l: e.dma_start(out=PAR, in_=par[l]), writes=["PAR"], dma="PAR")
            A.rewind(base_mark)
            u = A.bf16([8, NTMAX])
            g5 = A.bf16([8, NTMAX])
            for ci in range(2):
                wt, kw = load_w(w_in[l][:, ci * 512:(ci + 1) * 512], KT, 512)
                for mi in range(4):
                    m = ci * 4 + mi
                    banks = matmul_tiles(wt, kw, mi, KT, xb, kxb, cols, m)
                    for gi, (c0, n) in enumerate(cols):
                        b = banks[gi]
                        act(lambda e, b=b, m=m, c0=c0, n=n: e.activation(out=u[:, m, c0:c0 + n], in_=psb[b][:, 0:n], func=AF.Copy),
                            r=[kps[b]], w=["u"])
            s5_stage(l, p, NS, cols, u, g5, last)
            if debug and last and l == 0:
                S.op("pool", lambda e: e.dma_start(out=dbg_g5, in_=g5), reads=["g5"], writes=[key("o_dbg")], dma=key("o"))
                S.op("pool", lambda e: e.dma_start(out=dbg_u, in_=u), reads=["u"], writes=[key("o_dbg")], dma=key("o"))
            for ci in range(2):
                wt, kw = load_w(w_glu[l][:, ci * 512:(ci + 1) * 512], 8, 512)
                for mi in range(4):
                    m = ci * 4 + mi
                    banks = matmul_tiles(wt, kw, mi, 8, g5, "g5", cols, m)
                    for gi, (c0, n) in enumerate(cols):
                        b = banks[gi]
                        act(lambda e, b=b, c0=c0, n=n: e.activation(out=sig[:, c0:c0 + n], in_=psb[b][:, 0:n], func=AF.Sigmoid),
                            r=[kps[b]], w=["sig"])
                        dve(lambda e, m=m, c0=c0, n=n: e.tensor_tensor(out=mixcat[:, m, c0:c0 + n], in0=g5[:, m, c0:c0 + n], in1=sig[:, c0:c0 + n], op=ALU.mult),
                            r=["g5", "sig"], w=["mixcat"])
            barrier()

            A.rewind(base_mark)
            cv = A.f32([8, NTMAX])
            convy = A.f32([8, NTMAX])
            cbp = A.bf16([8, 30 + NPT])
            cbs = A.bf16([8, NSEQ, 34])
            csn = A.f32([8, NSEQ, 4])
            dg = [A.bf16([31, 128]) for _ in range(2)]
            lnx = (A.f32([512]), A.f32([512]))
            dve(lambda e, l=l: e.tensor_copy(out=cbp[:, :, 0:30], in_=chist[:, l]), r=["chist"], w=["cbp"])
            if NS:
                for hh in range(2):
                    S.op("pool", lambda e, l=l, hh=hh: e.dma_start(out=cbs[:, 4 * hh:4 * hh + 4, :, 0:30], in_=ccf[l][:, 4 * hh:4 * hh + 4]),
                         writes=["cbs"], dma="cbs%d" % hh)
            for ci in range(2, 6):
                wt, kw = load_w(w_in[l][:, ci * 512:(ci + 1) * 512], KT, 512)
                for mi in range(4):
                    m = ci * 4 + mi
                    banks = matmul_tiles(wt, kw, mi, KT, xb, kxb, cols, m)
                    for gi, (c0, n) in enumerate(cols):
                        b = banks[gi]
                        if m < 16:
                            act(lambda e, b=b, m=m, c0=c0, n=n: e.activation(out=cv[:, m - 8, c0:c0 + n], in_=psb[b][:, 0:n], func=AF.Copy),
                                r=[kps[b]], w=["cv"])
                        else:
                            mm = m - 16
                            act(lambda e, b=b, c0=c0, n=n: e.activation(out=sig[:, c0:c0 + n], in_=psb[b][:, 0:n], func=AF.Sigmoid),
                                r=[kps[b]], w=["sig"])
                            if gi == 0:
                                dve(lambda e, mm=mm: e.tensor_tensor(out=cbp[:, mm, 30:30 + NPT], in0=cv[:, mm, 0:NPT], in1=sig[:, 0:NPT], op=ALU.mult),
                                    r=["cv", "sig"], w=["cbp"])
                                dve(lambda e, mm=mm, l=l: e.tensor_tensor(out=chist[:, l, mm, :], in0=cv[:, mm, NPT - 30:NPT], in1=sig[:, NPT - 30:NPT], op=ALU.mult),
                                    r=["cv", "sig", "cbp"], w=["chist"])
                            else:
                                dve(lambda e, mm=mm: e.tensor_tensor(out=csn[:, mm],
                                                                     in0=cv[:, mm, NPT:NPT + NS].rearrange("p (s t) -> p s t", t=4),
                                                                     in1=sig[:, NPT:NPT + NS].rearrange("p (s t) -> p s t", t=4), op=ALU.mult),
                                    r=["cv", "sig"], w=["csn"])
                                dve(lambda e, mm=mm: e.tensor_copy(out=cbs[:, mm, :, 30:34], in_=csn[:, mm]), r=["csn"], w=["cbs"])
            if last:
                S.op("sp", lambda e, l=l: e.dma_start(out=o_cvp[l], in_=chist[:, l]), reads=["chist"], writes=[key("o_cvp")], dma=key("o"))
                S.op("sp", lambda e, l=l: e.dma_start(out=o_cvsn[l], in_=csn), reads=["csn"], writes=[key("o_cvsn")], dma=key("o"))
                S.op("sp", lambda e, l=l: e.dma_start(out=o_cvso[l], in_=ccr[l][:, 4:30, :]), writes=[key("o_cvso")], dma=key("o"))
            for m in range(8):
                d_ = dg[m % 2]
                kd = "dg%d" % (m % 2)
                for k in range(31):
                    if k % 2 == 0:
                        act(lambda e, d_=d_, m=m, k=k: e.activation(out=d_[:, k, :], in_=ident, func=AF.Identity,
                                                                    scale=PAR[:, O_CW + m * 31 + k:O_CW + m * 31 + k + 1]),
                            r=["ident", "PAR"], w=[kd + "a"])
                    else:
                        dve(lambda e, d_=d_, m=m, k=k: e.tensor_scalar(out=d_[:, k, :], in0=ident, scalar1=PAR[:, O_CW + m * 31 + k:O_CW + m * 31 + k + 1],
                                                                       scalar2=None, op0=ALU.mult),
                            r=["ident", "PAR"], w=[kd + "b"])
                for gi, (c0, n) in enumerate(cols):
                    b = (m % 2) * 2 + gi
                    for k in range(31):
                        if gi == 0:
                            pe(lambda e, b=b, k=k, m=m, d_=d_: e.matmul(psb[b][:, 0:NPT], d_[:, k, :], cbp[:, m, k:k + NPT], start=(k == 0), stop=(k == 30)),
                               r=[kd + "a", kd + "b", "cbp"], w=[kps[b]])
                        else:
                            pe(lambda e, b=b, k=k, m=m, d_=d_: e.matmul(psb[b][:, 0:NS].rearrange("p (s t) -> p s t", t=4), d_[:, k, :], cbs[:, m, :, k:k + 4],
                                                                        start=(k == 0), stop=(k == 30)),
                               r=[kd + "a", kd + "b", "cbs"], w=[kps[b]])
                    act(lambda e, b=b, m=m, c0=c0, n=n: e.activation(out=convy[:, m, c0:c0 + n], in_=psb[b][:, 0:n], func=AF.Identity,
                                                                    bias=PAR[:, O_CB + m:O_CB + m + 1]),
                        r=[kps[b], "PAR"], w=["convy"])
            layernorm(convy, "convy", cols, O_CLG, O_CLB, 8, None, None, mixcat[:, 8:16, :], "mixcat", func=AF.Silu, extra=lnx)

            def proj_residual(wsrc, nk, rhs, krhs):
                for ci in range(4):
                    wt, kw = load_w(wsrc[:, ci * 512:(ci + 1) * 512], nk, 512)
                    for mi in range(4):
                        m = ci * 4 + mi
                        banks = matmul_tiles(wt, kw, mi, nk, rhs, krhs, cols, m)
                        for gi, (c0, n) in enumerate(cols):
                            b = banks[gi]
                            dve(lambda e, b=b, m=m, c0=c0, n=n: e.scalar_tensor_tensor(out=X[:, m, c0:c0 + n], in0=X[:, m, c0:c0 + n], scalar=ALPHA,
                                                                                      in1=psb[b][:, 0:n], op0=ALU.mult, op1=ALU.add),
                                r=[kX, kps[b]], w=[kX])
            if debug and last and l == 0:
                S.op("pool", lambda e: e.dma_start(out=dbg_mix, in_=mixcat), reads=["mixcat"], writes=[key("o_dbg")], dma=key("o"))
            proj_residual(w_out[l], KT, mixcat, "mixcat")
            layernorm(X, kX, cols, O_LN1G, O_LN1B, KT, X, kX, xb, kxb, extra=lnx)
            if debug and last and l == 0:
                S.op("sp", lambda e: e.dma_start(out=dbg_x1, in_=X), reads=[kX], writes=[key("o_dbg")], dma=key("o"))
            barrier()

            A.rewind(base0_mark)
            actb = A.bf16([FT, NTMAX])
            extp = [A.f32([2 + NPT]) for _ in range(2)]
            exts = [A.f32([NSEQ, 6]) for _ in range(2)]
            hy = [A.f32([NTMAX]) for _ in range(2)]
            cfs = [A.f32([NSEQ, 2]) for _ in range(2)]
            ffs_out = A.f32([88, NSEQ, 2])
            pb = A.bf16([2, NTMAX])
            wpe = A.bf16([2, D])
            lnx = (A.f32([512]), A.f32([512]))
            for j in range(FT):
                jl = j % 2
                if jl == 0:
                    wg, kwa = load_slot(w_up[l][:, j * 128:(j + 2) * 128], KT)
                    wvl, kwb = load_slot(w_up[l][:, DFF + j * 128:DFF + (j + 2) * 128], KT)
                    wts = [wg, wvl]
                for h in range(2):
                    ft = h * FT + j
                    kwh = [kwa, kwb][h]
                    for gi, (c0, n) in enumerate(cols):
                        b = jl * 4 + h * 2 + gi
                        for k in range(KT):
                            pe(lambda e, b=b, k=k, h=h, c0=c0, n=n, wt=wts[h], jl=jl: e.matmul(psb[b][:, 0:n], wt[:, k, jl * 128:(jl + 1) * 128], xb[:, k, c0:c0 + n],
                                                                                     start=(k == 0), stop=(k == KT - 1)),
                               r=[kwh, kxb], w=[kps[b]])
                    wf = lambda k, ft=ft: PAR[:, O_FCW + ft * 3 + k:O_FCW + ft * 3 + k + 1]
                    bf = PAR[:, O_FCB + ft:O_FCB + ft + 1]
                    ke = "extp%d" % h
                    kh = "hy%d" % h
                    dve(lambda e, h=h, ft=ft, l=l: e.tensor_copy(out=extp[h][:, 0:2], in_=fhist[:, l, ft, :]), r=["fhist"], w=[ke])
                    act(lambda e, h=h, jl=jl: e.activation(out=extp[h][:, 2:2 + NPT], in_=psb[jl * 4 + h * 2][:, 0:NPT], func=AF.Copy), r=[kps[jl * 4 + h * 2]], w=[ke])
                    dve(lambda e, h=h, ft=ft, l=l: e.tensor_copy(out=fhist[:, l, ft, :], in_=extp[h][:, NPT:NPT + 2]), r=[ke], w=["fhist"])
                    dve(lambda e, h=h, wf=wf, bf=bf: e.tensor_scalar(out=hy[h][:, 0:NPT], in0=extp[h][:, 2:2 + NPT], scalar1=wf(2), scalar2=bf, op0=ALU.mult, op1=ALU.add),
                        r=[ke, "PAR"], w=[kh])
                    for k in range(2):
                        dve(lambda e, h=h, k=k, wf=wf: e.scalar_tensor_tensor(out=hy[h][:, 0:NPT], in0=extp[h][:, k:k + NPT], scalar=wf(k), in1=hy[h][:, 0:NPT],
                                                                             op0=ALU.mult, op1=ALU.add),
                            r=[ke, "PAR", kh], w=[kh])
                    if NS:
                        kes = "exts%d" % h
                        kcf = "cfs%d" % h
                        hys = hy[h][:, NPT:NPT + NS].rearrange("p (s t) -> p s t", t=4)
                        S.op("sp", lambda e, h=h, ft=ft, l=l: e.dma_start(out=cfs[h], in_=cff[l][:, ft]), writes=[kcf], dma=kcf)
                        dve(lambda e, h=h: e.tensor_copy(out=exts[h][:, :, 0:2], in_=cfs[h]), r=[kcf], w=[kes])
                        act(lambda e, h=h, jl=jl: e.activation(out=exts[h][:, :, 2:6], in_=psb[jl * 4 + h * 2 + 1][:, 0:NS].rearrange("p (s t) -> p s t", t=4), func=AF.Copy),
                            r=[kps[jl * 4 + h * 2 + 1]], w=[kes])
                        dve(lambda e, h=h, ft=ft: e.tensor_copy(out=ffs_out[:, ft], in_=exts[h][:, :, 4:6]), r=[kes], w=["ffs_out"])
                        dve(lambda e, h=h, wf=wf, bf=bf, hys=hys: e.tensor_scalar(out=hys, in0=exts[h][:, :, 2:6], scalar1=wf(2), scalar2=bf, op0=ALU.mult, op1=ALU.add),
                            r=[kes, "PAR"], w=[kh])
                        for k in range(2):
                            dve(lambda e, h=h, k=k, wf=wf, hys=hys: e.scalar_tensor_tensor(out=hys, in0=exts[h][:, :, k:k + 4], scalar=wf(k), in1=hys,
                                                                                          op0=ALU.mult, op1=ALU.add),
                                r=[kes, "PAR", kh], w=[kh])
                act(lambda e: e.activation(out=hy[0][:, 0:NT], in_=hy[0][:, 0:NT], func=AF.Silu), r=["hy0"], w=["hy0"])
                dve(lambda e, j=j: e.tensor_tensor(out=actb[:, j, 0:NT], in0=hy[0][:, 0:NT], in1=hy[1][:, 0:NT], op=ALU.mult), r=["hy0", "hy1"], w=["actb"])
            if last:
                S.op("sp", lambda e, l=l: e.dma_start(out=o_ffp[l], in_=fhist[:, l]), reads=["fhist"], writes=[key("o_ffp")], dma=key("o"))
                S.op("sp", lambda e, l=l: e.dma_start(out=o_ffs[l], in_=ffs_out), reads=["ffs_out"], writes=[key("o_ffs")], dma=key("o"))
            for mp in range(KT // 2):
                base = (mp % 2) * 4
                pieces = []
                for q in range(3):
                    nkq = min(16, FT - 16 * q)
                    wd, kwd = load_slot(w_down[l][16 * q * 128:(16 * q + nkq) * 128, mp * 256:(mp + 1) * 256], nkq)
                    pieces.append((wd, kwd, nkq))
                for q, (wd, kwd, nkq) in enumerate(pieces):
                    for mt in range(2):
                        for gi, (c0, n) in enumerate(cols):
                            b = base + mt * 2 + gi
                            for kk in range(nkq):
                                pe(lambda e, b=b, kk=kk, wd=wd, c0=c0, n=n, q=q, mt=mt, nkq=nkq: e.matmul(
                                    psb[b][:, 0:n], wd[:, kk, mt * 128:(mt + 1) * 128], actb[:, 16 * q + kk, c0:c0 + n],
                                    start=(q == 0 and kk == 0), stop=(q == 2 and kk == nkq - 1)),
                                   r=[kwd, "actb"], w=[kps[b]])
                for mt in range(2):
                    m = mp * 2 + mt
                    for gi, (c0, n) in enumerate(cols):
                        b = base + mt * 2 + gi
                        dve(lambda e, b=b, m=m, c0=c0, n=n: e.scalar_tensor_tensor(out=X[:, m, c0:c0 + n], in0=X[:, m, c0:c0 + n], scalar=ALPHA,
                                                                                  in1=psb[b][:, 0:n], op0=ALU.mult, op1=ALU.add),
                            r=[kX, kps[b]], w=[kX])
            layernorm(X, kX, cols, O_LN2G, O_LN2B, KT, X, kX, xb, kxb, extra=lnx)

            S.op("pool", lambda e, l=l, p=p: e.dma_start(out=pb, in_=pin[l, p]), writes=["pb"], dma="pb")
            S.op("pool", lambda e, l=l: e.dma_start(out=wpe, in_=w_pe[l].rearrange("(k p) c -> p k c", p=128)), writes=["wpe"], dma="wpe")
            for ci in range(4):
                wt, kw = load_w(w_gate[l][:, ci * 512:(ci + 1) * 512], KT, 512)
                for mi in range(4):
                    m = ci * 4 + mi
                    banks = matmul_tiles(wt, kw, mi, KT, xb, kxb, cols, m)
                    for gi, (c0, n) in enumerate(cols):
                        b = banks[gi]
                        be = 4 + gi
                        for k in range(2):
                            pe(lambda e, be=be, k=k, m=m, c0=c0, n=n: e.matmul(psb[be][:, 0:n], wpe[:, k, m * 128:(m + 1) * 128], pb[:, k, c0:c0 + n],
                                                                                 start=(k == 0), stop=(k == 1)),
                               r=["wpe", "pb"], w=[kps[be]])
                        act(lambda e, b=b, c0=c0, n=n: e.activation(out=sig[:, c0:c0 + n], in_=psb[b][:, 0:n], func=AF.Sigmoid), r=[kps[b]], w=["sig"])
                        dve(lambda e, be=be, c0=c0, n=n: e.tensor_tensor(out=sig[:, c0:c0 + n], in0=sig[:, c0:c0 + n], in1=psb[be][:, 0:n], op=ALU.mult),
                            r=["sig", kps[be]], w=["sig"])
                        dve(lambda e, m=m, c0=c0, n=n: e.scalar_tensor_tensor(out=X[:, m, c0:c0 + n], in0=X[:, m, c0:c0 + n], scalar=ALPHA,
                                                                             in1=sig[:, c0:c0 + n], op0=ALU.mult, op1=ALU.add),
                            r=[kX, "sig"], w=[kX])
            layernorm(X, kX, cols, O_LN3G, O_LN3B, KT, X, kX, xb, kxb, extra=lnx)
            barrier()
        S.op("sp", lambda e, p=p: e.dma_start(out=o_y[p], in_=X), reads=[kX], writes=[key("o_y")], dma="o_y")
    S.barrier(lambda e: e.memset(dummy, 0.0))
    S.emit(nc, st)
    st.close()
    return nc, S, A


def make_core_inputs(inp, b, s0, depth=DEPTH, npass=NPASS):
    f = np.float32
    L = depth
    m = {}
    xin = np.zeros((npass, 128, KT, NTMAX), f)
    pin = np.zeros((L, npass, 128, 2, NTMAX), f)
    for p in range(npass):
        xs = inp["x_prompt"][b, p * NPT:(p + 1) * NPT]
        xin[p, :, :, :NPT] = xs.reshape(NPT, KT, 128).transpose(2, 1, 0)
        ps = inp["p_prompt"][:L, b, p * NPT:(p + 1) * NPT]
        pin[:, p, :, :, :NPT] = ps.reshape(L, NPT, 2, 128).transpose(0, 3, 2, 1)
    xs = inp["x_sample"][s0:s0 + NSEQ].reshape(NSMP, D)
    xin[npass - 1, :, :, NPT:] = xs.reshape(NSMP, KT, 128).transpose(2, 1, 0)
    ps = inp["p_sample"][:L, s0:s0 + NSEQ].reshape(L, NSMP, 256)
    pin[:, npass - 1, :, :, NPT:] = ps.reshape(L, NSMP, 2, 128).transpose(0, 3, 2, 1)
    m["xin"], m["pin"] = xin, pin

    par = np.zeros((L, 128, NPAR), f)

    def put(off, v, ntile):
        par[:, :, off:off + ntile] = v.reshape(L, ntile, 128).transpose(0, 2, 1)
    put(O_LN1G, inp["ln1_g"][:L], 16); put(O_LN1B, inp["ln1_b"][:L], 16)
    put(O_LN2G, inp["ln2_g"][:L], 16); put(O_LN2B, inp["ln2_b"][:L], 16)
    put(O_LN3G, inp["ln3_g"][:L], 16); put(O_LN3B, inp["ln3_b"][:L], 16)
    put(O_CB, inp["conv_b"][:L], 8); put(O_CLG, inp["conv_ln_g"][:L], 8); put(O_CLB, inp["conv_ln_b"][:L], 8)
    put(O_D, inp["s5_d"][:L].reshape(L, 1024), 8)
    par[:, :, O_CW:O_CW + 248] = inp["conv_w"][:L].reshape(L, 31, 8, 128).transpose(0, 3, 2, 1).reshape(L, 128, 248)
    par[:, :, O_FCW:O_FCW + 264] = inp["ffn_conv_w"][:L].reshape(L, 3, 88, 128).transpose(0, 3, 2, 1).reshape(L, 128, 264)
    put(O_FCB, inp["ffn_conv_b"][:L], 88)
    m["par"] = par

    def lay_p(v):
        return v.reshape(L, 32, 2, 64).transpose(0, 2, 3, 1).reshape(L, 128, 32)
    ldt = np.broadcast_to(inp["s5_log_dt"][:L, :, None], (L, 64, 64))
    m["lamp"] = np.ascontiguousarray(np.stack([lay_p(inp["s5_lam_re"][:L]), lay_p(inp["s5_lam_im"][:L]), lay_p(ldt)], axis=2))
    ctp = np.zeros((L, 128, 2, 32, 32), f)
    for ri, nm in enumerate(("s5_c_re", "s5_c_im")):
        c = inp[nm][:L].reshape(L, 32, 2, 16, 64)
        for g2 in range(2):
            ctp[:, g2 * 64:(g2 + 1) * 64, ri, :, g2 * 16:(g2 + 1) * 16] = c[:, :, g2].transpose(0, 3, 1, 2)
    m["ctp"] = ctp
    def lay_f(v):
        w = v.reshape(L, 8, 4, 2, 64)
        w = w.transpose(0, 2, 1, 3, 4).reshape(L, 4, 1, 8, 128)
        return np.broadcast_to(w, (L, 4, 32, 8, 128)).reshape(L, 128, 8, 128)
    m["lamf"] = np.ascontiguousarray(np.stack([lay_f(inp["s5_lam_re"][:L]), lay_f(inp["s5_lam_im"][:L]), lay_f(ldt)], axis=2))
    btf = np.zeros((L, 4, 2, 16, 2, 8, 2, 64), f)
    for ri, nm in enumerate(("s5_b_re", "s5_b_im")):
        bb = inp[nm][:L].reshape(L, 8, 4, 2, 64, 16)
        for g2 in range(2):
            btf[:, :, g2, :, ri, :, g2, :] = bb[:, :, :, g2].transpose(0, 2, 4, 1, 3)
    m["btf"] = btf.reshape(L, 128, 2, 8, 128)
    sin_ = np.zeros((L, 128, 32, 2, NSEQ), f)
    for ri, nm in enumerate(("state_s5_re", "state_s5_im")):
        sv = inp[nm][:L, s0:s0 + NSEQ].reshape(L, NSEQ, 32, 2, 64)
        sin_[:, :, :, ri, :] = sv.transpose(0, 3, 4, 2, 1).reshape(L, 128, 32, NSEQ)
    m["sin"] = sin_
    cc = inp["cache_conv"][:L, s0:s0 + NSEQ]
    m["ccr"] = np.ascontiguousarray(cc)
    m["ccf"] = np.ascontiguousarray(cc.reshape(L, NSEQ, 30, 8, 128).transpose(0, 4, 3, 1, 2))
    cf = inp["cache_ffn_conv"][:L, s0:s0 + NSEQ]
    m["cff"] = np.ascontiguousarray(cf.reshape(L, NSEQ, 2, 88, 128).transpose(0, 4, 3, 1, 2))
    m["w_in"] = inp["w_in"][:L]; m["w_glu"] = inp["s5_w_glu"][:L]; m["w_out"] = inp["w_out"][:L]
    m["w_up"] = inp["ffn_w_up"][:L]; m["w_down"] = inp["ffn_w_down"][:L]
    m["w_pe"] = inp["pe_w"][:L]; m["w_gate"] = inp["pe_w_gate"][:L]
    return m


def unpack_core(res, depth=DEPTH, npass=NPASS):
    L = depth
    o = {}
    y = res["o_y"]
    yp = y[:, :, :, :NPT].transpose(0, 3, 2, 1).reshape(npass * NPT, D)
    ys = y[npass - 1, :, :, NPT:].transpose(2, 1, 0).reshape(NSEQ, 4, D)
    o["y_prompt"], o["y_sample"] = yp, ys
    sp = res["o_s5p"].reshape(L, 2, 64, 32, 2)
    sp = sp.transpose(0, 4, 3, 1, 2).reshape(L, 2, 64, 64)
    o["s5_re_prompt"], o["s5_im_prompt"] = sp[:, 0], sp[:, 1]
    ss = res["o_s5s"].reshape(L, 2, 64, 32, 2, NSEQ)
    ss = ss.transpose(0, 4, 5, 3, 1, 2).reshape(L, 2, NSEQ, 64, 64)
    o["s5_re_sample"], o["s5_im_sample"] = ss[:, 0], ss[:, 1]
    o["conv_prompt"] = res["o_cvp"].transpose(0, 3, 2, 1).reshape(L, 30, 1024)
    new = res["o_cvsn"].transpose(0, 3, 4, 2, 1).reshape(L, NSEQ, 4, 1024)
    o["conv_sample"] = np.concatenate([res["o_cvso"], new], axis=2)
    o["ffn_conv_prompt"] = res["o_ffp"].transpose(0, 3, 2, 1).reshape(L, 2, 11264)
    o["ffn_conv_sample"] = res["o_ffs"].transpose(0, 3, 4, 2, 1).reshape(L, NSEQ, 2, 11264)
    return o


_PROG = {}


def kernel(**inputs):
    inp = {k: np.asarray(v) for k, v in inputs.items()}
    if "nc" not in _PROG:
        _PROG["nc"] = build_program()[0]
    nc = _PROG["nc"]
    n = 8
    in_maps = [make_core_inputs(inp, c % 4, 16 * c) for c in range(n)]
    res = run_bass_kernel_spmd(nc, in_maps, core_ids=list(range(n)))
    outs = [unpack_core(r) for r in res.results]
    f = np.float32
    y_prompt = np.stack([outs[b]["y_prompt"] for b in range(4)]).astype(f)
    y_sample = np.concatenate([outs[c]["y_sample"] for c in range(n)]).astype(f)

    def pstack(nm):
        return np.stack([outs[b][nm] for b in range(4)], axis=1).astype(f)

    def sstack(nm):
        return np.concatenate([outs[c][nm] for c in range(n)], axis=1).astype(f)

    return (y_prompt, y_sample,
            pstack("s5_re_prompt"), pstack("s5_im_prompt"), pstack("conv_prompt"), pstack("ffn_conv_prompt"),
            sstack("s5_re_sample"), sstack("s5_im_sample"), sstack("conv_sample"), sstack("ffn_conv_sample"))
```
